# Optimizing a Trainium2 kernel written in Bass

```python
import jax, jax.numpy as jnp
from jax import lax
import numpy as np

D_MODEL = 2048
BATCH = 2
SEQ = 4096
DEPTH = 1

HEAD_DIM = 128
N_HEADS = D_MODEL // HEAD_DIM
N_HEADS_B = N_HEADS // 4
N_HEADS_A = N_HEADS - N_HEADS_B
DILATION_PATTERNS = ((128, 1), (512, 4), (2048, 16))
HEADS_PER_GROUP = N_HEADS_A // len(DILATION_PATTERNS)
WIDTH_A = HEADS_PER_GROUP * HEAD_DIM
WIDTH_B = N_HEADS_B * HEAD_DIM
GRID_W = 64
NA_ROWS = 8
NA_COLS = 16
D_FF = 4 * D_MODEL
N_BRANCHES = 2
BAND_BLOCK = 64
EPS = 1e-6
NEG = -1e30

kernel_name = "hybrid_dilated_neighbourhood_gated_encoder"


def _rmsnorm(x, g):
    xf = x.astype(jnp.float32)
    y = xf * lax.rsqrt(jnp.mean(xf * xf, axis=-1, keepdims=True) + EPS)
    return (y * g.astype(jnp.float32)).astype(x.dtype)


def _alibi_slopes(n):
    return jnp.asarray(2.0 ** (-8.0 * np.arange(1, n + 1) / n), dtype=jnp.float32)


def _banded_attention(q, k, v, slopes, half_window, stride):
    B, H, N, L, hd = q.shape
    Qb = BAND_BLOCK
    W = half_window
    nb = -(-L // Qb)
    Lp = nb * Qb
    Kb = Qb + 2 * W
    qp = jnp.pad(q, ((0, 0), (0, 0), (0, 0), (0, Lp - L), (0, 0)))
    kp = jnp.pad(k, ((0, 0), (0, 0), (0, 0), (W, Lp - L + W), (0, 0)))
    vp = jnp.pad(v, ((0, 0), (0, 0), (0, 0), (W, Lp - L + W), (0, 0)))
    key_idx = np.arange(nb)[:, None] * Qb + np.arange(Kb)[None, :]
    kb = jnp.take(kp, key_idx, axis=3)
    vb = jnp.take(vp, key_idx, axis=3)
    qb = qp.reshape(B, H, N, nb, Qb, hd)
    s = jnp.einsum('bhniqd,bhnikd->bhniqk', qb, kb).astype(jnp.float32) * (HEAD_DIM ** -0.5)
    qpos = np.arange(Lp).reshape(nb, Qb)
    kpos = key_idx - W
    rel = np.abs(kpos[:, None, :] - qpos[:, :, None])
    valid = (rel <= W) & (kpos >= 0)[:, None, :] & (kpos < L)[:, None, :]
    bias = -(slopes * stride)[:, None, None, None, None] * rel.astype(np.float32)[None, None]
    s = jnp.where(valid, s + bias, NEG)
    m = jnp.max(s, axis=-1, keepdims=True)
    p = jnp.exp(s - m)
    den = jnp.sum(p, axis=-1, keepdims=True)
    o = jnp.einsum('bhniqk,bhnikd->bhniqd', (p / den).astype(v.dtype), vb)
    lse = (m + jnp.log(den))[..., 0]
    o = o.reshape(B, H, N, Lp, hd)[:, :, :, :L]
    lse = lse.reshape(B, H, N, Lp)[:, :, :, :L]
    return o, lse


def _dilated_mixer(q, k, v):
    B, S = q.shape[0], q.shape[1]
    slopes = _alibi_slopes(N_HEADS_A)
    outs, lses = [], []
    for g, (window, d) in enumerate(DILATION_PATTERNS):
        hs = slice(g * HEADS_PER_GROUP, (g + 1) * HEADS_PER_GROUP)

        def to_residue(t):
            return t[:, :, hs].reshape(B, S // d, d, HEADS_PER_GROUP, HEAD_DIM).transpose(0, 3, 2, 1, 4)

        o, lse = _banded_attention(to_residue(q), to_residue(k), to_residue(v),
                                   slopes[hs], window // (2 * d), d)
        outs.append(o.transpose(0, 3, 2, 1, 4).reshape(B, S, HEADS_PER_GROUP, HEAD_DIM))
        lses.append(lse.transpose(0, 3, 2, 1).reshape(B, S, HEADS_PER_GROUP))
    o_all = jnp.stack(outs, axis=0).astype(jnp.float32)
    alpha = jax.nn.softmax(jnp.stack(lses, axis=0), axis=0)
    y = jnp.sum(alpha[..., None] * o_all, axis=0)
    return y.reshape(B, S, WIDTH_A).astype(q.dtype)


def _neighbourhood_mixer(q, k, v, rpb):
    B, S, H, hd = q.shape
    rows = S // GRID_W
    kh = min(NA_ROWS, rows)
    kw = NA_COLS

    def grid(t):
        return t.reshape(B, rows, GRID_W, H, hd).transpose(0, 3, 1, 2, 4)

    row_start = np.clip(np.arange(rows) - kh // 2, 0, rows - kh)
    row_idx = row_start[:, None] + np.arange(kh)[None, :]
    kg = jnp.take(grid(k), row_idx, axis=2).reshape(B, H, rows, kh * GRID_W, hd)
    vg = jnp.take(grid(v), row_idx, axis=2).reshape(B, H, rows, kh * GRID_W, hd)
    col = np.arange(GRID_W)
    col_start = np.clip(col - kw // 2, 0, GRID_W - kw)
    col_ok = (col[None, :] >= col_start[:, None]) & (col[None, :] < col_start[:, None] + kw)
    mask = np.tile(col_ok, (1, kh))
    dr = row_idx - np.arange(rows)[:, None]
    dc = np.clip(col[None, :] - col[:, None], -(kw - 1), kw - 1)
    bias = rpb.astype(jnp.float32)[:, dr + NA_ROWS - 1][..., dc + NA_COLS - 1]
    bias = bias.transpose(0, 1, 3, 2, 4).reshape(H, rows, GRID_W, kh * GRID_W)
    s = jnp.einsum('bhrqd,bhrkd->bhrqk', grid(q), kg).astype(jnp.float32) * (HEAD_DIM ** -0.5) + bias
    s = jnp.where(mask, s, NEG)
    p = jax.nn.softmax(s, axis=-1)
    o = jnp.einsum('bhrqk,bhrkd->bhrqd', p.astype(v.dtype), vg)
    return o.transpose(0, 2, 3, 1, 4).reshape(B, S, WIDTH_B)


def _mixer_block(h, w_qkv, w_gate, b_gate, rpb, w_proj_a, w_proj_b, w_out):
    B, S, _ = h.shape
    qkv = (h @ w_qkv).reshape(B, S, 3, N_HEADS, HEAD_DIM)
    q, k, v = qkv[:, :, 0], qkv[:, :, 1], qkv[:, :, 2]
    y_a = _dilated_mixer(q[:, :, :N_HEADS_A], k[:, :, :N_HEADS_A], v[:, :, :N_HEADS_A])
    y_b = _neighbourhood_mixer(q[:, :, N_HEADS_A:], k[:, :, N_HEADS_A:], v[:, :, N_HEADS_A:], rpb)
    gates = jax.nn.sigmoid(h @ w_gate + b_gate).reshape(B, S, N_BRANCHES, D_MODEL)
    merged = gates[:, :, 0] * (y_a @ w_proj_a) + gates[:, :, 1] * (y_b @ w_proj_b)
    return merged @ w_out


def _sqrelu_mlp(h, w_up, w_down):
    return jnp.square(jax.nn.relu(h @ w_up)) @ w_down


def setup_inputs(seed: int = 0) -> dict:
    key = jax.random.key(seed)
    ks = jax.random.split(key, 14)
    f32 = jnp.float32

    def nrm(k, shape, scale):
        return jax.random.normal(k, shape, f32) * scale

    return {
        "x": nrm(ks[0], (BATCH, SEQ, D_MODEL), 1.0),
        "norm_mix": 1.0 + nrm(ks[1], (DEPTH, D_MODEL), 0.02),
        "w_qkv": nrm(ks[2], (DEPTH, D_MODEL, 3 * N_HEADS * HEAD_DIM), D_MODEL ** -0.5),
        "w_gate": nrm(ks[3], (DEPTH, D_MODEL, N_BRANCHES * D_MODEL), D_MODEL ** -0.5),
        "b_gate": nrm(ks[4], (DEPTH, N_BRANCHES * D_MODEL), 0.02),
        "rpb": nrm(ks[5], (DEPTH, N_HEADS_B, 2 * NA_ROWS - 1, 2 * NA_COLS - 1), 0.1),
        "w_proj_a": nrm(ks[6], (DEPTH, WIDTH_A, D_MODEL), WIDTH_A ** -0.5),
        "w_proj_b": nrm(ks[7], (DEPTH, WIDTH_B, D_MODEL), WIDTH_B ** -0.5),
        "w_out": nrm(ks[8], (DEPTH, D_MODEL, D_MODEL), D_MODEL ** -0.5),
        "norm_mlp": 1.0 + nrm(ks[9], (DEPTH, D_MODEL), 0.02),
        "w_up": nrm(ks[10], (DEPTH, D_MODEL, D_FF), D_MODEL ** -0.5),
        "w_down": nrm(ks[11], (DEPTH, D_FF, D_MODEL), D_FF ** -0.5),
        "norm_final": 1.0 + nrm(ks[12], (D_MODEL,), 0.02),
    }


def reference(x, norm_mix, w_qkv, w_gate, b_gate, rpb, w_proj_a, w_proj_b, w_out,
              norm_mlp, w_up, w_down, norm_final):
    for l in range(DEPTH):
        h = _rmsnorm(x, norm_mix[l])
        x = x + _mixer_block(h, w_qkv[l], w_gate[l], b_gate[l], rpb[l],
                             w_proj_a[l], w_proj_b[l], w_out[l])
        h = _rmsnorm(x, norm_mlp[l])
        x = x + _sqrelu_mlp(h, w_up[l], w_down[l])
    return _rmsnorm(x, norm_final)
```

```python
import os
import numpy as np
import ml_dtypes
import concourse.bass as bass
import concourse.mybir as mybir
from concourse.bass_utils import run_bass_kernel_spmd

F32 = mybir.dt.float32
BF16 = mybir.dt.bfloat16
AF = mybir.ActivationFunctionType
ALU = mybir.AluOpType

D = 2048
S = 4096
NB = 2
HD = 128
DFF = 8192
EPS = 1e-6
NEG = -30000.0
SCALE = HD ** -0.5
NCORES = int(os.environ.get('KCORES', '8'))
DILS = (1, 4, 16)

STOP = os.environ.get("KSTOP", "")
DUMP = os.environ.get("KDUMP", "")


def set_tokens(c, name):
    base = 1024 * c - 1024
    if name in ("3a", "3b"):
        r0 = 0 if name == "3a" else 8
        r = np.arange(8)[:, None] + r0
        i = np.arange(192)[None, :]
        return (base + 16 * i + r).reshape(-1)
    if name == "near":
        return base + np.arange(768, 2304)
    if name == "own":
        return base + np.arange(1024, 2048)
    raise KeyError(name)


def pat_idx(start, dims):
    idx = np.array([start])
    for step, cnt in dims:
        idx = (idx[:, None] + step * np.arange(cnt)[None, :]).reshape(-1)
    return idx


def alibi_slopes():
    return 2.0 ** (-8.0 * np.arange(1, 13) / 12.0)


def bias_block(kind, hidx, ktok, qtok, rpb):
    k = ktok[:, None].astype(np.int64)
    q = qtok[None, :].astype(np.int64)
    kin = (k >= 0) & (k < S)
    if kind < 3:
        d = DILS[kind]
        slope = alibi_slopes()[kind * 4 + hidx]
        same = (np.mod(k, d) == np.mod(q, d))
        ik = np.floor_divide(k, d)
        iq = np.floor_divide(q, d)
        rel = np.abs(ik - iq)
        valid = kin & same & (rel <= 64)
        b = -(np.float32(slope) * np.float32(d)) * rel.astype(np.float32)
        return np.where(valid, b, np.float32(NEG)).astype(np.float32)
    rk, ck = np.floor_divide(k, 64), np.mod(k, 64)
    rq, cq = np.floor_divide(q, 64), np.mod(q, 64)
    rs = np.clip(rq - 4, 0, 56)
    cs = np.clip(cq - 8, 0, 48)
    valid = kin & (rk >= rs) & (rk < rs + 8) & (ck >= cs) & (ck < cs + 16)
    dr = np.clip(rk - rq + 7, 0, 14)
    dc = np.clip(ck - cq, -15, 15) + 15
    g = rpb[hidx][dr, dc]
    return np.where(valid, g, np.float32(NEG)).astype(np.float32)


class AttnPlan:
    def __init__(self, name, tokset, kind, head0):
        self.name, self.tokset, self.kind, self.head0 = name, tokset, kind, head0
        self.vtiles = []
        self.qgroups = []
        self.qblocks = []

    def vt(self, start, dims):
        key = (start, tuple(map(tuple, dims)))
        for n, v in enumerate(self.vtiles):
            if v == key:
                return n
        self.vtiles.append(key)
        return len(self.vtiles) - 1


def make_plans():
    plans = []
    for hf, nm in enumerate(("3a", "3b")):
        p = AttnPlan(nm, nm, 2, 8)
        for a in range(4):
            tl = [p.vt(128 * (3 * a + t), [(1, 128)]) for t in range(3)]
            p.qgroups.append((8 * hf + 2 * a, [(1, 2), (16, 64)], tl, (a * 128, [(1, 128)])))
        p.qblocks = [(64, [(192, 8), (1, 64)], 0)]
        plans.append(p)
    near_q = [(256, [(1, 512)], 0), (768, [(1, 512)], 512)]
    p = AttnPlan("g2", "near", 1, 4)
    for rp in range(4):
        kts = [p.vt(4 * k0 - 768 + rp, [(4, 128)]) for k0 in (192, 320, 448)]
        for t in range(2):
            own = (512 * t + rp, [(4, 128)])
            p.qgroups.append((own[0], own[1], [kts[t], kts[t + 1]], own))
    p.qblocks = near_q
    plans.append(p)
    p = AttnPlan("g1", "near", 0, 0)
    kts = [p.vt(192 + 128 * m, [(1, 128)]) for m in range(9)]
    for j in range(8):
        own = (128 * j, [(1, 128)])
        p.qgroups.append((own[0], own[1], [kts[j], kts[j + 1]], own))
    p.qblocks = near_q
    plans.append(p)
    p = AttnPlan("nb", "near", 3, 12)
    for g in range(8):
        R = 16 + 2 * g
        tiles = list(range(R - 4, R + 5, 2))
        if g == 0:
            tiles.append(22)
        if g == 7:
            tiles.insert(0, 24)
        tiles = sorted(set(tiles))
        kts = [p.vt(64 * K0 - 768, [(1, 128)]) for K0 in tiles]
        own = (128 * g, [(1, 128)])
        p.qgroups.append((own[0], own[1], kts, own))
    p.qblocks = near_q
    plans.append(p)
    return plans


def build_tables(plans, rpb_all):
    slot_maps, contents = [], [[] for _ in range(4)]
    for p in plans:
        toks = [set_tokens(c, p.tokset) for c in range(4)]
        owns = [set_tokens(c, "own") for c in range(4)]
        pm, pc = [], [[] for _ in range(4)]
        for j in range(4):
            seen, smap, tabs = {}, {}, [[] for _ in range(4)]
            for g, (os_, od, tl, _) in enumerate(p.qgroups):
                qi = pat_idx(os_, od)
                for n, vt in enumerate(tl):
                    ki = pat_idx(*p.vtiles[vt])
                    blks = [bias_block(p.kind, j, toks[c][ki], owns[c][qi],
                                       rpb_all) for c in range(4)]
                    key = b"".join(b.tobytes() for b in blks)
                    if key not in seen:
                        seen[key] = len(tabs[0])
                        for c in range(4):
                            tabs[c].append(blks[c])
                    smap[(g, n)] = seen[key]
            pm.append(smap)
            for c in range(4):
                pc[c].append(np.stack(tabs[c], axis=1))
        slot_maps.append(pm)
        for c in range(4):
            contents[c].append(pc[c])
    return slot_maps, contents


ENGS = ("sp", "act", "dve", "pool", "pe")


class Prog:
    NDSEM = 8

    def __init__(self, nc):
        self.nc = nc
        self.ops = []
        self.lastw = {}
        self.readers = {}

    def add(self, eng, fn, reads=(), writes=(), dma=False):
        idx = len(self.ops)
        deps = set()
        for r in reads:
            w = self.lastw.get(r)
            if w is not None:
                deps.add(w)
        for r in writes:
            w = self.lastw.get(r)
            if w is not None:
                deps.add(w)
            deps.update(self.readers.get(r, ()))
        for r in reads:
            self.readers.setdefault(r, []).append(idx)
        for r in writes:
            self.lastw[r] = idx
            self.readers[r] = []
        deps.discard(idx)
        if eng == "pe":
            deps = {d for d in deps if self.ops[d]["eng"] != "pe"}
        self.ops.append(dict(eng=eng, fn=fn, deps=deps, dma=dma, idx=idx))
        return idx

    def emit(self, block, sems, dsems):
        ops = self.ops
        needed = set()
        for o in ops:
            needed.update(o["deps"])
        cnt = {e: 0 for e in ENGS}
        dcnt = {e: 0 for e in ENGS}
        for o in ops:
            e = o["eng"]
            if o["dma"]:
                j = dcnt[e]
                dcnt[e] += 1
                o["sem"] = dsems[e][j % self.NDSEM]
                o["val"] = 16 * (j // self.NDSEM + 1)
                o["dj"] = j
            elif o["idx"] in needed or o.get("force"):
                cnt[e] += 1
                o["sem"] = sems[e]
                o["val"] = cnt[e]
            else:
                o["sem"] = None
        per = {e: [o for o in ops if o["eng"] == e] for e in ENGS}

        def run(engname, eng):
            waited = {}

            def wait(sem, val):
                if waited.get(sem.num if hasattr(sem, "num") else id(sem), 0) < val:
                    eng.wait_ge(sem, val)
                    waited[sem.num if hasattr(sem, "num") else id(sem)] = val

            for o in per[engname]:
                for d in sorted(o["deps"]):
                    od = ops[d]
                    wait(od["sem"], od["val"])
                if o["dma"] and o["val"] > 16:
                    wait(o["sem"], o["val"] - 16)
                ins = o["fn"](eng)
                if o["sem"] is not None:
                    ins.then_inc(o["sem"], 16 if o["dma"] else 1)

        block.sync(lambda e: run("sp", e))
        block.scalar(lambda e: run("act", e))
        block.vector(lambda e: run("dve", e))
        block.gpsimd(lambda e: run("pool", e))
        block.tensor(lambda e: run("pe", e))


class _Stop(Exception):
    pass


def MM(out, lhsT, rhs, start, stop):
    return lambda e: e.matmul(out, lhsT=lhsT, rhs=rhs, start=start, stop=stop)


def ACTF(out, in_, func, **kw):
    return lambda e: e.activation(out=out, in_=in_, func=func, **kw)


def TT(out, in0, in1, op):
    return lambda e: e.tensor_tensor(out=out, in0=in0, in1=in1, op=op)


def TS(out, in0, s1, s2, op0, op1=None):
    if op1 is None:
        return lambda e: e.tensor_scalar(out=out, in0=in0, scalar1=s1, scalar2=s2, op0=op0)
    return lambda e: e.tensor_scalar(out=out, in0=in0, scalar1=s1, scalar2=s2, op0=op0, op1=op1)


def STT(out, in0, scalar, in1, op0, op1):
    return lambda e: e.scalar_tensor_tensor(out=out, in0=in0, scalar=scalar, in1=in1, op0=op0, op1=op1)


def CP(out, in_):
    return lambda e: e.tensor_copy(out=out, in_=in_)


def DMA(out, in_):
    return lambda e: e.dma_start(out=out, in_=in_)


def TR(out, in_, ident):
    return lambda e: e.transpose(out=out, in_=in_, identity=ident)


def RCP(out, in_):
    return lambda e: e.reciprocal(out=out, in_=in_)


def MS(a, val):
    return lambda e: e.memset(a, val)


def build(plans, slot_maps, ntab):
    nc = bass.Bass("TRN2", target_bir_lowering=False)
    P = Prog(nc)

    def din(name, shape, dt=F32):
        return nc.dram_tensor(name, list(shape), dt, kind="ExternalInput")

    x_sets = {n: din("x_" + n, [1536, D]) for n in ("3a", "3b", "near")}
    w_qkv = din("w_qkv", [D, 3 * D])
    w_gate = din("w_gate", [D, 2 * D])
    w_pa = din("w_pa", [512, D])
    w_pb = din("w_pb", [512, D])
    w_out = din("w_out", [D, D])
    w_up = din("w_up", [D, DFF])
    w_down = din("w_down", [DFF, D])
    gmix_d = din("gmix", [128, 16])
    gmlp_d = din("gmlp", [128, 16])
    gfin_d = din("gfin", [128, D])
    bgate_d = din("bgate", [128, 32])
    ident_d = din("ident", [128, 128], BF16)
    tabs_d = din("tabs", [128, ntab, 128])
    out_d = nc.dram_tensor("out", [1024, D], F32, kind="ExternalOutput")
    dbg = [None]

    pstride = {}

    def sb(name, shape, dt, off):
        t = nc.alloc_sbuf_tensor_at(name, list(shape), dt, offset=16512 + off)
        pstride[t.name] = int(np.prod(shape[1:]))
        return t

    KB = 1024
    A = sb("A", [128, 16, 1536], BF16, 0)
    Bf = sb("Bf", [128, 8, 2048], F32, 48 * KB)
    NUM = sb("NUM", [128, 4, 1024], F32, 48 * KB)
    DEN = sb("DEN", [128, 4, 1024], F32, 64 * KB)
    KT = sb("KT", [128, 4, 1536], BF16, 80 * KB)
    V = sb("V", [128, 12, 512], BF16, 92 * KB)
    QT = sb("QT", [128, 4, 1024], BF16, 104 * KB)
    SCR = sb("SCR", [128, 4, 512], F32, 80 * KB)
    C = sb("C", [128, 16, 1024], BF16, 112 * KB)
    TAB = sb("TAB", [128, 44, 128], F32, 112 * KB)
    TMPS = sb("TMPS", [128, 2, 512], F32, 134 * KB)
    PT = sb("PT", [128, 4, 512], BF16, 138 * KB)
    RD = sb("RD", [128, 512], F32, 142 * KB)
    YA = sb("YA", [128, 4, 1024], BF16, 144 * KB)
    YB = sb("YB", [128, 4, 1024], BF16, 152 * KB)
    YAB = sb("YAB", [128, 8192], BF16, 144 * KB)
    RL = sb("RL", [128, 2, 512], F32, 144 * KB)
    STG = sb("STG", [128, 2, 2048], F32, 160 * KB)
    WBF = sb("WBF", [128, 4, 2048], BF16, 176 * KB)
    GFIN = sb("GFIN", [128, 2048], F32, 176 * KB)
    XN = sb("XN", [128, 2, 2048], BF16, 192 * KB)
    cb = 200 * KB
    IDENT = sb("IDENT", [128, 128], BF16, cb)
    ONES = sb("ONES", [128, 128], BF16, cb + 256)
    GMIX = sb("GMIX", [128, 16], F32, cb + 512)
    GMLP = sb("GMLP", [128, 16], F32, cb + 576)
    BGATE = sb("BGATE", [128, 32], F32, cb + 640)
    SS = sb("SS", [128, 64], F32, cb + 768)
    RS = sb("RS", [128, 64], F32, cb + 1024)
    JUNK = sb("JUNK", [128, 16], F32, cb + 1280)

    PS = [nc.alloc_psum_tensor("ps%d" % i, [128, 512], F32) for i in range(8)]
    for i in range(8):
        pstride[PS[i].name] = 512

    def ap(t, off, dims, p0=0, np_=128):
        ps_ = pstride[t.name]
        return bass.AP(t, p0 * ps_ + off, [[ps_, np_]] + [list(d) for d in dims])

    for dst, srcd, key in ((IDENT, ident_d, "ident"), (GMIX, gmix_d, "gmix"), (GMLP, gmlp_d, "gmlp"),
                           (BGATE, bgate_d, "bgate")):
        P.add("sp", DMA(dst[:, :], srcd.ap()), writes=[key], dma=True)
    P.add("pool", MS(ONES[:, :], 1.0), writes=["ones"])

    ctr = dict(stg=0, wbf=0, xn=0, ps=0, pss=0, pso=0, col=0, tmps=0, pt=0, cast=0, ev=0, rl=0, junk=0)

    def nxt(k, n):
        v = ctr[k] % n
        ctr[k] += 1
        return v

    def fence(old_keys, new_keys=()):
        jc = nxt("junk", 16)
        P.add("pool", MS(JUNK[:, jc:jc + 1], 0.0), writes=list(old_keys) + list(new_keys) + [("junk", jc)])

    def dump(tag, t, n, dt=F32):
        if DUMP != tag or dbg[0] is not None:
            return
        dbg[0] = nc.dram_tensor("dbg", [128, n], dt, kind="ExternalOutput")
        keys = list(P.lastw.keys())
        P.add("sp", DMA(dbg[0].ap(), ap(t, 0, [(1, n)])), reads=keys, writes=["dbg"], dma=True)
        if STOP == "dump":
            raise _Stop()

    def norm_tile(xin, xkeys, dst_t, dst_off, dst_kstride, gam, gam_key, extra_reads):
        col = nxt("col", 64)
        xs = nxt("xn", 2)
        P.add("act", ACTF(XN[:, xs, :], xin, AF.Square, accum_out=SS[:, col:col + 1]),
              reads=xkeys, writes=[("xn", xs), ("ss", col)])
        P.add("act", ACTF(RS[:, col:col + 1], SS[:, col:col + 1], AF.Sqrt, bias=EPS, scale=1.0 / D),
              reads=[("ss", col)], writes=[("rs", col)])
        P.add("dve", RCP(RS[:, col:col + 1], RS[:, col:col + 1]),
              reads=[("rs", col)], writes=[("rs", col)])
        P.add("pool", TS(XN[:, xs, :], xin, RS[:, col:col + 1], 1.0, ALU.mult, ALU.mult),
              reads=xkeys + [("rs", col)], writes=[("xn", xs)])
        for hb in range(2):
            b = nxt("ps", 8)
            pb = PS[b][:, :].bitcast(BF16)
            for kk in range(8):
                k = hb * 8 + kk
                P.add("pe", TR(pb[:, kk * 128:(kk + 1) * 128], XN[:, xs, k * 128:(k + 1) * 128], IDENT[:, :]),
                      reads=[("xn", xs), "ident"], writes=[("ps", b)])
            o = ap(dst_t, hb * 8 * dst_kstride + dst_off, [(dst_kstride, 8), (1, 128)])
            g = ap(gam, hb * 8, [(1, 8), (0, 128)])
            i0 = pb.rearrange("p (k t) -> p k t", k=8)
            P.add("dve", TT(o, i0, g, ALU.mult),
                  reads=[("ps", b), gam_key] + extra_reads, writes=[(dst_t.name, dst_off // 128, hb)])

    def load_w(src_ap, nk, ncol):
        s = nxt("stg", 2)
        ws = nxt("wbf", 4)
        n = nk * ncol
        P.add("sp", DMA(ap(STG, s * 2048, [(ncol, nk), (1, ncol)]), src_ap), writes=[("stg", s)], dma=True)
        ce = ("act", "pool")[nxt("cast", 2)]
        o = ap(WBF, ws * 2048, [(1, n)])
        i = ap(STG, s * 2048, [(1, n)])
        if ce == "act":
            P.add("act", ACTF(o, i, AF.Copy), reads=[("stg", s)], writes=[("wbf", ws)])
        else:
            P.add("pool", CP(o, i), reads=[("stg", s)], writes=[("wbf", ws)])
        return ws

    def wcols(w, c0, ncol, r0=0, nk=16):
        a = w.ap()[r0:r0 + nk * 128, c0:c0 + ncol]
        return a.rearrange("(k p) n -> p k n", p=128)

    def evac_copy(dst, src, rk, wk):
        if nxt("ev", 2) == 0:
            P.add("act", ACTF(dst, src, AF.Copy), reads=rk, writes=wk)
        else:
            P.add("dve", CP(dst, src), reads=rk, writes=wk)

    cur_set = [None]
    tab_off = [0]
    hk = [("A", m, hb) for m in range(12) for hb in range(2)]

    def build_hT(setname):
        xd = x_sets[setname]
        for m in range(12):
            s = nxt("stg", 2)
            P.add("sp", DMA(STG[:, s, :], xd.ap()[m * 128:(m + 1) * 128, :]), writes=[("stg", s)], dma=True)
            norm_tile(STG[:, s, :], [("stg", s)], A, m * 128, 1536, GMIX, "gmix", [])
        cur_set[0] = setname

    def project(plan):
        for j in range(4):
            ws = load_w(wcols(w_qkv, 2048 + (plan.head0 + j) * 128, 128), 16, 128)
            for tb in range(3):
                b = nxt("ps", 8)
                for k in range(16):
                    P.add("pe", MM(PS[b][:, :], ap(WBF, ws * 2048 + k * 128, [(1, 128)]),
                                   ap(A, k * 1536 + tb * 512, [(1, 512)]), k == 0, k == 15),
                          reads=[("wbf", ws)] + hk, writes=[("ps", b)])
                evac_copy(ap(KT, j * 1536 + tb * 512, [(1, 512)]), PS[b][:, :], [("ps", b)], [("KT", j, tb)])
        for j in range(4):
            ws = load_w(wcols(w_qkv, (plan.head0 + j) * 128, 128), 16, 128)
            for (st, dims, qoff) in plan.qblocks:
                b = nxt("ps", 8)
                for k in range(16):
                    P.add("pe", MM(PS[b][:, :], ap(WBF, ws * 2048 + k * 128, [(1, 128)]),
                                   ap(A, k * 1536 + st, dims), k == 0, k == 15),
                          reads=[("wbf", ws)] + hk, writes=[("ps", b)])
                evac_copy(ap(QT, j * 1024 + qoff, [(1, 512)]), PS[b][:, :], [("ps", b)], [("QT", j, qoff)])
        wss = [load_w(wcols(w_qkv, 4096 + (plan.head0 + j) * 128, 128), 16, 128) for j in range(4)]
        assert wss == [0, 1, 2, 3], wss
        for n, (st, dims) in enumerate(plan.vtiles):
            b = nxt("ps", 8)
            for k in range(16):
                P.add("pe", MM(PS[b][:, :], ap(A, k * 1536 + st, list(dims)),
                               ap(WBF, k * 128, [(2048, 4), (1, 128)]), k == 0, k == 15),
                      reads=[("wbf", w) for w in range(4)] + hk, writes=[("ps", b)])
            evac_copy(V[:, n, :], PS[b][:, :], [("ps", b)], [("V", n)])

    def attend(pi, plan, mode):
        smaps = slot_maps[pi]
        for j in range(4):
            smap = smaps[j]
            nslot = max(smap.values()) + 1
            assert nslot <= 44, nslot
            t0 = tab_off[0]
            tab_off[0] += nslot
            P.add("sp", DMA(TAB[:, 0:nslot, :], tabs_d.ap()[:, t0:t0 + nslot, :]), writes=["tab"], dma=True)
            blocks = []
            for g, (os_, od, tl, qpat) in enumerate(plan.qgroups):
                for n, vt in enumerate(tl):
                    blocks.append((g, n, vt, len(tl)))
            ob = [None, None]
            kt_keys = [("KT", j, tb) for tb in range(3)]
            for c0 in range(0, len(blocks), 4):
                chunk = blocks[c0:c0 + 4]
                sbk = nxt("pss", 4)
                for ci, (g, n, vt, ntl) in enumerate(chunk):
                    vst, vdims = plan.vtiles[vt]
                    qst, qd = plan.qgroups[g][3]
                    P.add("pe", MM(PS[sbk][:, ci * 128:(ci + 1) * 128], ap(KT, j * 1536 + vst, list(vdims)),
                                   ap(QT, j * 1024 + qst, list(qd)), True, True),
                          reads=kt_keys + [("QT", j, 0), ("QT", j, 512)], writes=[("ps", sbk)])
                ts = nxt("tmps", 2)
                slots = [smap[(g, n)] for (g, n, vt, ntl) in chunk]
                i = 0
                while i < len(slots):
                    jn = i + 1
                    stp = 0
                    if jn < len(slots):
                        stp = slots[jn] - slots[i]
                        jn += 1
                        while jn < len(slots) and slots[jn] - slots[jn - 1] == stp:
                            jn += 1
                    cnt_ = jn - i
                    P.add("dve", STT(ap(TMPS, ts * 512 + i * 128, [(128, cnt_), (1, 128)]),
                                     ap(PS[sbk], i * 128, [(128, cnt_), (1, 128)]), SCALE,
                                     ap(TAB, slots[i] * 128, [(stp * 128, cnt_), (1, 128)]), ALU.mult, ALU.add),
                          reads=[("ps", sbk), "tab"], writes=[("tmps", ts, i)])
                    i = jn
                pt = nxt("pt", 4)
                ncol = len(chunk) * 128
                P.add("act", ACTF(PT[:, pt, 0:ncol], TMPS[:, ts, 0:ncol], AF.Exp),
                      reads=[("tmps", ts, i2) for i2 in range(4)], writes=[("pt", pt)])
                for ci, (g, n, vt, ntl) in enumerate(chunk):
                    gi = g % 4
                    if gi == 0 and n == 0:
                        pr = nxt("pso", 2)
                        ob = [4 + 2 * pr, 5 + 2 * pr]
                    P.add("pe", MM(PS[ob[0]][:, gi * 128:(gi + 1) * 128], V[:, vt, j * 128:(j + 1) * 128],
                                   PT[:, pt, ci * 128:(ci + 1) * 128], n == 0, n == ntl - 1),
                          reads=[("V", vt), ("pt", pt)], writes=[("ps", ob[0])])
                    P.add("pe", MM(PS[ob[1]][:, gi * 128:(gi + 1) * 128], ONES[:, :],
                                   PT[:, pt, ci * 128:(ci + 1) * 128], n == 0, n == ntl - 1),
                          reads=["ones", ("pt", pt)], writes=[("ps", ob[1])])
                    if n != ntl - 1:
                        continue
                    os_, od = plan.qgroups[g][0], plan.qgroups[g][1]
                    if len(od) == 2:
                        c1, c0_ = od[1][1], od[0][1]
                        pdims = [(c1, c0_), (1, c1)]
                    else:
                        pdims = [(1, 128)]
                    pn = ap(PS[ob[0]], gi * 128, pdims)
                    pd = ap(PS[ob[1]], gi * 128, pdims)
                    if mode == "set":
                        P.add("dve", CP(ap(NUM, j * 1024 + os_, od), pn), reads=[("ps", ob[0])], writes=[("NUM", j, g)])
                        P.add("act", ACTF(ap(DEN, j * 1024 + os_, od), pd, AF.Copy),
                              reads=[("ps", ob[1])], writes=[("DEN", j, g)])
                    elif mode == "add":
                        dn = ap(NUM, j * 1024 + os_, od)
                        dd = ap(DEN, j * 1024 + os_, od)
                        P.add("dve", TT(dn, pn, dn, ALU.add), reads=[("ps", ob[0])], writes=["NUMall"])
                        P.add("dve", TT(dd, pd, dd, ALU.add), reads=[("ps", ob[1])], writes=["DENall"])
                    else:
                        rd = ap(RD, 0, pdims)
                        P.add("dve", RCP(rd, pd), reads=[("ps", ob[1])], writes=["rd"])
                        P.add("dve", TT(ap(YB, j * 1024 + os_, od), pn, rd, ALU.mult),
                              reads=[("ps", ob[0]), "rd"], writes=[("YB", j)])

    try:
        numden_set = [("NUM", j, g) for j in range(4) for g in range(4)] + [("DEN", j, g) for j in range(4) for g in range(4)]
        stopped = False
        for pi, plan in enumerate(plans):
            if plan.tokset != cur_set[0]:
                build_hT(plan.tokset)
                dump("hT_" + plan.tokset, A, 16 * 1536, BF16)
            project(plan)
            dump("KT_" + plan.name, KT, 4 * 1536, BF16)
            dump("V_" + plan.name, V, 12 * 512, BF16)
            dump("QT_" + plan.name, QT, 4 * 1024, BF16)
            mode = {"3a": "set", "3b": "set", "g2": "add", "g1": "add", "nb": "nb"}[plan.name]
            if plan.name == "g2":
                fence(numden_set, ["NUMall", "DENall"])
            attend(pi, plan, mode)
            if STOP == plan.name:
                stopped = True
                break
        dump("NUM", NUM, 4096)
        dump("DEN", DEN, 4096)
        dump("YB", YB, 4096, BF16)

        if not stopped:
            for j in range(4):
                for h in range(2):
                    sl = slice(h * 512, (h + 1) * 512)
                    P.add("dve", RCP(DEN[:, j, sl], DEN[:, j, sl]), reads=["DENall"], writes=[("rden", j, h)])
                    P.add("dve", TT(YA[:, j, sl], NUM[:, j, sl], DEN[:, j, sl], ALU.mult),
                          reads=["NUMall", ("rden", j, h)], writes=[("YA", j)])
            dump("YA", YA, 4096, BF16)
            dump("YAB", YAB, 8192, BF16)

            old = ([("KT", j, tb) for j in range(4) for tb in range(3)] + [("V", n) for n in range(12)] +
                   [("QT", j, q) for j in range(4) for q in (0, 512)] + ["tab", "rd"] +
                   [("tmps", t, i) for t in range(2) for i in range(4)] + [("pt", t) for t in range(4)])
            fence(old, ["arD"])
            own_blk = [(256, [(1, 512)]), (768, [(1, 512)])]
            for m in range(16):
                wsA = load_w(wcols(w_gate, m * 128, 128), 16, 128)
                wsB = load_w(wcols(w_gate, 2048 + m * 128, 128), 16, 128)
                wpa = load_w(wcols(w_pa, m * 128, 128, nk=4), 4, 128)
                wpb = load_w(wcols(w_pb, m * 128, 128, nk=4), 4, 128)
                for blk in range(2):
                    st, dims = own_blk[blk]
                    bs = [nxt("ps", 8) for _ in range(4)]
                    for br, ws in enumerate((wsA, wsB)):
                        for k in range(16):
                            P.add("pe", MM(PS[bs[br]][:, :], ap(WBF, ws * 2048 + k * 128, [(1, 128)]),
                                           ap(A, k * 1536 + st, dims), k == 0, k == 15),
                                  reads=[("wbf", ws)] + hk, writes=[("ps", bs[br])])
                        P.add("act", ACTF(SCR[:, br, :], PS[bs[br]][:, :], AF.Sigmoid,
                                          bias=BGATE[:, br * 16 + m:br * 16 + m + 1]),
                              reads=[("ps", bs[br]), "bgate", "arD"], writes=[("scr", br)])
                    for br, (ws, Y, yk) in enumerate(((wpa, YA, "YA"), (wpb, YB, "YB"))):
                        b = bs[2 + br]
                        for k in range(4):
                            P.add("pe", MM(PS[b][:, :], ap(WBF, ws * 2048 + k * 128, [(1, 128)]),
                                           Y[:, k, blk * 512:(blk + 1) * 512], k == 0, k == 3),
                                  reads=[("wbf", ws)] + [(yk, jj) for jj in range(4)], writes=[("ps", b)])
                        P.add("dve", TT(SCR[:, 2 + br, :], PS[b][:, :], SCR[:, br, :], ALU.mult),
                              reads=[("ps", b), ("scr", br), "arD"], writes=[("scr", 2 + br)])
                    P.add("pool", TT(C[:, m, blk * 512:(blk + 1) * 512], SCR[:, 2, :], SCR[:, 3, :], ALU.add),
                          reads=[("scr", 2), ("scr", 3), "arD"], writes=[("C", m, blk)])
            dump("MG", C, 16 * 1024, BF16)

            fence(["NUMall", "DENall"] + numden_set + [("rden", j, h) for j in range(4) for h in range(2)] +
                  [("scr", i) for i in range(4)], ["arE"])
            for tt in range(8):
                P.add("sp", DMA(Bf[:, tt, :], x_sets["near"].ap()[256 + tt * 128:256 + (tt + 1) * 128, :]),
                      reads=["arE"], writes=[("x1", tt, 0)], dma=True)
            ckeys = [("C", m, blk) for m in range(16) for blk in range(2)]
            for cg in range(4):
                wss = [load_w(wcols(w_out, cg * 512 + q * 128, 128), 16, 128) for q in range(4)]
                assert wss == [0, 1, 2, 3], wss
                for tt in range(8):
                    b = nxt("ps", 8)
                    for k in range(16):
                        P.add("pe", MM(PS[b][:, :], C[:, k, tt * 128:(tt + 1) * 128],
                                       ap(WBF, k * 128, [(2048, 4), (1, 128)]), k == 0, k == 15),
                              reads=[("wbf", w) for w in range(4)] + ckeys, writes=[("ps", b)])
                    xsl = Bf[:, tt, cg * 512:(cg + 1) * 512]
                    P.add("dve", TT(xsl, PS[b][:, :], xsl, ALU.add),
                          reads=[("ps", b), ("x1", tt, 0), ("x1", tt, 1)], writes=[("x1c", tt, cg)])
            dump("X1", Bf, 8 * 2048)

            fence(hk, ["arF"])
            for tt in range(8):
                norm_tile(Bf[:, tt, :], [("x1c", tt, cg) for cg in range(4)], A, tt * 128, 1536, GMLP, "gmlp", ["arF"])
            h2k = [("A", m, hb) for m in range(8) for hb in range(2)]
            dump("H2", A, 16 * 1536, BF16)

            fence(ckeys + [("YA", j) for j in range(4)] + [("YB", j) for j in range(4)], ["arG"])
            for fc in range(8):
                us = fc % 2
                for f in range(8):
                    ws = load_w(wcols(w_up, fc * 1024 + f * 128, 128), 16, 128)
                    for blk in range(2):
                        b = nxt("ps", 8)
                        for k in range(16):
                            P.add("pe", MM(PS[b][:, :], ap(WBF, ws * 2048 + k * 128, [(1, 128)]),
                                           A[:, k, blk * 512:(blk + 1) * 512], k == 0, k == 15),
                                  reads=[("wbf", ws)] + h2k, writes=[("ps", b)])
                        sc = nxt("rl", 2)
                        P.add("act", ACTF(RL[:, sc, :], PS[b][:, :], AF.Relu), reads=[("ps", b), "arG"], writes=[("rl", sc)])
                        P.add("pool", TT(C[:, us * 8 + f, blk * 512:(blk + 1) * 512], RL[:, sc, :], RL[:, sc, :], ALU.mult),
                              reads=[("rl", sc), "arG"], writes=[("uT", us, f, blk)])
                ukeys = [("uT", us, f, blk) for f in range(8) for blk in range(2)]
                for cg in range(4):
                    wss = [load_w(wcols(w_down, cg * 512 + q * 256, 256, r0=fc * 1024, nk=8), 8, 256) for q in range(2)]
                    assert wss[1] == wss[0] + 1, wss
                    for tt in range(8):
                        b = nxt("ps", 8)
                        for f in range(8):
                            P.add("pe", MM(PS[b][:, :], C[:, us * 8 + f, tt * 128:(tt + 1) * 128],
                                           ap(WBF, wss[0] * 2048 + f * 256, [(2048, 2), (1, 256)]), f == 0, f == 7),
                                  reads=[("wbf", wss[0]), ("wbf", wss[1])] + ukeys, writes=[("ps", b)])
                        xsl = Bf[:, tt, cg * 512:(cg + 1) * 512]
                        P.add("dve", TT(xsl, PS[b][:, :], xsl, ALU.add), reads=[("ps", b)], writes=[("x1c", tt, cg)])
            dump("X2", Bf, 8 * 2048)

            P.add("sp", DMA(GFIN[:, :], gfin_d.ap()), writes=["gfin", ("wbf", 0), ("wbf", 1)], dma=True)
            for tt in range(8):
                col = nxt("col", 64)
                xs = nxt("xn", 2)
                xk = [("x1c", tt, cg) for cg in range(4)]
                P.add("act", ACTF(XN[:, xs, :], Bf[:, tt, :], AF.Square, accum_out=SS[:, col:col + 1]),
                      reads=xk, writes=[("xn", xs), ("ss", col)])
                P.add("act", ACTF(RS[:, col:col + 1], SS[:, col:col + 1], AF.Sqrt, bias=EPS, scale=1.0 / D),
                      reads=[("ss", col)], writes=[("rs", col)])
                P.add("dve", RCP(RS[:, col:col + 1], RS[:, col:col + 1]),
                      reads=[("rs", col)], writes=[("rs", col)])
                s = nxt("stg", 2)
                P.add("dve", STT(STG[:, s, :], Bf[:, tt, :], RS[:, col:col + 1], GFIN[:, :], ALU.mult, ALU.mult),
                      reads=xk + [("rs", col), "gfin"], writes=[("stg", s)])
                P.add("sp", DMA(out_d.ap()[tt * 128:(tt + 1) * 128, :], STG[:, s, :]),
                      reads=[("stg", s)], writes=[("out", tt)], dma=True)

    except _Stop:
        pass

    tail_reads = [k for k in P.lastw if isinstance(k, tuple) and k[0] == "out"] + (["dbg"] if "dbg" in P.lastw else [])
    P.add("pool", MS(JUNK[:, 15:16], 0.0), reads=tail_reads, writes=["fin"])

    from contextlib import ExitStack
    with ExitStack() as st:
        sems = {e: st.enter_context(nc.semaphore("c_" + e)) for e in ENGS}
        dsems = {"sp": [st.enter_context(nc.semaphore("d_sp%d" % i)) for i in range(Prog.NDSEM)]}
        block = st.enter_context(nc.Block())
        P.emit(block, sems, dsems)
    print("ops:", len(P.ops), {e: sum(1 for o in P.ops if o["eng"] == e) for e in ENGS}, flush=True)
    return nc, (dbg[0] is not None)


_CACHE = {}


def kernel(x, norm_mix, w_qkv, w_gate, b_gate, rpb, w_proj_a, w_proj_b, w_out,
           norm_mlp, w_up, w_down, norm_final):
    f = lambda a: np.ascontiguousarray(np.asarray(a, dtype=np.float32))
    x = f(x)
    plans = make_plans()
    slot_maps, contents = build_tables(plans, f(rpb)[0])
    ntab = sum(contents[0][pi][j].shape[1] for pi in range(len(plans)) for j in range(4))
    nc, has_dbg = build(plans, slot_maps, ntab)

    def colvec(g, n):
        return np.ascontiguousarray(f(g).reshape(n, 128).T)

    shared = {
        "w_qkv": f(w_qkv)[0], "w_gate": f(w_gate)[0], "w_pa": f(w_proj_a)[0], "w_pb": f(w_proj_b)[0],
        "w_out": f(w_out)[0], "w_up": f(w_up)[0], "w_down": f(w_down)[0],
        "gmix": colvec(norm_mix[0], 16), "gmlp": colvec(norm_mlp[0], 16),
        "gfin": np.ascontiguousarray(np.broadcast_to(f(norm_final)[None, :], (128, D))),
        "bgate": colvec(b_gate[0], 32),
        "ident": np.eye(128, dtype=np.float32).astype(ml_dtypes.bfloat16),
    }
    in_maps = []
    for core in range(NCORES):
        b, c = core // 4, core % 4
        m = dict(shared)
        for nm in ("3a", "3b", "near"):
            tok = set_tokens(c, nm)
            ok = (tok >= 0) & (tok < S)
            xs = np.zeros((1536, D), np.float32)
            xs[ok] = x[b, tok[ok]]
            m["x_" + nm] = xs
        m["tabs"] = np.ascontiguousarray(np.concatenate(
            [contents[c][pi][j] for pi in range(len(plans)) for j in range(4)], axis=1))
        in_maps.append(m)
    res = run_bass_kernel_spmd(nc, in_maps, core_ids=list(range(NCORES)))
    if has_dbg:
        kernel.dbg = [np.asarray(r["dbg"]) for r in res.results]
    out = np.zeros((NB, S, D), np.float32)
    for core in range(NCORES):
        b, c = core // 4, core % 4
        out[b, set_tokens(c, "own")] = np.asarray(res.results[core]["out"])
    return out
```

```python
import os
import numpy as np
import ml_dtypes
import concourse.bass as bass
import concourse.mybir as mybir
from concourse.bass_utils import run_bass_kernel_spmd

F32 = mybir.dt.float32
BF16 = mybir.dt.bfloat16
AF = mybir.ActivationFunctionType
ALU = mybir.AluOpType

D = 2048
S = 4096
NB = 2
HD = 128
DFF = 8192
EPS = 1e-6
NEG = -30000.0
SCALE = HD ** -0.5
NCORES = int(os.environ.get('KCORES', '8'))
DILS = (1, 4, 16)

STOP = os.environ.get("KSTOP", "")
DUMP = os.environ.get("KDUMP", "")


def set_tokens(c, name):
    base = 1024 * c - 1024
    if name in ("3a", "3b"):
        r0 = 0 if name == "3a" else 8
        r = np.arange(8)[:, None] + r0
        i = np.arange(192)[None, :]
        return (base + 16 * i + r).reshape(-1)
    if name == "near":
        return base + np.arange(768, 2304)
    if name == "own":
        return base + np.arange(1024, 2048)
    raise KeyError(name)


def pat_idx(start, dims):
    idx = np.array([start])
    for step, cnt in dims:
        idx = (idx[:, None] + step * np.arange(cnt)[None, :]).reshape(-1)
    return idx


def alibi_slopes():
    return 2.0 ** (-8.0 * np.arange(1, 13) / 12.0)


def bias_block(kind, hidx, ktok, qtok, rpb):
    k = ktok[:, None].astype(np.int64)
    q = qtok[None, :].astype(np.int64)
    kin = (k >= 0) & (k < S)
    if kind < 3:
        d = DILS[kind]
        slope = alibi_slopes()[kind * 4 + hidx]
        same = (np.mod(k, d) == np.mod(q, d))
        ik = np.floor_divide(k, d)
        iq = np.floor_divide(q, d)
        rel = np.abs(ik - iq)
        valid = kin & same & (rel <= 64)
        b = -(np.float32(slope) * np.float32(d)) * rel.astype(np.float32)
        return np.where(valid, b, np.float32(NEG)).astype(np.float32)
    rk, ck = np.floor_divide(k, 64), np.mod(k, 64)
    rq, cq = np.floor_divide(q, 64), np.mod(q, 64)
    rs = np.clip(rq - 4, 0, 56)
    cs = np.clip(cq - 8, 0, 48)
    valid = kin & (rk >= rs) & (rk < rs + 8) & (ck >= cs) & (ck < cs + 16)
    dr = np.clip(rk - rq + 7, 0, 14)
    dc = np.clip(ck - cq, -15, 15) + 15
    g = rpb[hidx][dr, dc]
    return np.where(valid, g, np.float32(NEG)).astype(np.float32)


class AttnPlan:
    def __init__(self, name, tokset, kind, head0):
        self.name, self.tokset, self.kind, self.head0 = name, tokset, kind, head0
        self.vtiles = []
        self.qgroups = []
        self.qblocks = []

    def vt(self, start, dims):
        key = (start, tuple(map(tuple, dims)))
        for n, v in enumerate(self.vtiles):
            if v == key:
                return n
        self.vtiles.append(key)
        return len(self.vtiles) - 1


def make_plans():
    plans = []
    for hf, nm in enumerate(("3a", "3b")):
        p = AttnPlan(nm, nm, 2, 8)
        for a in range(4):
            tl = [p.vt(128 * (3 * a + t), [(1, 128)]) for t in range(3)]
            p.qgroups.append((8 * hf + 2 * a, [(1, 2), (16, 64)], tl, (a * 128, [(1, 128)])))
        p.qblocks = [(64, [(192, 8), (1, 64)], 0)]
        plans.append(p)
    near_q = [(256, [(1, 512)], 0), (768, [(1, 512)], 512)]
    p = AttnPlan("g2", "near", 1, 4)
    for rp in range(4):
        kts = [p.vt(4 * k0 - 768 + rp, [(4, 128)]) for k0 in (192, 320, 448)]
        for t in range(2):
            own = (512 * t + rp, [(4, 128)])
            p.qgroups.append((own[0], own[1], [kts[t], kts[t + 1]], own))
    p.qblocks = near_q
    plans.append(p)
    p = AttnPlan("g1", "near", 0, 0)
    kts = [p.vt(192 + 128 * m, [(1, 128)]) for m in range(9)]
    for j in range(8):
        own = (128 * j, [(1, 128)])
        p.qgroups.append((own[0], own[1], [kts[j], kts[j + 1]], own))
    p.qblocks = near_q
    plans.append(p)
    p = AttnPlan("nb", "near", 3, 12)
    for g in range(8):
        R = 16 + 2 * g
        tiles = list(range(R - 4, R + 5, 2))
        if g == 0:
            tiles.append(22)
        if g == 7:
            tiles.insert(0, 24)
        tiles = sorted(set(tiles))
        kts = [p.vt(64 * K0 - 768, [(1, 128)]) for K0 in tiles]
        own = (128 * g, [(1, 128)])
        p.qgroups.append((own[0], own[1], kts, own))
    p.qblocks = near_q
    plans.append(p)
    return plans


def build_tables(plans, rpb_all):
    slot_maps, contents = [], [[] for _ in range(4)]
    for p in plans:
        toks = [set_tokens(c, p.tokset) for c in range(4)]
        owns = [set_tokens(c, "own") for c in range(4)]
        pm, pc = [], [[] for _ in range(4)]
        for j in range(4):
            seen, smap, tabs = {}, {}, [[] for _ in range(4)]
            for g, (os_, od, tl, _) in enumerate(p.qgroups):
                qi = pat_idx(os_, od)
                for n, vt in enumerate(tl):
                    ki = pat_idx(*p.vtiles[vt])
                    blks = [bias_block(p.kind, j, toks[c][ki], owns[c][qi],
                                       rpb_all) for c in range(4)]
                    key = b"".join(b.tobytes() for b in blks)
                    if key not in seen:
                        seen[key] = len(tabs[0])
                        for c in range(4):
                            tabs[c].append(blks[c])
                    smap[(g, n)] = seen[key]
            pm.append(smap)
            for c in range(4):
                pc[c].append(np.stack(tabs[c], axis=1))
        slot_maps.append(pm)
        for c in range(4):
            contents[c].append(pc[c])
    return slot_maps, contents


ENGS = ("sp", "act", "dve", "pool", "pe")


class Prog:
    NDSEM = 8

    def __init__(self, nc):
        self.nc = nc
        self.ops = []
        self.lastw = {}
        self.readers = {}

    def add(self, eng, fn, reads=(), writes=(), dma=False):
        idx = len(self.ops)
        deps = set()
        for r in reads:
            w = self.lastw.get(r)
            if w is not None:
                deps.add(w)
        for r in writes:
            w = self.lastw.get(r)
            if w is not None:
                deps.add(w)
            deps.update(self.readers.get(r, ()))
        for r in reads:
            self.readers.setdefault(r, []).append(idx)
        for r in writes:
            self.lastw[r] = idx
            self.readers[r] = []
        deps.discard(idx)
        if eng == "pe":
            deps = {d for d in deps if self.ops[d]["eng"] != "pe"}
        self.ops.append(dict(eng=eng, fn=fn, deps=deps, dma=dma, idx=idx))
        return idx

    def emit(self, block, sems, dsems):
        ops = self.ops
        needed = set()
        for o in ops:
            needed.update(o["deps"])
        cnt = {e: 0 for e in ENGS}
        dcnt = {e: 0 for e in ENGS}
        for o in ops:
            e = o["eng"]
            if o["dma"]:
                j = dcnt[e]
                dcnt[e] += 1
                o["sem"] = dsems[e][j % self.NDSEM]
                o["val"] = 16 * (j // self.NDSEM + 1)
                o["dj"] = j
            elif o["idx"] in needed or o.get("force"):
                cnt[e] += 1
                o["sem"] = sems[e]
                o["val"] = cnt[e]
            else:
                o["sem"] = None
        per = {e: [o for o in ops if o["eng"] == e] for e in ENGS}

        def run(engname, eng):
            waited = {}

            def wait(sem, val):
                if waited.get(sem.num if hasattr(sem, "num") else id(sem), 0) < val:
                    eng.wait_ge(sem, val)
                    waited[sem.num if hasattr(sem, "num") else id(sem)] = val

            for o in per[engname]:
                for d in sorted(o["deps"]):
                    od = ops[d]
                    wait(od["sem"], od["val"])
                if o["dma"] and o["val"] > 16:
                    wait(o["sem"], o["val"] - 16)
                ins = o["fn"](eng)
                if o["sem"] is not None:
                    ins.then_inc(o["sem"], 16 if o["dma"] else 1)

        block.sync(lambda e: run("sp", e))
        block.scalar(lambda e: run("act", e))
        block.vector(lambda e: run("dve", e))
        block.gpsimd(lambda e: run("pool", e))
        block.tensor(lambda e: run("pe", e))


class _Stop(Exception):
    pass


def MM(out, lhsT, rhs, start, stop):
    return lambda e: e.matmul(out, lhsT=lhsT, rhs=rhs, start=start, stop=stop)


def ACTF(out, in_, func, **kw):
    return lambda e: e.activation(out=out, in_=in_, func=func, **kw)


def TT(out, in0, in1, op):
    return lambda e: e.tensor_tensor(out=out, in0=in0, in1=in1, op=op)


def TS(out, in0, s1, s2, op0, op1=None):
    if op1 is None:
        return lambda e: e.tensor_scalar(out=out, in0=in0, scalar1=s1, scalar2=s2, op0=op0)
    return lambda e: e.tensor_scalar(out=out, in0=in0, scalar1=s1, scalar2=s2, op0=op0, op1=op1)


def STT(out, in0, scalar, in1, op0, op1):
    return lambda e: e.scalar_tensor_tensor(out=out, in0=in0, scalar=scalar, in1=in1, op0=op0, op1=op1)


def CP(out, in_):
    return lambda e: e.tensor_copy(out=out, in_=in_)


def DMA(out, in_):
    return lambda e: e.dma_start(out=out, in_=in_)


def TR(out, in_, ident):
    return lambda e: e.transpose(out=out, in_=in_, identity=ident)


def RCP(out, in_):
    return lambda e: e.reciprocal(out=out, in_=in_)


def MS(a, val):
    return lambda e: e.memset(a, val)


def build(plans, slot_maps, ntab):
    nc = bass.Bass("TRN2", target_bir_lowering=False)
    P = Prog(nc)

    def din(name, shape, dt=F32):
        return nc.dram_tensor(name, list(shape), dt, kind="ExternalInput")

    x_sets = {n: din("x_" + n, [1536, D]) for n in ("3a", "3b", "near")}
    w_qkv = din("w_qkv", [D, 3 * D])
    w_gate = din("w_gate", [D, 2 * D])
    w_pa = din("w_pa", [512, D])
    w_pb = din("w_pb", [512, D])
    w_out = din("w_out", [D, D])
    w_up = din("w_up", [D, DFF])
    w_down = din("w_down", [DFF, D])
    gmix_d = din("gmix", [128, 16])
    gmlp_d = din("gmlp", [128, 16])
    gfin_d = din("gfin", [128, D])
    bgate_d = din("bgate", [128, 32])
    ident_d = din("ident", [128, 128], BF16)
    tabs_d = din("tabs", [128, ntab, 128])
    out_d = nc.dram_tensor("out", [1024, D], F32, kind="ExternalOutput")
    dbg = [None]

    pstride = {}

    def sb(name, shape, dt, off):
        t = nc.alloc_sbuf_tensor_at(name, list(shape), dt, offset=16512 + off)
        pstride[t.name] = int(np.prod(shape[1:]))
        return t

    KB = 1024
    A = sb("A", [128, 16, 1536], BF16, 0)
    Bf = sb("Bf", [128, 8, 2048], F32, 48 * KB)
    NUM = sb("NUM", [128, 4, 1024], F32, 48 * KB)
    DEN = sb("DEN", [128, 4, 1024], F32, 64 * KB)
    KT = sb("KT", [128, 4, 1536], BF16, 80 * KB)
    V = sb("V", [128, 12, 512], BF16, 92 * KB)
    QT = sb("QT", [128, 4, 1024], BF16, 104 * KB)
    SCR = sb("SCR", [128, 8, 512], F32, 80 * KB)
    C = sb("C", [128, 16, 1024], BF16, 112 * KB)
    TAB = sb("TAB", [128, 44, 128], F32, 112 * KB)
    TMPS = sb("TMPS", [128, 2, 512], F32, 134 * KB)
    PT = sb("PT", [128, 4, 512], BF16, 138 * KB)
    RD = sb("RD", [128, 512], F32, 142 * KB)
    YA = sb("YA", [128, 4, 1024], BF16, 144 * KB)
    YB = sb("YB", [128, 4, 1024], BF16, 152 * KB)
    YAB = sb("YAB", [128, 8192], BF16, 144 * KB)
    RL = sb("RL", [128, 2, 512], F32, 144 * KB)
    STG = sb("STG", [128, 2, 2048], F32, 160 * KB)
    WBF = sb("WBF", [128, 4, 2048], BF16, 176 * KB)
    GFIN = sb("GFIN", [128, 2048], F32, 176 * KB)
    XN = sb("XN", [128, 2, 2048], BF16, 192 * KB)
    cb = 200 * KB
    IDENT = sb("IDENT", [128, 128], BF16, cb)
    ONES = sb("ONES", [128, 128], BF16, cb + 256)
    GMIX = sb("GMIX", [128, 16], F32, cb + 512)
    GMLP = sb("GMLP", [128, 16], F32, cb + 576)
    BGATE = sb("BGATE", [128, 32], F32, cb + 640)
    SS = sb("SS", [128, 64], F32, cb + 768)
    RS = sb("RS", [128, 64], F32, cb + 1024)
    JUNK = sb("JUNK", [128, 16], F32, cb + 1280)

    PS = [nc.alloc_psum_tensor("ps%d" % i, [128, 512], F32) for i in range(8)]
    for i in range(8):
        pstride[PS[i].name] = 512

    def ap(t, off, dims, p0=0, np_=128):
        ps_ = pstride[t.name]
        return bass.AP(t, p0 * ps_ + off, [[ps_, np_]] + [list(d) for d in dims])

    for dst, srcd, key in ((IDENT, ident_d, "ident"), (GMIX, gmix_d, "gmix"), (GMLP, gmlp_d, "gmlp"),
                           (BGATE, bgate_d, "bgate")):
        P.add("sp", DMA(dst[:, :], srcd.ap()), writes=[key], dma=True)
    P.add("pool", MS(ONES[:, :], 1.0), writes=["ones"])

    ctr = dict(stg=0, wbf=0, xn=0, ps=0, pss=0, pso=0, col=0, tmps=0, pt=0, cast=0, ev=0, rl=0, junk=0)

    def nxt(k, n):
        v = ctr[k] % n
        ctr[k] += 1
        return v

    def fence(old_keys, new_keys=()):
        jc = nxt("junk", 16)
        P.add("pool", MS(JUNK[:, jc:jc + 1], 0.0), writes=list(old_keys) + list(new_keys) + [("junk", jc)])

    def dump(tag, t, n, dt=F32):
        if DUMP != tag or dbg[0] is not None:
            return
        dbg[0] = nc.dram_tensor("dbg", [128, n], dt, kind="ExternalOutput")
        keys = list(P.lastw.keys())
        P.add("sp", DMA(dbg[0].ap(), ap(t, 0, [(1, n)])), reads=keys, writes=["dbg"], dma=True)
        if STOP == "dump":
            raise _Stop()

    def norm_tile(xin, xkeys, dst_t, dst_off, dst_kstride, gam, gam_key, extra_reads):
        col = nxt("col", 64)
        xs = nxt("xn", 2)
        P.add("act", ACTF(XN[:, xs, :], xin, AF.Square, accum_out=SS[:, col:col + 1]),
              reads=xkeys, writes=[("xn", xs), ("ss", col)])
        P.add("act", ACTF(RS[:, col:col + 1], SS[:, col:col + 1], AF.Sqrt, bias=EPS, scale=1.0 / D),
              reads=[("ss", col)], writes=[("rs", col)])
        P.add("dve", RCP(RS[:, col:col + 1], RS[:, col:col + 1]),
              reads=[("rs", col)], writes=[("rs", col)])
        P.add("dve", TS(XN[:, xs, :], xin, RS[:, col:col + 1], None, ALU.mult),
              reads=xkeys + [("rs", col)], writes=[("xn", xs)])
        for hb in range(2):
            b = nxt("ps", 8)
            pb = PS[b][:, :].bitcast(BF16)
            for kk in range(8):
                k = hb * 8 + kk
                P.add("pe", TR(pb[:, kk * 128:(kk + 1) * 128], XN[:, xs, k * 128:(k + 1) * 128], IDENT[:, :]),
                      reads=[("xn", xs), "ident"], writes=[("ps", b)])
            o = ap(dst_t, hb * 8 * dst_kstride + dst_off, [(dst_kstride, 8), (1, 128)])
            g = ap(gam, hb * 8, [(1, 8), (0, 128)])
            i0 = pb.rearrange("p (k t) -> p k t", k=8)
            P.add("dve", TT(o, i0, g, ALU.mult),
                  reads=[("ps", b), gam_key] + extra_reads, writes=[(dst_t.name, dst_off // 128, hb)])

    def load_w(src_ap, nk, ncol, ws):
        s = nxt("stg", 2)
        n = nk * ncol
        P.add("sp", DMA(ap(STG, s * 2048, [(ncol, nk), (1, ncol)]), src_ap), writes=[("stg", s)], dma=True)
        ce = ("dve", "act", "dve")[nxt("cast", 3)]
        o = ap(WBF, ws * 2048, [(1, n)])
        i = ap(STG, s * 2048, [(1, n)])
        if ce == "act":
            P.add("act", ACTF(o, i, AF.Copy), reads=[("stg", s)], writes=[("wbf", ws)])
        else:
            P.add("dve", CP(o, i), reads=[("stg", s)], writes=[("wbf", ws)])
        return ws

    def wcols(w, c0, ncol, r0=0, nk=16):
        a = w.ap()[r0:r0 + nk * 128, c0:c0 + ncol]
        return a.rearrange("(k p) n -> p k n", p=128)

    def evac_copy(dst, src, rk, wk):
        if nxt("ev", 2) == 0:
            P.add("act", ACTF(dst, src, AF.Copy), reads=rk, writes=wk)
        else:
            P.add("dve", CP(dst, src), reads=rk, writes=wk)

    units = []

    def U(loads, fn):
        units.append((loads, fn))

    def run_units():
        wpos = 0
        pending = []
        slots_of = {}
        nl = 0
        for i, (loads, fn) in enumerate(units):
            while nl < len(units):
                lo = units[nl][0]
                if not lo:
                    slots_of[nl] = []
                    nl += 1
                    continue
                n = len(lo)
                start = wpos + (1 if (n == 2 and wpos % 2 == 1) else 0)
                oldest = pending[0][1] if pending else start
                if start + n - oldest > 4 or nl > i + 6:
                    break
                wpos = start + n
                sl = [(start + q) % 4 for q in range(n)]
                for (sa, nk, ncol), ws in zip(lo, sl):
                    load_w(sa, nk, ncol, ws)
                slots_of[nl] = sl
                pending.append((nl, start))
                nl += 1
            assert i in slots_of, (i, nl)
            fn(slots_of[i])
            pending = [p for p in pending if p[0] != i]

    tab_off = [0]
    hk = [("A", m, hb) for m in range(12) for hb in range(2)]

    def build_hT(setname):
        xd = x_sets[setname]
        for m in range(12):
            s = nxt("stg", 2)
            P.add("sp", DMA(STG[:, s, :], xd.ap()[m * 128:(m + 1) * 128, :]), writes=[("stg", s)], dma=True)
            norm_tile(STG[:, s, :], [("stg", s)], A, m * 128, 1536, GMIX, "gmix", [])

    def emit_K(ws, j):
        for tb in range(3):
            b = nxt("ps", 8)
            for k in range(16):
                P.add("pe", MM(PS[b][:, :], ap(WBF, ws * 2048 + k * 128, [(1, 128)]),
                               ap(A, k * 1536 + tb * 512, [(1, 512)]), k == 0, k == 15),
                      reads=[("wbf", ws)] + hk, writes=[("ps", b)])
            evac_copy(ap(KT, j * 1536 + tb * 512, [(1, 512)]), PS[b][:, :], [("ps", b)], [("KT", j, tb)])

    def emit_Q(ws, j, qblocks):
        for (st, dims, qoff) in qblocks:
            b = nxt("ps", 8)
            for k in range(16):
                P.add("pe", MM(PS[b][:, :], ap(WBF, ws * 2048 + k * 128, [(1, 128)]),
                               ap(A, k * 1536 + st, dims), k == 0, k == 15),
                      reads=[("wbf", ws)] + hk, writes=[("ps", b)])
            evac_copy(ap(QT, j * 1024 + qoff, [(1, 512)]), PS[b][:, :], [("ps", b)], [("QT", j, qoff)])

    def emit_V(wss, half, vtiles):
        assert wss[1] == wss[0] + 1 and wss[0] % 2 == 0, wss
        for n, (st, dims) in enumerate(vtiles):
            b = nxt("ps", 8)
            for k in range(16):
                P.add("pe", MM(PS[b][:, 0:256], ap(A, k * 1536 + st, list(dims)),
                               ap(WBF, wss[0] * 2048 + k * 128, [(2048, 2), (1, 128)]), k == 0, k == 15),
                      reads=[("wbf", wss[0]), ("wbf", wss[1])] + hk, writes=[("ps", b)])
            evac_copy(V[:, n, half * 256:(half + 1) * 256], PS[b][:, 0:256], [("ps", b)], [("V", n, half)])

    def project(plan):
        for j in range(4):
            U([(wcols(w_qkv, 2048 + (plan.head0 + j) * 128, 128), 16, 128)],
              lambda ws, j=j: emit_K(ws[0], j))
        for j in range(4):
            U([(wcols(w_qkv, (plan.head0 + j) * 128, 128), 16, 128)],
              lambda ws, j=j, qb=plan.qblocks: emit_Q(ws[0], j, qb))
        for half in range(2):
            U([(wcols(w_qkv, 4096 + (plan.head0 + 2 * half + q) * 128, 128), 16, 128) for q in range(2)],
              lambda ws, half=half, vt=list(plan.vtiles): emit_V(ws, half, vt))

    def attend(pi, plan, mode):
        smaps = slot_maps[pi]
        for j in range(4):
            smap = smaps[j]
            nslot = max(smap.values()) + 1
            assert nslot <= 44, nslot
            t0 = tab_off[0]
            tab_off[0] += nslot
            P.add("sp", DMA(TAB[:, 0:nslot, :], tabs_d.ap()[:, t0:t0 + nslot, :]), writes=["tab"], dma=True)
            blocks = []
            for g, (os_, od, tl, qpat) in enumerate(plan.qgroups):
                for n, vt in enumerate(tl):
                    blocks.append((g, n, vt, len(tl)))
            ob = [None, None]
            kt_keys = [("KT", j, tb) for tb in range(3)]
            for c0 in range(0, len(blocks), 4):
                chunk = blocks[c0:c0 + 4]
                sbk = nxt("pss", 4)
                for ci, (g, n, vt, ntl) in enumerate(chunk):
                    vst, vdims = plan.vtiles[vt]
                    qst, qd = plan.qgroups[g][3]
                    P.add("pe", MM(PS[sbk][:, ci * 128:(ci + 1) * 128], ap(KT, j * 1536 + vst, list(vdims)),
                                   ap(QT, j * 1024 + qst, list(qd)), True, True),
                          reads=kt_keys + [("QT", j, 0), ("QT", j, 512)], writes=[("ps", sbk)])
                ts = nxt("tmps", 2)
                slots = [smap[(g, n)] for (g, n, vt, ntl) in chunk]
                i = 0
                while i < len(slots):
                    jn = i + 1
                    stp = 0
                    if jn < len(slots):
                        stp = slots[jn] - slots[i]
                        jn += 1
                        while jn < len(slots) and slots[jn] - slots[jn - 1] == stp:
                            jn += 1
                    cnt_ = jn - i
                    P.add("dve", STT(ap(TMPS, ts * 512 + i * 128, [(128, cnt_), (1, 128)]),
                                     ap(PS[sbk], i * 128, [(128, cnt_), (1, 128)]), SCALE,
                                     ap(TAB, slots[i] * 128, [(stp * 128, cnt_), (1, 128)]), ALU.mult, ALU.add),
                          reads=[("ps", sbk), "tab"], writes=[("tmps", ts, i)])
                    i = jn
                pt = nxt("pt", 4)
                ncol = len(chunk) * 128
                P.add("act", ACTF(PT[:, pt, 0:ncol], TMPS[:, ts, 0:ncol], AF.Exp),
                      reads=[("tmps", ts, i2) for i2 in range(4)], writes=[("pt", pt)])
                for ci, (g, n, vt, ntl) in enumerate(chunk):
                    gi = g % 4
                    if gi == 0 and n == 0:
                        pr = nxt("pso", 2)
                        ob = [4 + 2 * pr, 5 + 2 * pr]
                    P.add("pe", MM(PS[ob[0]][:, gi * 128:(gi + 1) * 128], V[:, vt, j * 128:(j + 1) * 128],
                                   PT[:, pt, ci * 128:(ci + 1) * 128], n == 0, n == ntl - 1),
                          reads=[("V", vt, j // 2), ("pt", pt)], writes=[("ps", ob[0])])
                    P.add("pe", MM(PS[ob[1]][:, gi * 128:(gi + 1) * 128], ONES[:, :],
                                   PT[:, pt, ci * 128:(ci + 1) * 128], n == 0, n == ntl - 1),
                          reads=["ones", ("pt", pt)], writes=[("ps", ob[1])])
                    if n != ntl - 1:
                        continue
                    os_, od = plan.qgroups[g][0], plan.qgroups[g][1]
                    if len(od) == 2:
                        c1, c0_ = od[1][1], od[0][1]
                        pdims = [(c1, c0_), (1, c1)]
                    else:
                        pdims = [(1, 128)]
                    pn = ap(PS[ob[0]], gi * 128, pdims)
                    pd = ap(PS[ob[1]], gi * 128, pdims)
                    if mode == "set":
                        P.add("dve", CP(ap(NUM, j * 1024 + os_, od), pn), reads=[("ps", ob[0])], writes=[("NUM", j, g)])
                        P.add("act", ACTF(ap(DEN, j * 1024 + os_, od), pd, AF.Copy),
                              reads=[("ps", ob[1])], writes=[("DEN", j, g)])
                    elif mode == "add":
                        dn = ap(NUM, j * 1024 + os_, od)
                        dd = ap(DEN, j * 1024 + os_, od)
                        P.add("dve", TT(dn, pn, dn, ALU.add), reads=[("ps", ob[0])], writes=["NUMall"])
                        P.add("dve", TT(dd, pd, dd, ALU.add), reads=[("ps", ob[1])], writes=["DENall"])
                    else:
                        rd = ap(RD, 0, pdims)
                        P.add("dve", RCP(rd, pd), reads=[("ps", ob[1])], writes=["rd"])
                        P.add("dve", TT(ap(YB, j * 1024 + os_, od), pn, rd, ALU.mult),
                              reads=[("ps", ob[0]), "rd"], writes=[("YB", j)])

    numden_set = [("NUM", j, g) for j in range(4) for g in range(4)] + [("DEN", j, g) for j in range(4) for g in range(4)]
    vkeys = [("V", n, h) for n in range(12) for h in range(2)]

    class _Halt(Exception):
        pass

    def halt_if(tag, t, n, dt):
        def f(ws):
            dump(tag, t, n, dt)
        return f

    cur = None
    for pi, plan in enumerate(plans):
        if plan.tokset != cur:
            U([], lambda ws, nm=plan.tokset: build_hT(nm))
            U([], halt_if("hT_" + plan.tokset, A, 16 * 1536, BF16))
            cur = plan.tokset
        project(plan)
        mode = {"3a": "set", "3b": "set", "g2": "add", "g1": "add", "nb": "nb"}[plan.name]
        if plan.name == "g2":
            U([], lambda ws: fence(numden_set, ["NUMall", "DENall"]))
        U([], lambda ws, pi=pi, plan=plan, mode=mode: attend(pi, plan, mode))
        if STOP == plan.name:
            break
    U([], halt_if("NUM", NUM, 4096, F32))

    def fin_ya(ws):
        for j in range(4):
            for h in range(2):
                sl = slice(h * 512, (h + 1) * 512)
                P.add("dve", RCP(DEN[:, j, sl], DEN[:, j, sl]), reads=["DENall"], writes=[("rden", j, h)])
                P.add("dve", TT(YA[:, j, sl], NUM[:, j, sl], DEN[:, j, sl], ALU.mult),
                      reads=["NUMall", ("rden", j, h)], writes=[("YA", j)])
        dump("YAB", YAB, 8192, BF16)
        old = ([("KT", j, tb) for j in range(4) for tb in range(3)] + vkeys +
               [("QT", j, q) for j in range(4) for q in (0, 512)] + ["tab", "rd"] +
               [("tmps", t, i) for t in range(2) for i in range(4)] + [("pt", t) for t in range(4)])
        fence(old, ["arD"])

    full = STOP not in [p.name for p in plans]
    if full:
        U([], fin_ya)
        own_blk = [(256, [(1, 512)]), (768, [(1, 512)])]

        def emit_gate(ws, m, br):
            for blk in range(2):
                st, dims = own_blk[blk]
                b = nxt("ps", 8)
                sc = (m * 2 + blk) % 2
                for k in range(16):
                    P.add("pe", MM(PS[b][:, :], ap(WBF, ws * 2048 + k * 128, [(1, 128)]),
                                   ap(A, k * 1536 + st, dims), k == 0, k == 15),
                          reads=[("wbf", ws)] + hk, writes=[("ps", b)])
                P.add("act", ACTF(SCR[:, sc * 4 + br, :], PS[b][:, :], AF.Sigmoid,
                                  bias=BGATE[:, br * 16 + m:br * 16 + m + 1]),
                      reads=[("ps", b), "bgate", "arD"], writes=[("scr", sc, br)])

        def emit_proj(ws, m):
            for blk in range(2):
                sc = (m * 2 + blk) % 2
                for br, (Y, yk) in enumerate(((YA, "YA"), (YB, "YB"))):
                    b = nxt("ps", 8)
                    for k in range(4):
                        P.add("pe", MM(PS[b][:, :], ap(WBF, ws[br] * 2048 + k * 128, [(1, 128)]),
                                       Y[:, k, blk * 512:(blk + 1) * 512], k == 0, k == 3),
                              reads=[("wbf", ws[br])] + [(yk, jj) for jj in range(4)], writes=[("ps", b)])
                    P.add("dve", TT(SCR[:, sc * 4 + 2 + br, :], PS[b][:, :], SCR[:, sc * 4 + br, :], ALU.mult),
                          reads=[("ps", b), ("scr", sc, br), "arD"], writes=[("scr", sc, 2 + br)])
                P.add("pool", TT(C[:, m, blk * 512:(blk + 1) * 512], SCR[:, sc * 4 + 2, :], SCR[:, sc * 4 + 3, :], ALU.add),
                      reads=[("scr", sc, 2), ("scr", sc, 3), "arD"], writes=[("C", m, blk)])

        for m in range(16):
            U([(wcols(w_gate, m * 128, 128), 16, 128)], lambda ws, m=m: emit_gate(ws[0], m, 0))
            U([(wcols(w_gate, 2048 + m * 128, 128), 16, 128)], lambda ws, m=m: emit_gate(ws[0], m, 1))
            U([(wcols(w_pa, m * 128, 128, nk=4), 4, 128), (wcols(w_pb, m * 128, 128, nk=4), 4, 128)],
              lambda ws, m=m: emit_proj(ws, m))
        U([], halt_if("MG", C, 16 * 1024, BF16))

        ckeys = [("C", m, blk) for m in range(16) for blk in range(2)]

        def start_E(ws):
            fence(["NUMall", "DENall"] + numden_set + [("rden", j, h) for j in range(4) for h in range(2)] +
                  [("scr", s_, i) for s_ in range(2) for i in range(4)], ["arE"])
            for tt in range(8):
                P.add("sp", DMA(Bf[:, tt, :], x_sets["near"].ap()[256 + tt * 128:256 + (tt + 1) * 128, :]),
                      reads=["arE"], writes=[("x1", tt, 0)], dma=True)

        def emit_out(wss, cg2):
            assert wss[1] == wss[0] + 1 and wss[0] % 2 == 0, wss
            for tt in range(8):
                b = nxt("ps", 8)
                for k in range(16):
                    P.add("pe", MM(PS[b][:, 0:256], C[:, k, tt * 128:(tt + 1) * 128],
                                   ap(WBF, wss[0] * 2048 + k * 128, [(2048, 2), (1, 128)]), k == 0, k == 15),
                          reads=[("wbf", wss[0]), ("wbf", wss[1])] + ckeys, writes=[("ps", b)])
                xsl = Bf[:, tt, cg2 * 256:(cg2 + 1) * 256]
                P.add("dve", TT(xsl, PS[b][:, 0:256], xsl, ALU.add),
                      reads=[("ps", b), ("x1", tt, 0)], writes=[("x1c", tt, cg2 // 2)])

        U([], start_E)
        for cg2 in range(8):
            U([(wcols(w_out, cg2 * 256 + q * 128, 128), 16, 128) for q in range(2)],
              lambda ws, cg2=cg2: emit_out(ws, cg2))
        U([], halt_if("X1", Bf, 8 * 2048, F32))

        h2k = [("A", m, hb) for m in range(8) for hb in range(2)]

        def phase_F(ws):
            fence(hk, ["arF"])
            for tt in range(8):
                norm_tile(Bf[:, tt, :], [("x1c", tt, cg) for cg in range(4)], A, tt * 128, 1536, GMLP, "gmlp", ["arF"])
            fence(ckeys + [("YA", j) for j in range(4)] + [("YB", j) for j in range(4)], ["arG"])

        U([], phase_F)
        U([], halt_if("H2", A, 16 * 1536, BF16))

        def emit_up(ws, fc, f):
            us = fc % 2
            for blk in range(2):
                b = nxt("ps", 8)
                for k in range(16):
                    P.add("pe", MM(PS[b][:, :], ap(WBF, ws * 2048 + k * 128, [(1, 128)]),
                                   A[:, k, blk * 512:(blk + 1) * 512], k == 0, k == 15),
                          reads=[("wbf", ws)] + h2k, writes=[("ps", b)])
                sc = nxt("rl", 2)
                P.add("act", ACTF(RL[:, sc, :], PS[b][:, :], AF.Relu), reads=[("ps", b), "arG"], writes=[("rl", sc)])
                P.add("pool", TT(C[:, us * 8 + f, blk * 512:(blk + 1) * 512], RL[:, sc, :], RL[:, sc, :], ALU.mult),
                      reads=[("rl", sc), "arG"], writes=[("uT", us, f, blk)])

        def emit_down(wss, fc, cg):
            us = fc % 2
            assert wss[1] == wss[0] + 1 and wss[0] % 2 == 0, wss
            ukeys = [("uT", us, f, blk) for f in range(8) for blk in range(2)]
            for tt in range(8):
                b = nxt("ps", 8)
                for f in range(8):
                    P.add("pe", MM(PS[b][:, :], C[:, us * 8 + f, tt * 128:(tt + 1) * 128],
                                   ap(WBF, wss[0] * 2048 + f * 256, [(2048, 2), (1, 256)]), f == 0, f == 7),
                          reads=[("wbf", wss[0]), ("wbf", wss[1])] + ukeys, writes=[("ps", b)])
                xsl = Bf[:, tt, cg * 512:(cg + 1) * 512]
                P.add("dve", TT(xsl, PS[b][:, :], xsl, ALU.add), reads=[("ps", b)], writes=[("x1c", tt, cg)])

        for fc in range(8):
            for f in range(8):
                U([(wcols(w_up, fc * 1024 + f * 128, 128), 16, 128)], lambda ws, fc=fc, f=f: emit_up(ws[0], fc, f))
            for cg in range(4):
                U([(wcols(w_down, cg * 512 + q * 256, 256, r0=fc * 1024, nk=8), 8, 256) for q in range(2)],
                  lambda ws, fc=fc, cg=cg: emit_down(ws, fc, cg))
        U([], halt_if("X2", Bf, 8 * 2048, F32))

        def phase_H(ws):
            P.add("sp", DMA(GFIN[:, :], gfin_d.ap()), writes=["gfin", ("wbf", 0), ("wbf", 1)], dma=True)
            for tt in range(8):
                col = nxt("col", 64)
                xs = nxt("xn", 2)
                xk = [("x1c", tt, cg) for cg in range(4)]
                P.add("act", ACTF(XN[:, xs, :], Bf[:, tt, :], AF.Square, accum_out=SS[:, col:col + 1]),
                      reads=xk, writes=[("xn", xs), ("ss", col)])
                P.add("act", ACTF(RS[:, col:col + 1], SS[:, col:col + 1], AF.Sqrt, bias=EPS, scale=1.0 / D),
                      reads=[("ss", col)], writes=[("rs", col)])
                P.add("dve", RCP(RS[:, col:col + 1], RS[:, col:col + 1]),
                      reads=[("rs", col)], writes=[("rs", col)])
                s = nxt("stg", 2)
                P.add("dve", STT(STG[:, s, :], Bf[:, tt, :], RS[:, col:col + 1], GFIN[:, :], ALU.mult, ALU.mult),
                      reads=xk + [("rs", col), "gfin"], writes=[("stg", s)])
                P.add("sp", DMA(out_d.ap()[tt * 128:(tt + 1) * 128, :], STG[:, s, :]),
                      reads=[("stg", s)], writes=[("out", tt)], dma=True)

        U([], phase_H)

    try:
        run_units()
    except _Stop:
        pass

    tail_reads = [k for k in P.lastw if isinstance(k, tuple) and k[0] == "out"] + (["dbg"] if "dbg" in P.lastw else [])
    P.add("pool", MS(JUNK[:, 15:16], 0.0), reads=tail_reads, writes=["fin"])

    from contextlib import ExitStack
    with ExitStack() as st:
        sems = {e: st.enter_context(nc.semaphore("c_" + e)) for e in ENGS}
        dsems = {"sp": [st.enter_context(nc.semaphore("d_sp%d" % i)) for i in range(Prog.NDSEM)]}
        block = st.enter_context(nc.Block())
        P.emit(block, sems, dsems)
    print("ops:", len(P.ops), {e: sum(1 for o in P.ops if o["eng"] == e) for e in ENGS}, flush=True)
    return nc, (dbg[0] is not None)


_CACHE = {}


def kernel(x, norm_mix, w_qkv, w_gate, b_gate, rpb, w_proj_a, w_proj_b, w_out,
           norm_mlp, w_up, w_down, norm_final):
    f = lambda a: np.ascontiguousarray(np.asarray(a, dtype=np.float32))
    x = f(x)
    plans = make_plans()
    slot_maps, contents = build_tables(plans, f(rpb)[0])
    ntab = sum(contents[0][pi][j].shape[1] for pi in range(len(plans)) for j in range(4))
    nc, has_dbg = build(plans, slot_maps, ntab)

    def colvec(g, n):
        return np.ascontiguousarray(f(g).reshape(n, 128).T)

    shared = {
        "w_qkv": f(w_qkv)[0], "w_gate": f(w_gate)[0], "w_pa": f(w_proj_a)[0], "w_pb": f(w_proj_b)[0],
        "w_out": f(w_out)[0], "w_up": f(w_up)[0], "w_down": f(w_down)[0],
        "gmix": colvec(norm_mix[0], 16), "gmlp": colvec(norm_mlp[0], 16),
        "gfin": np.ascontiguousarray(np.broadcast_to(f(norm_final)[None, :], (128, D))),
        "bgate": colvec(b_gate[0], 32),
        "ident": np.eye(128, dtype=np.float32).astype(ml_dtypes.bfloat16),
    }
    in_maps = []
    for core in range(NCORES):
        b, c = core // 4, core % 4
        m = dict(shared)
        for nm in ("3a", "3b", "near"):
            tok = set_tokens(c, nm)
            ok = (tok >= 0) & (tok < S)
            xs = np.zeros((1536, D), np.float32)
            xs[ok] = x[b, tok[ok]]
            m["x_" + nm] = xs
        m["tabs"] = np.ascontiguousarray(np.concatenate(
            [contents[c][pi][j] for pi in range(len(plans)) for j in range(4)], axis=1))
        in_maps.append(m)
    if os.environ.get("KTRACE"):
        res = run_bass_kernel_spmd(nc, in_maps, core_ids=list(range(NCORES)), trace=True)
        print("exec_time_ns", res.exec_time_ns, flush=True)
    else:
        res = run_bass_kernel_spmd(nc, in_maps, core_ids=list(range(NCORES)))
    if has_dbg:
        kernel.dbg = [np.asarray(r["dbg"]) for r in res.results]
    out = np.zeros((NB, S, D), np.float32)
    for core in range(NCORES):
        b, c = core // 4, core % 4
        out[b, set_tokens(c, "own")] = np.asarray(res.results[core]["out"])
    return out
```

```python
import os
import numpy as np
import ml_dtypes
import concourse.bass as bass
import concourse.mybir as mybir
from concourse.bass_utils import run_bass_kernel_spmd

F32 = mybir.dt.float32
BF16 = mybir.dt.bfloat16
AF = mybir.ActivationFunctionType
ALU = mybir.AluOpType

D = 2048
S = 4096
NB = 2
HD = 128
DFF = 8192
EPS = 1e-6
NEG = -30000.0
SCALE = HD ** -0.5
NCORES = int(os.environ.get('KCORES', '8'))
DILS = (1, 4, 16)

STOP = os.environ.get("KSTOP", "")
DUMP = os.environ.get("KDUMP", "")


def set_tokens(c, name):
    base = 1024 * c - 1024
    if name in ("3a", "3b"):
        r0 = 0 if name == "3a" else 8
        r = np.arange(8)[:, None] + r0
        i = np.arange(192)[None, :]
        return (base + 16 * i + r).reshape(-1)
    if name == "near":
        return base + np.arange(768, 2304)
    if name == "own":
        return base + np.arange(1024, 2048)
    raise KeyError(name)


def pat_idx(start, dims):
    idx = np.array([start])
    for step, cnt in dims:
        idx = (idx[:, None] + step * np.arange(cnt)[None, :]).reshape(-1)
    return idx


def alibi_slopes():
    return 2.0 ** (-8.0 * np.arange(1, 13) / 12.0)


def bias_block(kind, hidx, ktok, qtok, rpb):
    k = ktok[:, None].astype(np.int64)
    q = qtok[None, :].astype(np.int64)
    kin = (k >= 0) & (k < S)
    if kind < 3:
        d = DILS[kind]
        slope = alibi_slopes()[kind * 4 + hidx]
        same = (np.mod(k, d) == np.mod(q, d))
        ik = np.floor_divide(k, d)
        iq = np.floor_divide(q, d)
        rel = np.abs(ik - iq)
        valid = kin & same & (rel <= 64)
        b = -(np.float32(slope) * np.float32(d)) * rel.astype(np.float32)
        return np.where(valid, b, np.float32(NEG)).astype(np.float32)
    rk, ck = np.floor_divide(k, 64), np.mod(k, 64)
    rq, cq = np.floor_divide(q, 64), np.mod(q, 64)
    rs = np.clip(rq - 4, 0, 56)
    cs = np.clip(cq - 8, 0, 48)
    valid = kin & (rk >= rs) & (rk < rs + 8) & (ck >= cs) & (ck < cs + 16)
    dr = np.clip(rk - rq + 7, 0, 14)
    dc = np.clip(ck - cq, -15, 15) + 15
    g = rpb[hidx][dr, dc]
    return np.where(valid, g, np.float32(NEG)).astype(np.float32)


class AttnPlan:
    def __init__(self, name, tokset, kind, head0):
        self.name, self.tokset, self.kind, self.head0 = name, tokset, kind, head0
        self.vtiles = []
        self.qgroups = []
        self.qblocks = []

    def vt(self, start, dims):
        key = (start, tuple(map(tuple, dims)))
        for n, v in enumerate(self.vtiles):
            if v == key:
                return n
        self.vtiles.append(key)
        return len(self.vtiles) - 1


def make_plans():
    plans = []
    for hf, nm in enumerate(("3a", "3b")):
        p = AttnPlan(nm, nm, 2, 8)
        for a in range(4):
            tl = [p.vt(128 * (3 * a + t), [(1, 128)]) for t in range(3)]
            p.qgroups.append((8 * hf + 2 * a, [(1, 2), (16, 64)], tl, (a * 128, [(1, 128)])))
        p.qblocks = [(64, [(192, 8), (1, 64)], 0)]
        plans.append(p)
    near_q = [(256, [(1, 512)], 0), (768, [(1, 512)], 512)]
    p = AttnPlan("g2", "near", 1, 4)
    for rp in range(4):
        kts = [p.vt(4 * k0 - 768 + rp, [(4, 128)]) for k0 in (192, 320, 448)]
        for t in range(2):
            own = (512 * t + rp, [(4, 128)])
            p.qgroups.append((own[0], own[1], [kts[t], kts[t + 1]], own))
    p.qblocks = near_q
    plans.append(p)
    p = AttnPlan("g1", "near", 0, 0)
    kts = [p.vt(192 + 128 * m, [(1, 128)]) for m in range(9)]
    for j in range(8):
        own = (128 * j, [(1, 128)])
        p.qgroups.append((own[0], own[1], [kts[j], kts[j + 1]], own))
    p.qblocks = near_q
    plans.append(p)
    p = AttnPlan("nb", "near", 3, 12)
    for g in range(8):
        R = 16 + 2 * g
        tiles = list(range(R - 4, R + 5, 2))
        if g == 0:
            tiles.append(22)
        if g == 7:
            tiles.insert(0, 24)
        tiles = sorted(set(tiles))
        kts = [p.vt(64 * K0 - 768, [(1, 128)]) for K0 in tiles]
        own = (128 * g, [(1, 128)])
        p.qgroups.append((own[0], own[1], kts, own))
    p.qblocks = near_q
    plans.append(p)
    return plans


def build_tables(plans, rpb_all):
    slot_maps, contents = [], [[] for _ in range(4)]
    for p in plans:
        toks = [set_tokens(c, p.tokset) for c in range(4)]
        owns = [set_tokens(c, "own") for c in range(4)]
        pm, pc = [], [[] for _ in range(4)]
        for j in range(4):
            seen, smap, tabs = {}, {}, [[] for _ in range(4)]
            for g, (os_, od, tl, _) in enumerate(p.qgroups):
                qi = pat_idx(os_, od)
                for n, vt in enumerate(tl):
                    ki = pat_idx(*p.vtiles[vt])
                    blks = [bias_block(p.kind, j, toks[c][ki], owns[c][qi],
                                       rpb_all) for c in range(4)]
                    key = b"".join(b.tobytes() for b in blks)
                    if key not in seen:
                        seen[key] = len(tabs[0])
                        for c in range(4):
                            tabs[c].append(blks[c])
                    smap[(g, n)] = seen[key]
            pm.append(smap)
            for c in range(4):
                pc[c].append(np.stack(tabs[c], axis=1))
        slot_maps.append(pm)
        for c in range(4):
            contents[c].append(pc[c])
    return slot_maps, contents


ENGS = ("sp", "act", "dve", "pool", "pe")


class Prog:
    NDSEM = 8

    def __init__(self, nc):
        self.nc = nc
        self.ops = []
        self.lastw = {}
        self.readers = {}

    def add(self, eng, fn, reads=(), writes=(), dma=False):
        idx = len(self.ops)
        deps = set()
        for r in reads:
            w = self.lastw.get(r)
            if w is not None:
                deps.add(w)
        for r in writes:
            w = self.lastw.get(r)
            if w is not None:
                deps.add(w)
            deps.update(self.readers.get(r, ()))
        for r in reads:
            self.readers.setdefault(r, []).append(idx)
        for r in writes:
            self.lastw[r] = idx
            self.readers[r] = []
        deps.discard(idx)
        if eng == "pe":
            deps = {d for d in deps if self.ops[d]["eng"] != "pe"}
        self.ops.append(dict(eng=eng, fn=fn, deps=deps, dma=dma, idx=idx))
        return idx

    def emit(self, block, sems, dsems):
        ops = self.ops
        needed = set()
        for o in ops:
            needed.update(o["deps"])
        cnt = {e: 0 for e in ENGS}
        dcnt = {e: 0 for e in ENGS}
        for o in ops:
            e = o["eng"]
            if o["dma"]:
                j = dcnt[e]
                dcnt[e] += 1
                o["sem"] = dsems[e][j % self.NDSEM]
                o["val"] = 16 * (j // self.NDSEM + 1)
                o["dj"] = j
            elif o["idx"] in needed or o.get("force"):
                cnt[e] += 1
                o["sem"] = sems[e]
                o["val"] = cnt[e]
            else:
                o["sem"] = None
        per = {e: [o for o in ops if o["eng"] == e] for e in ENGS}

        def run(engname, eng):
            waited = {}

            def wait(sem, val):
                if waited.get(sem.num if hasattr(sem, "num") else id(sem), 0) < val:
                    eng.wait_ge(sem, val)
                    waited[sem.num if hasattr(sem, "num") else id(sem)] = val

            for o in per[engname]:
                for d in sorted(o["deps"]):
                    od = ops[d]
                    wait(od["sem"], od["val"])
                if o["dma"] and o["val"] > 16:
                    wait(o["sem"], o["val"] - 16)
                ins = o["fn"](eng)
                if o["sem"] is not None:
                    ins.then_inc(o["sem"], 16 if o["dma"] else 1)

        block.sync(lambda e: run("sp", e))
        block.scalar(lambda e: run("act", e))
        block.vector(lambda e: run("dve", e))
        block.gpsimd(lambda e: run("pool", e))
        block.tensor(lambda e: run("pe", e))


class _Stop(Exception):
    pass


def MM(out, lhsT, rhs, start, stop):
    return lambda e: e.matmul(out, lhsT=lhsT, rhs=rhs, start=start, stop=stop)


def ACTF(out, in_, func, **kw):
    return lambda e: e.activation(out=out, in_=in_, func=func, **kw)


def TT(out, in0, in1, op):
    return lambda e: e.tensor_tensor(out=out, in0=in0, in1=in1, op=op)


def TS(out, in0, s1, s2, op0, op1=None):
    if op1 is None:
        return lambda e: e.tensor_scalar(out=out, in0=in0, scalar1=s1, scalar2=s2, op0=op0)
    return lambda e: e.tensor_scalar(out=out, in0=in0, scalar1=s1, scalar2=s2, op0=op0, op1=op1)


def STT(out, in0, scalar, in1, op0, op1):
    return lambda e: e.scalar_tensor_tensor(out=out, in0=in0, scalar=scalar, in1=in1, op0=op0, op1=op1)


def CP(out, in_):
    return lambda e: e.tensor_copy(out=out, in_=in_)


def DMA(out, in_):
    return lambda e: e.dma_start(out=out, in_=in_)


def TR(out, in_, ident):
    return lambda e: e.transpose(out=out, in_=in_, identity=ident)


def RCP(out, in_):
    return lambda e: e.reciprocal(out=out, in_=in_)


def MS(a, val):
    return lambda e: e.memset(a, val)


def build(plans, slot_maps, ntab):
    nc = bass.Bass("TRN2", target_bir_lowering=False)
    P = Prog(nc)

    def din(name, shape, dt=F32):
        return nc.dram_tensor(name, list(shape), dt, kind="ExternalInput")

    x_sets = {n: din("x_" + n, [1536, D]) for n in ("3a", "3b", "near")}
    w_qkv = din("w_qkv", [D, 3 * D])
    w_gate = din("w_gate", [D, 2 * D])
    w_pa = din("w_pa", [512, D])
    w_pb = din("w_pb", [512, D])
    w_out = din("w_out", [D, D])
    w_up = din("w_up", [D, DFF])
    w_down = din("w_down", [DFF, D])
    gmix_d = din("gmix", [128, 16])
    gmlp_d = din("gmlp", [128, 16])
    gfin_d = din("gfin", [128, D])
    bgate_d = din("bgate", [128, 32])
    ident_d = din("ident", [128, 128], BF16)
    tabs_d = din("tabs", [128, ntab, 128])
    out_d = nc.dram_tensor("out", [1024, D], F32, kind="ExternalOutput")
    dbg = [None]

    pstride = {}

    def sb(name, shape, dt, off):
        t = nc.alloc_sbuf_tensor_at(name, list(shape), dt, offset=16512 + off)
        pstride[t.name] = int(np.prod(shape[1:]))
        return t

    KB = 1024
    A = sb("A", [128, 16, 1536], BF16, 0)
    Bf = sb("Bf", [128, 8, 2048], F32, 48 * KB)
    NUM = sb("NUM", [128, 4, 1024], F32, 48 * KB)
    DEN = sb("DEN", [128, 4, 1024], F32, 64 * KB)
    KT = sb("KT", [128, 4, 1536], BF16, 80 * KB)
    V = sb("V", [128, 12, 512], BF16, 92 * KB)
    QT = sb("QT", [128, 4, 1024], BF16, 104 * KB)
    SCR = sb("SCR", [128, 8, 512], F32, 80 * KB)
    C = sb("C", [128, 16, 1024], BF16, 112 * KB)
    TAB = sb("TAB", [128, 44, 128], F32, 112 * KB)
    TMPS = sb("TMPS", [128, 2, 512], F32, 134 * KB)
    PT = sb("PT", [128, 4, 512], BF16, 138 * KB)
    RD = sb("RD", [128, 512], F32, 142 * KB)
    YA = sb("YA", [128, 4, 1024], BF16, 144 * KB)
    YB = sb("YB", [128, 4, 1024], BF16, 152 * KB)
    YAB = sb("YAB", [128, 8192], BF16, 144 * KB)
    RL = sb("RL", [128, 2, 512], F32, 144 * KB)
    STG = sb("STG", [128, 2, 2048], F32, 160 * KB)
    WBF = sb("WBF", [128, 4, 2048], BF16, 176 * KB)
    GFIN = sb("GFIN", [128, 2048], F32, 176 * KB)
    XN = sb("XN", [128, 2, 2048], BF16, 192 * KB)
    cb = 200 * KB
    IDENT = sb("IDENT", [128, 128], BF16, cb)
    ONES = sb("ONES", [128, 128], BF16, cb + 256)
    GMIX = sb("GMIX", [128, 16], F32, cb + 512)
    GMLP = sb("GMLP", [128, 16], F32, cb + 576)
    BGATE = sb("BGATE", [128, 32], F32, cb + 640)
    SS = sb("SS", [128, 64], F32, cb + 768)
    RS = sb("RS", [128, 64], F32, cb + 1024)
    JUNK = sb("JUNK", [128, 16], F32, cb + 1280)

    PS = [nc.alloc_psum_tensor("ps%d" % i, [128, 512], F32) for i in range(8)]
    for i in range(8):
        pstride[PS[i].name] = 512

    def ap(t, off, dims, p0=0, np_=128):
        ps_ = pstride[t.name]
        return bass.AP(t, p0 * ps_ + off, [[ps_, np_]] + [list(d) for d in dims])

    for dst, srcd, key in ((IDENT, ident_d, "ident"), (GMIX, gmix_d, "gmix"), (GMLP, gmlp_d, "gmlp"),
                           (BGATE, bgate_d, "bgate")):
        P.add("sp", DMA(dst[:, :], srcd.ap()), writes=[key], dma=True)
    P.add("pool", MS(ONES[:, :], 1.0), writes=["ones"])

    ctr = dict(stg=0, wbf=0, xn=0, ps=0, pss=0, pso=0, col=0, tmps=0, pt=0, cast=0, ev=0, rl=0, junk=0, tb=0)

    def nxt(k, n):
        v = ctr[k] % n
        ctr[k] += 1
        return v

    def fence(old_keys, new_keys=()):
        jc = nxt("junk", 16)
        P.add("pool", MS(JUNK[:, jc:jc + 1], 0.0), writes=list(old_keys) + list(new_keys) + [("junk", jc)])

    def dump(tag, t, n, dt=F32):
        if DUMP != tag or dbg[0] is not None:
            return
        dbg[0] = nc.dram_tensor("dbg", [128, n], dt, kind="ExternalOutput")
        keys = list(P.lastw.keys())
        P.add("sp", DMA(dbg[0].ap(), ap(t, 0, [(1, n)])), reads=keys, writes=["dbg"], dma=True)
        if STOP == "dump":
            raise _Stop()

    def norm_steps(xin, xkeys, dst_t, dst_off, dst_kstride, gam, gam_key, extra_reads, tbanks, pre=None):
        st = {}

        def part1():
            if pre is not None:
                pre()
            col = nxt("col", 64)
            xs = nxt("xn", 2)
            st["xs"] = xs
            P.add("act", ACTF(XN[:, xs, :], xin, AF.Square, accum_out=SS[:, col:col + 1]),
                  reads=xkeys, writes=[("xn", xs), ("ss", col)])
            P.add("act", ACTF(RS[:, col:col + 1], SS[:, col:col + 1], AF.Ln, bias=EPS, scale=1.0 / D),
                  reads=[("ss", col)], writes=[("rs", col)])
            P.add("act", ACTF(RS[:, col:col + 1], RS[:, col:col + 1], AF.Exp, scale=-0.5),
                  reads=[("rs", col)], writes=[("rs", col)])
            P.add("dve", TS(XN[:, xs, :], xin, RS[:, col:col + 1], None, ALU.mult),
                  reads=xkeys + [("rs", col)], writes=[("xn", xs)])

        def part2():
            xs = st["xs"]
            for hb in range(2):
                b = tbanks[nxt("tb", len(tbanks))]
                pb = PS[b][:, :].bitcast(BF16)
                for kk in range(8):
                    k = hb * 8 + kk
                    P.add("pe", TR(pb[:, kk * 128:(kk + 1) * 128], XN[:, xs, k * 128:(k + 1) * 128], IDENT[:, :]),
                          reads=[("xn", xs), "ident"], writes=[("ps", b)])
                o = ap(dst_t, hb * 8 * dst_kstride + dst_off, [(dst_kstride, 8), (1, 128)])
                g = ap(gam, hb * 8, [(1, 8), (0, 128)])
                i0 = pb.rearrange("p (k t) -> p k t", k=8)
                P.add("dve", TT(o, i0, g, ALU.mult),
                      reads=[("ps", b), gam_key] + extra_reads, writes=[(dst_t.name, dst_off // 128, hb)])

        return part1, part2

    def skewed(parts, skew=1):
        steps = []
        n = len(parts)
        for i in range(n + skew):
            if i < n:
                steps.append(parts[i][0])
            if i - skew >= 0:
                steps.append(parts[i - skew][1])
        return steps

    def run_merged(sa, sb_):
        ia = ib = 0
        while ia < len(sa) or ib < len(sb_):
            if ib >= len(sb_) or (ia < len(sa) and ia * len(sb_) <= ib * len(sa)):
                sa[ia]()
                ia += 1
            else:
                sb_[ib]()
                ib += 1

    def load_w(src_ap, nk, ncol, ws):
        s = nxt("stg", 2)
        n = nk * ncol
        P.add("sp", DMA(ap(STG, s * 2048, [(ncol, nk), (1, ncol)]), src_ap), writes=[("stg", s)], dma=True)
        ce = ("dve", "act", "dve")[nxt("cast", 3)]
        o = ap(WBF, ws * 2048, [(1, n)])
        i = ap(STG, s * 2048, [(1, n)])
        if ce == "act":
            P.add("act", ACTF(o, i, AF.Copy), reads=[("stg", s)], writes=[("wbf", ws)])
        else:
            P.add("dve", CP(o, i), reads=[("stg", s)], writes=[("wbf", ws)])
        return ws

    def wcols(w, c0, ncol, r0=0, nk=16):
        a = w.ap()[r0:r0 + nk * 128, c0:c0 + ncol]
        return a.rearrange("(k p) n -> p k n", p=128)

    def evac_copy(dst, src, rk, wk):
        if nxt("ev", 2) == 0:
            P.add("act", ACTF(dst, src, AF.Copy), reads=rk, writes=wk)
        else:
            P.add("dve", CP(dst, src), reads=rk, writes=wk)

    units = []

    def U(loads, fn):
        units.append((loads, fn))

    def run_units():
        wpos = 0
        pending = []
        slots_of = {}
        nl = 0
        for i, (loads, fn) in enumerate(units):
            while nl < len(units):
                lo = units[nl][0]
                if not lo:
                    slots_of[nl] = []
                    nl += 1
                    continue
                n = len(lo)
                start = wpos + (1 if (n == 2 and wpos % 2 == 1) else 0)
                oldest = pending[0][1] if pending else start
                if start + n - oldest > 4 or nl > i + 6:
                    break
                wpos = start + n
                sl = [(start + q) % 4 for q in range(n)]
                for (sa, nk, ncol), ws in zip(lo, sl):
                    load_w(sa, nk, ncol, ws)
                slots_of[nl] = sl
                pending.append((nl, start))
                nl += 1
            assert i in slots_of, (i, nl)
            fn(slots_of[i])
            pending = [p for p in pending if p[0] != i]

    tab_off = [0]
    hk = [("A", m, hb) for m in range(12) for hb in range(2)]

    def hT_steps(setname, tbanks):
        xd = x_sets[setname]
        parts = []
        for m in range(12):
            box = {}

            def p1(m=m, box=box):
                s = nxt("stg", 2)
                P.add("sp", DMA(STG[:, s, :], xd.ap()[m * 128:(m + 1) * 128, :]), writes=[("stg", s)], dma=True)
                q1, q2 = norm_steps(STG[:, s, :], [("stg", s)], A, m * 128, 1536, GMIX, "gmix", [], tbanks)
                box["p2"] = q2
                q1()

            def p2(box=box):
                box["p2"]()

            parts.append((p1, p2))
        return skewed(parts, 1)

    def build_hT(setname):
        for s_ in hT_steps(setname, list(range(8))):
            s_()

    def emit_K(ws, j):
        for tb in range(3):
            b = nxt("ps", 8)
            for k in range(16):
                P.add("pe", MM(PS[b][:, :], ap(WBF, ws * 2048 + k * 128, [(1, 128)]),
                               ap(A, k * 1536 + tb * 512, [(1, 512)]), k == 0, k == 15),
                      reads=[("wbf", ws)] + hk, writes=[("ps", b)])
            evac_copy(ap(KT, j * 1536 + tb * 512, [(1, 512)]), PS[b][:, :], [("ps", b)], [("KT", j, tb)])

    def emit_Q(ws, j, qblocks):
        for (st, dims, qoff) in qblocks:
            b = nxt("ps", 8)
            for k in range(16):
                P.add("pe", MM(PS[b][:, :], ap(WBF, ws * 2048 + k * 128, [(1, 128)]),
                               ap(A, k * 1536 + st, dims), k == 0, k == 15),
                      reads=[("wbf", ws)] + hk, writes=[("ps", b)])
            evac_copy(ap(QT, j * 1024 + qoff, [(1, 512)]), PS[b][:, :], [("ps", b)], [("QT", j, qoff)])

    def emit_V(wss, half, vtiles):
        assert wss[1] == wss[0] + 1 and wss[0] % 2 == 0, wss
        for n, (st, dims) in enumerate(vtiles):
            b = nxt("ps", 8)
            for k in range(16):
                P.add("pe", MM(PS[b][:, 0:256], ap(A, k * 1536 + st, list(dims)),
                               ap(WBF, wss[0] * 2048 + k * 128, [(2048, 2), (1, 128)]), k == 0, k == 15),
                      reads=[("wbf", wss[0]), ("wbf", wss[1])] + hk, writes=[("ps", b)])
            evac_copy(V[:, n, half * 256:(half + 1) * 256], PS[b][:, 0:256], [("ps", b)], [("V", n, half)])

    def project(plan):
        for j in range(4):
            U([(wcols(w_qkv, 2048 + (plan.head0 + j) * 128, 128), 16, 128)],
              lambda ws, j=j: emit_K(ws[0], j))
        for j in range(4):
            U([(wcols(w_qkv, (plan.head0 + j) * 128, 128), 16, 128)],
              lambda ws, j=j, qb=plan.qblocks: emit_Q(ws[0], j, qb))
        for half in range(2):
            U([(wcols(w_qkv, 4096 + (plan.head0 + 2 * half + q) * 128, 128), 16, 128) for q in range(2)],
              lambda ws, half=half, vt=list(plan.vtiles): emit_V(ws, half, vt))

    SBANKS = [0, 1]
    OBANKS = [(2, 3), (4, 5)]

    def attend_steps(pi, plan, mode):
        smaps = slot_maps[pi]
        parts = []
        obox = {"ob": None}
        for j in range(4):
            smap = smaps[j]
            nslot = max(smap.values()) + 1
            assert nslot <= 44, nslot
            t0 = tab_off[0]
            tab_off[0] += nslot
            blocks = []
            for g, (os_, od, tl, qpat) in enumerate(plan.qgroups):
                for n, vt in enumerate(tl):
                    blocks.append((g, n, vt, len(tl)))
            kt_keys = [("KT", j, tb) for tb in range(3)]
            for c0 in range(0, len(blocks), 4):
                chunk = blocks[c0:c0 + 4]
                box = {}

                def pA(j=j, smap=smap, nslot=nslot, t0=t0, chunk=chunk, first=(c0 == 0), box=box, kt_keys=kt_keys):
                    if first:
                        P.add("sp", DMA(TAB[:, 0:nslot, :], tabs_d.ap()[:, t0:t0 + nslot, :]), writes=["tab"], dma=True)
                    sbk = SBANKS[nxt("pss", len(SBANKS))]
                    for ci, (g, n, vt, ntl) in enumerate(chunk):
                        vst, vdims = plan.vtiles[vt]
                        qst, qd = plan.qgroups[g][3]
                        P.add("pe", MM(PS[sbk][:, ci * 128:(ci + 1) * 128], ap(KT, j * 1536 + vst, list(vdims)),
                                       ap(QT, j * 1024 + qst, list(qd)), True, True),
                              reads=kt_keys + [("QT", j, 0), ("QT", j, 512)], writes=[("ps", sbk)])
                    ts = nxt("tmps", 2)
                    slots = [smap[(g, n)] for (g, n, vt, ntl) in chunk]
                    i = 0
                    while i < len(slots):
                        jn = i + 1
                        stp = 0
                        if jn < len(slots):
                            stp = slots[jn] - slots[i]
                            jn += 1
                            while jn < len(slots) and slots[jn] - slots[jn - 1] == stp:
                                jn += 1
                        cnt_ = jn - i
                        P.add("dve", STT(ap(TMPS, ts * 512 + i * 128, [(128, cnt_), (1, 128)]),
                                         ap(PS[sbk], i * 128, [(128, cnt_), (1, 128)]), SCALE,
                                         ap(TAB, slots[i] * 128, [(stp * 128, cnt_), (1, 128)]), ALU.mult, ALU.add),
                              reads=[("ps", sbk), "tab"], writes=[("tmps", ts, i)])
                        i = jn
                    pt = nxt("pt", 4)
                    ncol = len(chunk) * 128
                    P.add("act", ACTF(PT[:, pt, 0:ncol], TMPS[:, ts, 0:ncol], AF.Exp),
                          reads=[("tmps", ts, i2) for i2 in range(4)], writes=[("pt", pt)])
                    box["pt"] = pt

                def pB(j=j, chunk=chunk, box=box):
                    pt = box["pt"]
                    for ci, (g, n, vt, ntl) in enumerate(chunk):
                        gi = g % 4
                        if gi == 0 and n == 0:
                            obox["ob"] = OBANKS[nxt("pso", len(OBANKS))]
                        ob = obox["ob"]
                        P.add("pe", MM(PS[ob[0]][:, gi * 128:(gi + 1) * 128], V[:, vt, j * 128:(j + 1) * 128],
                                       PT[:, pt, ci * 128:(ci + 1) * 128], n == 0, n == ntl - 1),
                              reads=[("V", vt, j // 2), ("pt", pt)], writes=[("ps", ob[0])])
                        P.add("pe", MM(PS[ob[1]][:, gi * 128:(gi + 1) * 128], ONES[:, :],
                                       PT[:, pt, ci * 128:(ci + 1) * 128], n == 0, n == ntl - 1),
                              reads=["ones", ("pt", pt)], writes=[("ps", ob[1])])
                        if n != ntl - 1:
                            continue
                        os_, od = plan.qgroups[g][0], plan.qgroups[g][1]
                        if len(od) == 2:
                            c1, c0_ = od[1][1], od[0][1]
                            pdims = [(c1, c0_), (1, c1)]
                        else:
                            pdims = [(1, 128)]
                        pn = ap(PS[ob[0]], gi * 128, pdims)
                        pd = ap(PS[ob[1]], gi * 128, pdims)
                        if mode == "set":
                            P.add("dve", CP(ap(NUM, j * 1024 + os_, od), pn), reads=[("ps", ob[0])], writes=[("NUM", j, g)])
                            P.add("act", ACTF(ap(DEN, j * 1024 + os_, od), pd, AF.Copy),
                                  reads=[("ps", ob[1])], writes=[("DEN", j, g)])
                        elif mode == "add":
                            dn = ap(NUM, j * 1024 + os_, od)
                            dd = ap(DEN, j * 1024 + os_, od)
                            P.add("dve", TT(dn, pn, dn, ALU.add), reads=[("ps", ob[0])], writes=["NUMall"])
                            P.add("dve", TT(dd, pd, dd, ALU.add), reads=[("ps", ob[1])], writes=["DENall"])
                        else:
                            rd = ap(RD, 0, pdims)
                            P.add("dve", RCP(rd, pd), reads=[("ps", ob[1])], writes=["rd"])
                            P.add("dve", TT(ap(YB, j * 1024 + os_, od), pn, rd, ALU.mult),
                                  reads=[("ps", ob[0]), "rd"], writes=[("YB", j)])

                parts.append((pA, pB))
        return skewed(parts, 1)

    numden_set = [("NUM", j, g) for j in range(4) for g in range(4)] + [("DEN", j, g) for j in range(4) for g in range(4)]
    vkeys = [("V", n, h) for n in range(12) for h in range(2)]

    class _Halt(Exception):
        pass

    def halt_if(tag, t, n, dt):
        def f(ws):
            dump(tag, t, n, dt)
        return f

    cur = None
    for pi, plan in enumerate(plans):
        if plan.tokset != cur:
            U([], lambda ws, nm=plan.tokset: build_hT(nm))
            U([], halt_if("hT_" + plan.tokset, A, 16 * 1536, BF16))
            cur = plan.tokset
        project(plan)
        mode = {"3a": "set", "3b": "set", "g2": "add", "g1": "add", "nb": "nb"}[plan.name]
        if plan.name == "g2":
            U([], lambda ws: fence(numden_set, ["NUMall", "DENall"]))
        nxt_set = plans[pi + 1].tokset if pi + 1 < len(plans) else None
        if nxt_set is not None and nxt_set != cur and STOP != plan.name:
            U([], lambda ws, pi=pi, plan=plan, mode=mode, ns=nxt_set:
              run_merged(attend_steps(pi, plan, mode), hT_steps(ns, [6, 7])))
            cur = nxt_set
        else:
            U([], lambda ws, pi=pi, plan=plan, mode=mode: [s_() for s_ in attend_steps(pi, plan, mode)])
        if STOP == plan.name:
            break
    U([], halt_if("NUM", NUM, 4096, F32))

    def fin_ya(ws):
        for j in range(4):
            for h in range(2):
                sl = slice(h * 512, (h + 1) * 512)
                P.add("dve", RCP(DEN[:, j, sl], DEN[:, j, sl]), reads=["DENall"], writes=[("rden", j, h)])
                P.add("dve", TT(YA[:, j, sl], NUM[:, j, sl], DEN[:, j, sl], ALU.mult),
                      reads=["NUMall", ("rden", j, h)], writes=[("YA", j)])
        dump("YAB", YAB, 8192, BF16)
        old = ([("KT", j, tb) for j in range(4) for tb in range(3)] + vkeys +
               [("QT", j, q) for j in range(4) for q in (0, 512)] + ["tab", "rd"] +
               [("tmps", t, i) for t in range(2) for i in range(4)] + [("pt", t) for t in range(4)])
        fence(old, ["arD"])

    full = STOP not in [p.name for p in plans]
    if full:
        U([], fin_ya)
        own_blk = [(256, [(1, 512)]), (768, [(1, 512)])]

        def emit_gate(ws, m, br):
            for blk in range(2):
                st, dims = own_blk[blk]
                b = nxt("ps", 8)
                sc = (m * 2 + blk) % 2
                for k in range(16):
                    P.add("pe", MM(PS[b][:, :], ap(WBF, ws * 2048 + k * 128, [(1, 128)]),
                                   ap(A, k * 1536 + st, dims), k == 0, k == 15),
                          reads=[("wbf", ws)] + hk, writes=[("ps", b)])
                P.add("act", ACTF(SCR[:, sc * 4 + br, :], PS[b][:, :], AF.Sigmoid,
                                  bias=BGATE[:, br * 16 + m:br * 16 + m + 1]),
                      reads=[("ps", b), "bgate", "arD"], writes=[("scr", sc, br)])

        def emit_proj(ws, m):
            for blk in range(2):
                sc = (m * 2 + blk) % 2
                for br, (Y, yk) in enumerate(((YA, "YA"), (YB, "YB"))):
                    b = nxt("ps", 8)
                    for k in range(4):
                        P.add("pe", MM(PS[b][:, :], ap(WBF, ws[br] * 2048 + k * 128, [(1, 128)]),
                                       Y[:, k, blk * 512:(blk + 1) * 512], k == 0, k == 3),
                              reads=[("wbf", ws[br])] + [(yk, jj) for jj in range(4)], writes=[("ps", b)])
                    P.add("dve", TT(SCR[:, sc * 4 + 2 + br, :], PS[b][:, :], SCR[:, sc * 4 + br, :], ALU.mult),
                          reads=[("ps", b), ("scr", sc, br), "arD"], writes=[("scr", sc, 2 + br)])
                P.add("pool", TT(C[:, m, blk * 512:(blk + 1) * 512], SCR[:, sc * 4 + 2, :], SCR[:, sc * 4 + 3, :], ALU.add),
                      reads=[("scr", sc, 2), ("scr", sc, 3), "arD"], writes=[("C", m, blk)])

        for m in range(16):
            U([(wcols(w_gate, m * 128, 128), 16, 128)], lambda ws, m=m: emit_gate(ws[0], m, 0))
            U([(wcols(w_gate, 2048 + m * 128, 128), 16, 128)], lambda ws, m=m: emit_gate(ws[0], m, 1))
            U([(wcols(w_pa, m * 128, 128, nk=4), 4, 128), (wcols(w_pb, m * 128, 128, nk=4), 4, 128)],
              lambda ws, m=m: emit_proj(ws, m))
        U([], halt_if("MG", C, 16 * 1024, BF16))

        ckeys = [("C", m, blk) for m in range(16) for blk in range(2)]

        def start_E(ws):
            fence(["NUMall", "DENall"] + numden_set + [("rden", j, h) for j in range(4) for h in range(2)] +
                  [("scr", s_, i) for s_ in range(2) for i in range(4)], ["arE"])
            for tt in range(8):
                P.add("sp", DMA(Bf[:, tt, :], x_sets["near"].ap()[256 + tt * 128:256 + (tt + 1) * 128, :]),
                      reads=["arE"], writes=[("x1", tt, 0)], dma=True)

        def emit_out(wss, cg2):
            assert wss[1] == wss[0] + 1 and wss[0] % 2 == 0, wss
            for tt in range(8):
                b = nxt("ps", 8)
                for k in range(16):
                    P.add("pe", MM(PS[b][:, 0:256], C[:, k, tt * 128:(tt + 1) * 128],
                                   ap(WBF, wss[0] * 2048 + k * 128, [(2048, 2), (1, 128)]), k == 0, k == 15),
                          reads=[("wbf", wss[0]), ("wbf", wss[1])] + ckeys, writes=[("ps", b)])
                xsl = Bf[:, tt, cg2 * 256:(cg2 + 1) * 256]
                P.add("dve", TT(xsl, PS[b][:, 0:256], xsl, ALU.add),
                      reads=[("ps", b), ("x1", tt, 0)], writes=[("x1c", tt, cg2 // 2)])

        U([], start_E)
        for cg2 in range(8):
            U([(wcols(w_out, cg2 * 256 + q * 128, 128), 16, 128) for q in range(2)],
              lambda ws, cg2=cg2: emit_out(ws, cg2))
        U([], halt_if("X1", Bf, 8 * 2048, F32))

        h2k = [("A", m, hb) for m in range(8) for hb in range(2)]

        def phase_F(ws):
            fence(hk, ["arF"])
            parts = [norm_steps(Bf[:, tt, :], [("x1c", tt, cg) for cg in range(4)], A, tt * 128, 1536,
                                GMLP, "gmlp", ["arF"], list(range(8))) for tt in range(8)]
            for s_ in skewed(parts, 1):
                s_()
            fence(ckeys + [("YA", j) for j in range(4)] + [("YB", j) for j in range(4)], ["arG"])

        U([], phase_F)
        U([], halt_if("H2", A, 16 * 1536, BF16))

        def emit_up(ws, fc, f):
            us = fc % 2
            for blk in range(2):
                b = nxt("ps", 8)
                for k in range(16):
                    P.add("pe", MM(PS[b][:, :], ap(WBF, ws * 2048 + k * 128, [(1, 128)]),
                                   A[:, k, blk * 512:(blk + 1) * 512], k == 0, k == 15),
                          reads=[("wbf", ws)] + h2k, writes=[("ps", b)])
                sc = nxt("rl", 2)
                P.add("act", ACTF(RL[:, sc, :], PS[b][:, :], AF.Relu), reads=[("ps", b), "arG"], writes=[("rl", sc)])
                P.add("pool", TT(C[:, us * 8 + f, blk * 512:(blk + 1) * 512], RL[:, sc, :], RL[:, sc, :], ALU.mult),
                      reads=[("rl", sc), "arG"], writes=[("uT", us, f, blk)])

        def emit_down(wss, fc, cg):
            us = fc % 2
            assert wss[1] == wss[0] + 1 and wss[0] % 2 == 0, wss
            ukeys = [("uT", us, f, blk) for f in range(8) for blk in range(2)]
            for tt in range(8):
                b = nxt("ps", 8)
                for f in range(8):
                    P.add("pe", MM(PS[b][:, :], C[:, us * 8 + f, tt * 128:(tt + 1) * 128],
                                   ap(WBF, wss[0] * 2048 + f * 256, [(2048, 2), (1, 256)]), f == 0, f == 7),
                          reads=[("wbf", wss[0]), ("wbf", wss[1])] + ukeys, writes=[("ps", b)])
                xsl = Bf[:, tt, cg * 512:(cg + 1) * 512]
                P.add("dve", TT(xsl, PS[b][:, :], xsl, ALU.add), reads=[("ps", b)], writes=[("x1c", tt, cg)])

        for fc in range(8):
            for f in range(8):
                U([(wcols(w_up, fc * 1024 + f * 128, 128), 16, 128)], lambda ws, fc=fc, f=f: emit_up(ws[0], fc, f))
            for cg in range(4):
                U([(wcols(w_down, cg * 512 + q * 256, 256, r0=fc * 1024, nk=8), 8, 256) for q in range(2)],
                  lambda ws, fc=fc, cg=cg: emit_down(ws, fc, cg))
        U([], halt_if("X2", Bf, 8 * 2048, F32))

        def phase_H(ws):
            P.add("sp", DMA(GFIN[:, :], gfin_d.ap()), writes=["gfin", ("wbf", 0), ("wbf", 1)], dma=True)
            for tt in range(8):
                col = nxt("col", 64)
                xs = nxt("xn", 2)
                xk = [("x1c", tt, cg) for cg in range(4)]
                P.add("act", ACTF(XN[:, xs, :], Bf[:, tt, :], AF.Square, accum_out=SS[:, col:col + 1]),
                      reads=xk, writes=[("xn", xs), ("ss", col)])
                P.add("act", ACTF(RS[:, col:col + 1], SS[:, col:col + 1], AF.Ln, bias=EPS, scale=1.0 / D),
                      reads=[("ss", col)], writes=[("rs", col)])
                P.add("act", ACTF(RS[:, col:col + 1], RS[:, col:col + 1], AF.Exp, scale=-0.5),
                      reads=[("rs", col)], writes=[("rs", col)])
                s = nxt("stg", 2)
                P.add("dve", STT(STG[:, s, :], Bf[:, tt, :], RS[:, col:col + 1], GFIN[:, :], ALU.mult, ALU.mult),
                      reads=xk + [("rs", col), "gfin"], writes=[("stg", s)])
                P.add("sp", DMA(out_d.ap()[tt * 128:(tt + 1) * 128, :], STG[:, s, :]),
                      reads=[("stg", s)], writes=[("out", tt)], dma=True)

        U([], phase_H)

    try:
        run_units()
    except _Stop:
        pass

    tail_reads = [k for k in P.lastw if isinstance(k, tuple) and k[0] == "out"] + (["dbg"] if "dbg" in P.lastw else [])
    P.add("pool", MS(JUNK[:, 15:16], 0.0), reads=tail_reads, writes=["fin"])

    from contextlib import ExitStack
    with ExitStack() as st:
        sems = {e: st.enter_context(nc.semaphore("c_" + e)) for e in ENGS}
        dsems = {"sp": [st.enter_context(nc.semaphore("d_sp%d" % i)) for i in range(Prog.NDSEM)]}
        block = st.enter_context(nc.Block())
        P.emit(block, sems, dsems)
    print("ops:", len(P.ops), {e: sum(1 for o in P.ops if o["eng"] == e) for e in ENGS}, flush=True)
    return nc, (dbg[0] is not None)


_CACHE = {}


def kernel(x, norm_mix, w_qkv, w_gate, b_gate, rpb, w_proj_a, w_proj_b, w_out,
           norm_mlp, w_up, w_down, norm_final):
    f = lambda a: np.ascontiguousarray(np.asarray(a, dtype=np.float32))
    x = f(x)
    plans = make_plans()
    slot_maps, contents = build_tables(plans, f(rpb)[0])
    ntab = sum(contents[0][pi][j].shape[1] for pi in range(len(plans)) for j in range(4))
    nc, has_dbg = build(plans, slot_maps, ntab)

    def colvec(g, n):
        return np.ascontiguousarray(f(g).reshape(n, 128).T)

    shared = {
        "w_qkv": f(w_qkv)[0], "w_gate": f(w_gate)[0], "w_pa": f(w_proj_a)[0], "w_pb": f(w_proj_b)[0],
        "w_out": f(w_out)[0], "w_up": f(w_up)[0], "w_down": f(w_down)[0],
        "gmix": colvec(norm_mix[0], 16), "gmlp": colvec(norm_mlp[0], 16),
        "gfin": np.ascontiguousarray(np.broadcast_to(f(norm_final)[None, :], (128, D))),
        "bgate": colvec(b_gate[0], 32),
        "ident": np.eye(128, dtype=np.float32).astype(ml_dtypes.bfloat16),
    }
    in_maps = []
    for core in range(NCORES):
        b, c = core // 4, core % 4
        m = dict(shared)
        for nm in ("3a", "3b", "near"):
            tok = set_tokens(c, nm)
            ok = (tok >= 0) & (tok < S)
            xs = np.zeros((1536, D), np.float32)
            xs[ok] = x[b, tok[ok]]
            m["x_" + nm] = xs
        m["tabs"] = np.ascontiguousarray(np.concatenate(
            [contents[c][pi][j] for pi in range(len(plans)) for j in range(4)], axis=1))
        in_maps.append(m)
    if os.environ.get("KTRACE"):
        res = run_bass_kernel_spmd(nc, in_maps, core_ids=list(range(NCORES)), trace=True)
        print("exec_time_ns", res.exec_time_ns, flush=True)
    else:
        res = run_bass_kernel_spmd(nc, in_maps, core_ids=list(range(NCORES)))
    if has_dbg:
        kernel.dbg = [np.asarray(r["dbg"]) for r in res.results]
    out = np.zeros((NB, S, D), np.float32)
    for core in range(NCORES):
        b, c = core // 4, core % 4
        out[b, set_tokens(c, "own")] = np.asarray(res.results[core]["out"])
    return out
```

```python
import os
import numpy as np
import ml_dtypes
import concourse.bass as bass
import concourse.mybir as mybir
from concourse.bass_utils import run_bass_kernel_spmd

F32 = mybir.dt.float32
BF16 = mybir.dt.bfloat16
AF = mybir.ActivationFunctionType
ALU = mybir.AluOpType

D = 2048
S = 4096
NB = 2
HD = 128
DFF = 8192
EPS = 1e-6
NEG = -30000.0
SCALE = HD ** -0.5
NCORES = int(os.environ.get('KCORES', '8'))
DILS = (1, 4, 16)

STOP = os.environ.get("KSTOP", "")
DUMP = os.environ.get("KDUMP", "")


def set_tokens(c, name):
    base = 1024 * c - 1024
    if name in ("3a", "3b"):
        r0 = 0 if name == "3a" else 8
        r = np.arange(8)[:, None] + r0
        i = np.arange(192)[None, :]
        return (base + 16 * i + r).reshape(-1)
    if name == "near":
        return base + np.arange(768, 2304)
    if name == "own":
        return base + np.arange(1024, 2048)
    raise KeyError(name)


def pat_idx(start, dims):
    idx = np.array([start])
    for step, cnt in dims:
        idx = (idx[:, None] + step * np.arange(cnt)[None, :]).reshape(-1)
    return idx


def alibi_slopes():
    return 2.0 ** (-8.0 * np.arange(1, 13) / 12.0)


def bias_block(kind, hidx, ktok, qtok, rpb):
    k = ktok[:, None].astype(np.int64)
    q = qtok[None, :].astype(np.int64)
    kin = (k >= 0) & (k < S)
    if kind < 3:
        d = DILS[kind]
        slope = alibi_slopes()[kind * 4 + hidx]
        same = (np.mod(k, d) == np.mod(q, d))
        ik = np.floor_divide(k, d)
        iq = np.floor_divide(q, d)
        rel = np.abs(ik - iq)
        valid = kin & same & (rel <= 64)
        b = -(np.float32(slope) * np.float32(d)) * rel.astype(np.float32)
        return np.where(valid, b, np.float32(NEG)).astype(np.float32)
    rk, ck = np.floor_divide(k, 64), np.mod(k, 64)
    rq, cq = np.floor_divide(q, 64), np.mod(q, 64)
    rs = np.clip(rq - 4, 0, 56)
    cs = np.clip(cq - 8, 0, 48)
    valid = kin & (rk >= rs) & (rk < rs + 8) & (ck >= cs) & (ck < cs + 16)
    dr = np.clip(rk - rq + 7, 0, 14)
    dc = np.clip(ck - cq, -15, 15) + 15
    g = rpb[hidx][dr, dc]
    return np.where(valid, g, np.float32(NEG)).astype(np.float32)


class AttnPlan:
    def __init__(self, name, tokset, kind, head0):
        self.name, self.tokset, self.kind, self.head0 = name, tokset, kind, head0
        self.vtiles = []
        self.qgroups = []
        self.qblocks = []

    def vt(self, start, dims):
        key = (start, tuple(map(tuple, dims)))
        for n, v in enumerate(self.vtiles):
            if v == key:
                return n
        self.vtiles.append(key)
        return len(self.vtiles) - 1


def make_plans():
    plans = []
    for hf, nm in enumerate(("3a", "3b")):
        p = AttnPlan(nm, nm, 2, 8)
        for a in range(4):
            tl = [p.vt(128 * (3 * a + t), [(1, 128)]) for t in range(3)]
            p.qgroups.append((8 * hf + 2 * a, [(1, 2), (16, 64)], tl, (a * 128, [(1, 128)])))
        p.qblocks = [(64, [(192, 8), (1, 64)], 0)]
        plans.append(p)
    near_q = [(256, [(1, 512)], 0), (768, [(1, 512)], 512)]
    p = AttnPlan("g2", "near", 1, 4)
    for rp in range(4):
        kts = [p.vt(4 * k0 - 768 + rp, [(4, 128)]) for k0 in (192, 320, 448)]
        for t in range(2):
            own = (512 * t + rp, [(4, 128)])
            p.qgroups.append((own[0], own[1], [kts[t], kts[t + 1]], own))
    p.qblocks = near_q
    plans.append(p)
    p = AttnPlan("g1", "near", 0, 0)
    kts = [p.vt(192 + 128 * m, [(1, 128)]) for m in range(9)]
    for j in range(8):
        own = (128 * j, [(1, 128)])
        p.qgroups.append((own[0], own[1], [kts[j], kts[j + 1]], own))
    p.qblocks = near_q
    plans.append(p)
    p = AttnPlan("nb", "near", 3, 12)
    for g in range(8):
        R = 16 + 2 * g
        tiles = list(range(R - 4, R + 5, 2))
        if g == 0:
            tiles.append(22)
        if g == 7:
            tiles.insert(0, 24)
        tiles = sorted(set(tiles))
        kts = [p.vt(64 * K0 - 768, [(1, 128)]) for K0 in tiles]
        own = (128 * g, [(1, 128)])
        p.qgroups.append((own[0], own[1], kts, own))
    p.qblocks = near_q
    plans.append(p)
    return plans


def build_tables(plans, rpb_all):
    slot_maps, contents = [], [[] for _ in range(4)]
    for p in plans:
        toks = [set_tokens(c, p.tokset) for c in range(4)]
        owns = [set_tokens(c, "own") for c in range(4)]
        pm, pc = [], [[] for _ in range(4)]
        for j in range(4):
            seen, smap, tabs = {}, {}, [[] for _ in range(4)]
            for g, (os_, od, tl, _) in enumerate(p.qgroups):
                qi = pat_idx(os_, od)
                for n, vt in enumerate(tl):
                    ki = pat_idx(*p.vtiles[vt])
                    blks = [bias_block(p.kind, j, toks[c][ki], owns[c][qi],
                                       rpb_all) for c in range(4)]
                    key = b"".join(b.tobytes() for b in blks)
                    if key not in seen:
                        seen[key] = len(tabs[0])
                        for c in range(4):
                            tabs[c].append(blks[c])
                    smap[(g, n)] = seen[key]
            pm.append(smap)
            for c in range(4):
                pc[c].append(np.stack(tabs[c], axis=1))
        slot_maps.append(pm)
        for c in range(4):
            contents[c].append(pc[c])
    return slot_maps, contents


ENGS = ("sp", "act", "dve", "pool", "pe")


class Prog:
    NDSEM = 8

    def __init__(self, nc):
        self.nc = nc
        self.ops = []
        self.lastw = {}
        self.readers = {}

    def add(self, eng, fn, reads=(), writes=(), dma=False):
        idx = len(self.ops)
        deps = set()
        for r in reads:
            w = self.lastw.get(r)
            if w is not None:
                deps.add(w)
        for r in writes:
            w = self.lastw.get(r)
            if w is not None:
                deps.add(w)
            deps.update(self.readers.get(r, ()))
        for r in reads:
            self.readers.setdefault(r, []).append(idx)
        for r in writes:
            self.lastw[r] = idx
            self.readers[r] = []
        deps.discard(idx)
        if eng == "pe":
            deps = {d for d in deps if self.ops[d]["eng"] != "pe"}
        self.ops.append(dict(eng=eng, fn=fn, deps=deps, dma=dma, idx=idx))
        return idx

    def emit(self, block, sems, dsems):
        ops = self.ops
        needed = set()
        for o in ops:
            needed.update(o["deps"])
        cnt = {e: 0 for e in ENGS}
        dcnt = {e: 0 for e in ENGS}
        for o in ops:
            e = o["eng"]
            if o["dma"]:
                j = dcnt[e]
                dcnt[e] += 1
                o["sem"] = dsems[e][j % self.NDSEM]
                o["val"] = 16 * (j // self.NDSEM + 1)
                o["dj"] = j
            elif o["idx"] in needed or o.get("force"):
                cnt[e] += 1
                o["sem"] = sems[e]
                o["val"] = cnt[e]
            else:
                o["sem"] = None
        per = {e: [o for o in ops if o["eng"] == e] for e in ENGS}

        def run(engname, eng):
            waited = {}

            def wait(sem, val):
                if waited.get(sem.num if hasattr(sem, "num") else id(sem), 0) < val:
                    eng.wait_ge(sem, val)
                    waited[sem.num if hasattr(sem, "num") else id(sem)] = val

            for o in per[engname]:
                for d in sorted(o["deps"]):
                    od = ops[d]
                    wait(od["sem"], od["val"])
                if o["dma"] and o["val"] > 16:
                    wait(o["sem"], o["val"] - 16)
                ins = o["fn"](eng)
                if o["sem"] is not None:
                    ins.then_inc(o["sem"], 16 if o["dma"] else 1)

        block.sync(lambda e: run("sp", e))
        block.scalar(lambda e: run("act", e))
        block.vector(lambda e: run("dve", e))
        block.gpsimd(lambda e: run("pool", e))
        block.tensor(lambda e: run("pe", e))


class _Stop(Exception):
    pass


def MM(out, lhsT, rhs, start, stop):
    return lambda e: e.matmul(out, lhsT=lhsT, rhs=rhs, start=start, stop=stop)


def ACTF(out, in_, func, **kw):
    return lambda e: e.activation(out=out, in_=in_, func=func, **kw)


def TT(out, in0, in1, op):
    return lambda e: e.tensor_tensor(out=out, in0=in0, in1=in1, op=op)


def TS(out, in0, s1, s2, op0, op1=None):
    if op1 is None:
        return lambda e: e.tensor_scalar(out=out, in0=in0, scalar1=s1, scalar2=s2, op0=op0)
    return lambda e: e.tensor_scalar(out=out, in0=in0, scalar1=s1, scalar2=s2, op0=op0, op1=op1)


def STT(out, in0, scalar, in1, op0, op1):
    return lambda e: e.scalar_tensor_tensor(out=out, in0=in0, scalar=scalar, in1=in1, op0=op0, op1=op1)


def CP(out, in_):
    return lambda e: e.tensor_copy(out=out, in_=in_)


def DMA(out, in_):
    return lambda e: e.dma_start(out=out, in_=in_)


def TR(out, in_, ident):
    return lambda e: e.transpose(out=out, in_=in_, identity=ident)


def RCP(out, in_):
    return lambda e: e.reciprocal(out=out, in_=in_)


def MS(a, val):
    return lambda e: e.memset(a, val)


def build(plans, slot_maps, ntab):
    nc = bass.Bass("TRN2", target_bir_lowering=False)
    P = Prog(nc)

    def din(name, shape, dt=F32):
        return nc.dram_tensor(name, list(shape), dt, kind="ExternalInput")

    x_sets = {n: din("x_" + n, [1536, D]) for n in ("3a", "3b", "near")}
    w_qkv = din("w_qkv", [D, 3 * D])
    w_gate = din("w_gate", [D, 2 * D])
    w_pa = din("w_pa", [512, D])
    w_pb = din("w_pb", [512, D])
    w_out = din("w_out", [D, D])
    w_up = din("w_up", [D, DFF])
    w_down = din("w_down", [DFF, D])
    gmix_d = din("gmix", [128, 16])
    gmlp_d = din("gmlp", [128, 16])
    gfin_d = din("gfin", [128, D])
    bgate_d = din("bgate", [128, 32])
    ident_d = din("ident", [128, 128], BF16)
    tabs_d = din("tabs", [128, ntab, 128])
    out_d = nc.dram_tensor("out", [1024, D], F32, kind="ExternalOutput")
    dbg = [None]

    pstride = {}

    def sb(name, shape, dt, off):
        t = nc.alloc_sbuf_tensor_at(name, list(shape), dt, offset=16512 + off)
        pstride[t.name] = int(np.prod(shape[1:]))
        return t

    KB = 1024
    A = sb("A", [128, 16, 1536], BF16, 0)
    Bf = sb("Bf", [128, 8, 2048], F32, 48 * KB)
    NUM = sb("NUM", [128, 4, 1024], F32, 48 * KB)
    DEN = sb("DEN", [128, 4, 1024], F32, 64 * KB)
    KT = sb("KT", [128, 4, 1536], BF16, 80 * KB)
    V = sb("V", [128, 12, 512], BF16, 92 * KB)
    QT = sb("QT", [128, 4, 1024], BF16, 104 * KB)
    SCR = sb("SCR", [128, 8, 512], F32, 80 * KB)
    C = sb("C", [128, 16, 1024], BF16, 112 * KB)
    TAB = sb("TAB", [128, 44, 128], F32, 112 * KB)
    TMPS = sb("TMPS", [128, 2, 512], F32, 134 * KB)
    PT = sb("PT", [128, 4, 512], BF16, 138 * KB)
    RD = sb("RD", [128, 512], F32, 142 * KB)
    YA = sb("YA", [128, 4, 1024], BF16, 144 * KB)
    YB = sb("YB", [128, 4, 1024], BF16, 152 * KB)
    YAB = sb("YAB", [128, 8192], BF16, 144 * KB)
    SQJ = sb("SQJ", [128, 2048], BF16, 144 * KB)
    RL = sb("RL", [128, 2, 512], F32, 144 * KB)
    STG = sb("STG", [128, 2, 2048], F32, 160 * KB)
    WBF = sb("WBF", [128, 4, 2048], BF16, 176 * KB)
    GFIN = sb("GFIN", [128, 2048], F32, 152 * KB)
    XN = sb("XN", [128, 2, 2048], BF16, 192 * KB)
    cb = 200 * KB
    IDENT = sb("IDENT", [128, 128], BF16, cb)
    ONES = sb("ONES", [128, 128], BF16, cb + 256)
    GMIX = sb("GMIX", [128, 16], F32, cb + 512)
    GMLP = sb("GMLP", [128, 16], F32, cb + 576)
    BGATE = sb("BGATE", [128, 32], F32, cb + 640)
    SS = sb("SS", [128, 64], F32, cb + 768)
    RS = sb("RS", [128, 64], F32, cb + 1024)
    JUNK = sb("JUNK", [128, 16], F32, cb + 1280)

    PS = [nc.alloc_psum_tensor("ps%d" % i, [128, 512], F32) for i in range(8)]
    for i in range(8):
        pstride[PS[i].name] = 512

    def ap(t, off, dims, p0=0, np_=128):
        ps_ = pstride[t.name]
        return bass.AP(t, p0 * ps_ + off, [[ps_, np_]] + [list(d) for d in dims])

    for dst, srcd, key in ((IDENT, ident_d, "ident"), (GMIX, gmix_d, "gmix"), (GMLP, gmlp_d, "gmlp"),
                           (BGATE, bgate_d, "bgate")):
        P.add("sp", DMA(dst[:, :], srcd.ap()), writes=[key], dma=True)
    P.add("pool", MS(ONES[:, :], 1.0), writes=["ones"])

    ctr = dict(stg=0, wbf=0, xn=0, ps=0, pss=0, pso=0, col=0, tmps=0, pt=0, cast=0, ev=0, rl=0, junk=0, tb=0, pb=0)

    def nxt(k, n):
        v = ctr[k] % n
        ctr[k] += 1
        return v

    def fence(old_keys, new_keys=()):
        jc = nxt("junk", 16)
        P.add("pool", MS(JUNK[:, jc:jc + 1], 0.0), writes=list(old_keys) + list(new_keys) + [("junk", jc)])

    def dump(tag, t, n, dt=F32):
        if DUMP != tag or dbg[0] is not None:
            return
        dbg[0] = nc.dram_tensor("dbg", [128, n], dt, kind="ExternalOutput")
        keys = list(P.lastw.keys())
        P.add("sp", DMA(dbg[0].ap(), ap(t, 0, [(1, n)])), reads=keys, writes=["dbg"], dma=True)
        if STOP == "dump":
            raise _Stop()

    def norm_steps(xin, xkeys, dst_t, dst_off, dst_kstride, gam, gam_key, extra_reads, tbanks, pre=None):
        st = {}

        def part1a():
            if pre is not None:
                pre()
            col = nxt("col", 64)
            st["col"] = col
            P.add("act", ACTF(SQJ[:, :], xin, AF.Square, accum_out=SS[:, col:col + 1]),
                  reads=xkeys, writes=[("ss", col)])
            P.add("act", ACTF(RS[:, col:col + 1], SS[:, col:col + 1], AF.Ln, bias=EPS, scale=1.0 / D),
                  reads=[("ss", col)], writes=[("rs", col)])
            P.add("act", ACTF(RS[:, col:col + 1], RS[:, col:col + 1], AF.Exp, scale=-0.5),
                  reads=[("rs", col)], writes=[("rs", col)])

        def part1b():
            col = st["col"]
            xs = nxt("xn", 2)
            st["xs"] = xs
            P.add("dve", TS(XN[:, xs, :], xin, RS[:, col:col + 1], None, ALU.mult),
                  reads=xkeys + [("rs", col)], writes=[("xn", xs)])

        def part1():
            part1a()
            part1b()

        def part2():
            xs = st["xs"]
            for hb in range(2):
                b = tbanks[nxt("tb", len(tbanks))]
                pb = PS[b][:, :].bitcast(BF16)
                for kk in range(8):
                    k = hb * 8 + kk
                    P.add("pe", TR(pb[:, kk * 128:(kk + 1) * 128], XN[:, xs, k * 128:(k + 1) * 128], IDENT[:, :]),
                          reads=[("xn", xs), "ident"], writes=[("ps", b)])
                o = ap(dst_t, hb * 8 * dst_kstride + dst_off, [(dst_kstride, 8), (1, 128)])
                g = ap(gam, hb * 8, [(1, 8), (0, 128)])
                i0 = pb.rearrange("p (k t) -> p k t", k=8)
                P.add("dve", TT(o, i0, g, ALU.mult),
                      reads=[("ps", b), gam_key] + extra_reads, writes=[(dst_t.name, dst_off // 128, hb)])

        return part1, part2, part1a, part1b

    def skewed(parts, skew=1):
        steps = []
        n = len(parts)
        for i in range(n + skew):
            if i < n:
                steps.append(parts[i][0])
            if i - skew >= 0:
                steps.append(parts[i - skew][1])
        return steps

    def run_merged(sa, sb_):
        ia = ib = 0
        while ia < len(sa) or ib < len(sb_):
            if ib >= len(sb_) or (ia < len(sa) and ia * len(sb_) <= ib * len(sa)):
                sa[ia]()
                ia += 1
            else:
                sb_[ib]()
                ib += 1

    def load_w(src_ap, nk, ncol, ws):
        s = nxt("stg", 2)
        n = nk * ncol
        P.add("sp", DMA(ap(STG, s * 2048, [(ncol, nk), (1, ncol)]), src_ap), writes=[("stg", s)], dma=True)
        ce = ("dve", "act", "dve")[nxt("cast", 3)]
        o = ap(WBF, ws * 2048, [(1, n)])
        i = ap(STG, s * 2048, [(1, n)])
        if ce == "act":
            P.add("act", ACTF(o, i, AF.Copy), reads=[("stg", s)], writes=[("wbf", ws)])
        else:
            P.add("dve", CP(o, i), reads=[("stg", s)], writes=[("wbf", ws)])
        return ws

    def wcols(w, c0, ncol, r0=0, nk=16):
        a = w.ap()[r0:r0 + nk * 128, c0:c0 + ncol]
        return a.rearrange("(k p) n -> p k n", p=128)

    def evac_copy(dst, src, rk, wk):
        if nxt("ev", 2) == 0:
            P.add("act", ACTF(dst, src, AF.Copy), reads=rk, writes=wk)
        else:
            P.add("dve", CP(dst, src), reads=rk, writes=wk)

    units = []

    def U(loads, fn):
        units.append((loads, fn))

    def run_units():
        wpos = 0
        pending = []
        slots_of = {}
        nl = 0
        for i, (loads, fn) in enumerate(units):
            while nl < len(units):
                lo = units[nl][0]
                if not lo:
                    slots_of[nl] = []
                    nl += 1
                    continue
                n = len(lo)
                start = wpos + (1 if (n == 2 and wpos % 2 == 1) else 0)
                oldest = pending[0][1] if pending else start
                if start + n - oldest > 4 or nl > i + 6:
                    break
                wpos = start + n
                sl = [(start + q) % 4 for q in range(n)]
                for (sa, nk, ncol), ws in zip(lo, sl):
                    load_w(sa, nk, ncol, ws)
                slots_of[nl] = sl
                pending.append((nl, start))
                nl += 1
            assert i in slots_of, (i, nl)
            fn(slots_of[i])
            pending = [p for p in pending if p[0] != i]

    tab_base = {}
    _t = 0
    for _pi in range(len(plans)):
        for _j in range(4):
            tab_base[(_pi, _j)] = _t
            _t += max(slot_maps[_pi][_j].values()) + 1
    hk = [("A", m, hb) for m in range(12) for hb in range(2)]

    def hT_steps(setname, tbanks):
        xd = x_sets[setname]
        parts = []
        for m in range(12):
            box = {}

            def p1(m=m, box=box):
                s = nxt("stg", 2)
                P.add("sp", DMA(STG[:, s, :], xd.ap()[m * 128:(m + 1) * 128, :]), writes=[("stg", s)], dma=True)
                q1, q2, _, _ = norm_steps(STG[:, s, :], [("stg", s)], A, m * 128, 1536, GMIX, "gmix", [], tbanks)
                box["p2"] = q2
                q1()

            def p2(box=box):
                box["p2"]()

            parts.append((p1, p2))
        return skewed(parts, 1)

    def build_hT(setname):
        for s_ in hT_steps(setname, list(range(8))):
            s_()

    PBANKS = [6, 7]

    def k_steps(ws, kidx):
        def step(tb):
            b = PBANKS[nxt("pb", 2)]
            for k in range(16):
                P.add("pe", MM(PS[b][:, :], ap(WBF, ws * 2048 + k * 128, [(1, 128)]),
                               ap(A, k * 1536 + tb * 512, [(1, 512)]), k == 0, k == 15),
                      reads=[("wbf", ws)] + hk, writes=[("ps", b)])
            evac_copy(ap(KT, kidx * 1536 + tb * 512, [(1, 512)]), PS[b][:, :], [("ps", b)], [("KT", kidx, tb)])
        return [lambda tb=tb: step(tb) for tb in range(3)]

    def q_steps(ws, qidx, qblocks):
        def step(st, dims, qoff):
            b = PBANKS[nxt("pb", 2)]
            for k in range(16):
                P.add("pe", MM(PS[b][:, :], ap(WBF, ws * 2048 + k * 128, [(1, 128)]),
                               ap(A, k * 1536 + st, dims), k == 0, k == 15),
                      reads=[("wbf", ws)] + hk, writes=[("ps", b)])
            evac_copy(ap(QT, qidx * 1024 + qoff, [(1, 512)]), PS[b][:, :], [("ps", b)], [("QT", qidx, qoff)])
        return [lambda st=st, dims=dims, qoff=qoff: step(st, dims, qoff) for (st, dims, qoff) in qblocks]

    def v_steps(wss, vset, vtiles):
        assert wss[1] == wss[0] + 1 and wss[0] % 2 == 0, wss

        def step(n, st, dims):
            b = PBANKS[nxt("pb", 2)]
            for k in range(16):
                P.add("pe", MM(PS[b][:, 0:256], ap(A, k * 1536 + st, list(dims)),
                               ap(WBF, wss[0] * 2048 + k * 128, [(2048, 2), (1, 128)]), k == 0, k == 15),
                      reads=[("wbf", wss[0]), ("wbf", wss[1])] + hk, writes=[("ps", b)])
            evac_copy(V[:, n, vset * 256:(vset + 1) * 256], PS[b][:, 0:256], [("ps", b)], [("V", n, vset)])
        return [lambda n=n, st=st, dims=dims: step(n, st, dims) for n, (st, dims) in enumerate(vtiles)]

    from collections import deque
    att_q = deque()
    own_left = [0]

    def mix(own):
        for s_ in own:
            s_()
            own_left[0] -= 1
            if att_q:
                r = -(-len(att_q) // max(own_left[0] + 1, 1))
                for _ in range(min(r, len(att_q))):
                    att_q.popleft()()

    def drain():
        while att_q:
            att_q.popleft()()

    def project_half(plan, half, pset):
        nq = len(plan.qblocks)
        total = 2 * 3 + 2 * nq + len(plan.vtiles)

        def first(ws, jj, fn):
            pass

        for jj in range(2):
            head = plan.head0 + 2 * half + jj
            U([(wcols(w_qkv, 2048 + head * 128, 128), 16, 128)],
              lambda ws, jj=jj, first_=(jj == 0): (own_left.__setitem__(0, total) if first_ else None,
                                                   mix(k_steps(ws[0], pset * 2 + jj))))
        for jj in range(2):
            head = plan.head0 + 2 * half + jj
            U([(wcols(w_qkv, head * 128, 128), 16, 128)],
              lambda ws, jj=jj, qb=plan.qblocks: mix(q_steps(ws[0], pset * 2 + jj, qb)))
        U([(wcols(w_qkv, 4096 + (plan.head0 + 2 * half + q) * 128, 128), 16, 128) for q in range(2)],
          lambda ws, vt=list(plan.vtiles): (mix(v_steps(ws, pset, vt)), drain()))

    SBANKS = [0, 1]
    OBANKS = [(2, 3), (4, 5)]

    def attend_steps(pi, plan, mode, half, pset):
        smaps = slot_maps[pi]
        parts = []
        obox = {"ob": None}
        for jj in range(2):
            j = 2 * half + jj
            kidx = pset * 2 + jj
            vcol = pset * 256 + jj * 128
            smap = smaps[j]
            nslot = max(smap.values()) + 1
            assert nslot <= 44, nslot
            t0 = tab_base[(pi, j)]
            blocks = []
            for g, (os_, od, tl, qpat) in enumerate(plan.qgroups):
                for n, vt in enumerate(tl):
                    blocks.append((g, n, vt, len(tl)))
            kt_keys = [("KT", kidx, tb) for tb in range(3)]
            for c0 in range(0, len(blocks), 4):
                chunk = blocks[c0:c0 + 4]
                box = {}

                def pA(j=j, kidx=kidx, smap=smap, nslot=nslot, t0=t0, chunk=chunk, first=(c0 == 0), box=box, kt_keys=kt_keys):
                    if first:
                        P.add("sp", DMA(TAB[:, 0:nslot, :], tabs_d.ap()[:, t0:t0 + nslot, :]), writes=["tab"], dma=True)
                    sbk = SBANKS[nxt("pss", len(SBANKS))]
                    for ci, (g, n, vt, ntl) in enumerate(chunk):
                        vst, vdims = plan.vtiles[vt]
                        qst, qd = plan.qgroups[g][3]
                        P.add("pe", MM(PS[sbk][:, ci * 128:(ci + 1) * 128], ap(KT, kidx * 1536 + vst, list(vdims)),
                                       ap(QT, kidx * 1024 + qst, list(qd)), True, True),
                              reads=kt_keys + [("QT", kidx, 0), ("QT", kidx, 512)], writes=[("ps", sbk)])
                    ts = nxt("tmps", 2)
                    slots = [smap[(g, n)] for (g, n, vt, ntl) in chunk]
                    i = 0
                    while i < len(slots):
                        jn = i + 1
                        stp = 0
                        if jn < len(slots):
                            stp = slots[jn] - slots[i]
                            jn += 1
                            while jn < len(slots) and slots[jn] - slots[jn - 1] == stp:
                                jn += 1
                        cnt_ = jn - i
                        P.add("dve", STT(ap(TMPS, ts * 512 + i * 128, [(128, cnt_), (1, 128)]),
                                         ap(PS[sbk], i * 128, [(128, cnt_), (1, 128)]), SCALE,
                                         ap(TAB, slots[i] * 128, [(stp * 128, cnt_), (1, 128)]), ALU.mult, ALU.add),
                              reads=[("ps", sbk), "tab"], writes=[("tmps", ts, i)])
                        i = jn
                    pt = nxt("pt", 4)
                    ncol = len(chunk) * 128
                    P.add("act", ACTF(PT[:, pt, 0:ncol], TMPS[:, ts, 0:ncol], AF.Exp),
                          reads=[("tmps", ts, i2) for i2 in range(4)], writes=[("pt", pt)])
                    box["pt"] = pt

                def pB(j=j, vcol=vcol, pset=pset, chunk=chunk, box=box):
                    pt = box["pt"]
                    for ci, (g, n, vt, ntl) in enumerate(chunk):
                        gi = g % 4
                        if gi == 0 and n == 0:
                            obox["ob"] = OBANKS[nxt("pso", len(OBANKS))]
                        ob = obox["ob"]
                        P.add("pe", MM(PS[ob[0]][:, gi * 128:(gi + 1) * 128], V[:, vt, vcol:vcol + 128],
                                       PT[:, pt, ci * 128:(ci + 1) * 128], n == 0, n == ntl - 1),
                              reads=[("V", vt, pset), ("pt", pt)], writes=[("ps", ob[0])])
                        P.add("pe", MM(PS[ob[1]][:, gi * 128:(gi + 1) * 128], ONES[:, :],
                                       PT[:, pt, ci * 128:(ci + 1) * 128], n == 0, n == ntl - 1),
                              reads=["ones", ("pt", pt)], writes=[("ps", ob[1])])
                        if n != ntl - 1:
                            continue
                        os_, od = plan.qgroups[g][0], plan.qgroups[g][1]
                        if len(od) == 2:
                            c1, c0_ = od[1][1], od[0][1]
                            pdims = [(c1, c0_), (1, c1)]
                        else:
                            pdims = [(1, 128)]
                        pn = ap(PS[ob[0]], gi * 128, pdims)
                        pd = ap(PS[ob[1]], gi * 128, pdims)
                        if mode == "set":
                            P.add("dve", CP(ap(NUM, j * 1024 + os_, od), pn), reads=[("ps", ob[0])], writes=[("NUM", j, g)])
                            P.add("act", ACTF(ap(DEN, j * 1024 + os_, od), pd, AF.Copy),
                                  reads=[("ps", ob[1])], writes=[("DEN", j, g)])
                        elif mode == "add":
                            dn = ap(NUM, j * 1024 + os_, od)
                            dd = ap(DEN, j * 1024 + os_, od)
                            P.add("dve", TT(dn, pn, dn, ALU.add), reads=[("ps", ob[0])], writes=["NUMall"])
                            P.add("dve", TT(dd, pd, dd, ALU.add), reads=[("ps", ob[1])], writes=["DENall"])
                        else:
                            rd = ap(RD, 0, pdims)
                            P.add("dve", RCP(rd, pd), reads=[("ps", ob[1])], writes=["rd"])
                            P.add("dve", TT(ap(YB, j * 1024 + os_, od), pn, rd, ALU.mult),
                                  reads=[("ps", ob[0]), "rd"], writes=[("YB", j)])

                parts.append((pA, pB))
        return skewed(parts, 1)

    numden_set = [("NUM", j, g) for j in range(4) for g in range(4)] + [("DEN", j, g) for j in range(4) for g in range(4)]
    vkeys = [("V", n, h) for n in range(12) for h in range(2)]

    class _Halt(Exception):
        pass

    def halt_if(tag, t, n, dt):
        def f(ws):
            dump(tag, t, n, dt)
        return f

    cur = None
    hp = 0
    for pi, plan in enumerate(plans):
        mode = {"3a": "set", "3b": "set", "g2": "add", "g1": "add", "nb": "nb"}[plan.name]
        if plan.tokset != cur:
            def hb(ws, nm=plan.tokset):
                sa = list(att_q)
                att_q.clear()
                run_merged(sa, hT_steps(nm, [6, 7]))
            U([], hb)
            cur = plan.tokset
        for half in range(2):
            pset = hp % 2
            hp += 1
            project_half(plan, half, pset)

            def queue_att(ws, pi=pi, plan=plan, mode=mode, half=half, pset=pset):
                st = attend_steps(pi, plan, mode, half, pset)
                if plan.name == "g2" and half == 0:
                    st = [lambda: fence(numden_set, ["NUMall", "DENall"])] + st
                att_q.extend(st)
            U([], queue_att)
    U([], lambda ws: drain())
    U([], halt_if("NUM", NUM, 4096, F32))

    def fin_ya(ws):
        for j in range(4):
            for h in range(2):
                sl = slice(h * 512, (h + 1) * 512)
                P.add("dve", RCP(DEN[:, j, sl], DEN[:, j, sl]), reads=["DENall"], writes=[("rden", j, h)])
                P.add("dve", TT(YA[:, j, sl], NUM[:, j, sl], DEN[:, j, sl], ALU.mult),
                      reads=["NUMall", ("rden", j, h)], writes=[("YA", j)])
        dump("YAB", YAB, 8192, BF16)
        old = ([("KT", j, tb) for j in range(4) for tb in range(3)] + vkeys +
               [("QT", j, q) for j in range(4) for q in (0, 512)] + ["tab", "rd"] +
               [("tmps", t, i) for t in range(2) for i in range(4)] + [("pt", t) for t in range(4)])
        fence(old, ["arD"])

    full = STOP not in [p.name for p in plans]
    if full:
        U([], fin_ya)
        own_blk = [(256, [(1, 512)]), (768, [(1, 512)])]

        def emit_gate(ws, m, br):
            for blk in range(2):
                st, dims = own_blk[blk]
                b = nxt("ps", 8)
                sc = (m * 2 + blk) % 2
                for k in range(16):
                    P.add("pe", MM(PS[b][:, :], ap(WBF, ws * 2048 + k * 128, [(1, 128)]),
                                   ap(A, k * 1536 + st, dims), k == 0, k == 15),
                          reads=[("wbf", ws)] + hk, writes=[("ps", b)])
                P.add("act", ACTF(SCR[:, sc * 4 + br, :], PS[b][:, :], AF.Sigmoid,
                                  bias=BGATE[:, br * 16 + m:br * 16 + m + 1]),
                      reads=[("ps", b), "bgate", "arD"], writes=[("scr", sc, br)])

        def emit_proj(ws, m):
            for blk in range(2):
                sc = (m * 2 + blk) % 2
                for br, (Y, yk) in enumerate(((YA, "YA"), (YB, "YB"))):
                    b = nxt("ps", 8)
                    for k in range(4):
                        P.add("pe", MM(PS[b][:, :], ap(WBF, ws[br] * 2048 + k * 128, [(1, 128)]),
                                       Y[:, k, blk * 512:(blk + 1) * 512], k == 0, k == 3),
                              reads=[("wbf", ws[br])] + [(yk, jj) for jj in range(4)], writes=[("ps", b)])
                    P.add("dve", TT(SCR[:, sc * 4 + 2 + br, :], PS[b][:, :], SCR[:, sc * 4 + br, :], ALU.mult),
                          reads=[("ps", b), ("scr", sc, br), "arD"], writes=[("scr", sc, 2 + br)])
                P.add("pool", TT(C[:, m, blk * 512:(blk + 1) * 512], SCR[:, sc * 4 + 2, :], SCR[:, sc * 4 + 3, :], ALU.add),
                      reads=[("scr", sc, 2), ("scr", sc, 3), "arD"], writes=[("C", m, blk)])

        for m in range(16):
            U([(wcols(w_gate, m * 128, 128), 16, 128)], lambda ws, m=m: emit_gate(ws[0], m, 0))
            U([(wcols(w_gate, 2048 + m * 128, 128), 16, 128)], lambda ws, m=m: emit_gate(ws[0], m, 1))
            U([(wcols(w_pa, m * 128, 128, nk=4), 4, 128), (wcols(w_pb, m * 128, 128, nk=4), 4, 128)],
              lambda ws, m=m: emit_proj(ws, m))
        U([], halt_if("MG", C, 16 * 1024, BF16))

        ckeys = [("C", m, blk) for m in range(16) for blk in range(2)]

        def start_E(ws):
            fence(["NUMall", "DENall"] + numden_set + [("rden", j, h) for j in range(4) for h in range(2)] +
                  [("scr", s_, i) for s_ in range(2) for i in range(4)], ["arE"])
            for tt in range(8):
                P.add("sp", DMA(Bf[:, tt, :], x_sets["near"].ap()[256 + tt * 128:256 + (tt + 1) * 128, :]),
                      reads=["arE"], writes=[("x1", tt, 0)], dma=True)

        def emit_out(wss, cg2):
            assert wss[1] == wss[0] + 1 and wss[0] % 2 == 0, wss
            last = (cg2 == 7)
            if last:
                fence(hk, ["arF"])
                parts = [norm_steps(Bf[:, tt, :], [("x1c", tt, cg) for cg in range(4)], A, tt * 128, 1536,
                                    GMLP, "gmlp", ["arF"], list(range(8))) for tt in range(8)]
            for tt in range(8):
                b = nxt("ps", 8)
                for k in range(16):
                    P.add("pe", MM(PS[b][:, 0:256], C[:, k, tt * 128:(tt + 1) * 128],
                                   ap(WBF, wss[0] * 2048 + k * 128, [(2048, 2), (1, 128)]), k == 0, k == 15),
                          reads=[("wbf", wss[0]), ("wbf", wss[1])] + ckeys, writes=[("ps", b)])
                xsl = Bf[:, tt, cg2 * 256:(cg2 + 1) * 256]
                P.add("dve", TT(xsl, PS[b][:, 0:256], xsl, ALU.add),
                      reads=[("ps", b), ("x1", tt, 0)], writes=[("x1c", tt, cg2 // 2)])
                if last:
                    parts[tt][2]()
                    if tt >= 1:
                        parts[tt - 1][3]()
                    if tt >= 2:
                        parts[tt - 2][1]()
            if last:
                parts[7][3]()
                parts[6][1]()
                parts[7][1]()
                fence(ckeys + [("YA", j) for j in range(4)], ["arG"])
                P.add("sp", DMA(GFIN[:, :], gfin_d.ap()), reads=["arG"], writes=["gfin"] + [("YB", j) for j in range(4)], dma=True)

        U([], start_E)
        for cg2 in range(8):
            U([(wcols(w_out, cg2 * 256 + q * 128, 128), 16, 128) for q in range(2)],
              lambda ws, cg2=cg2: emit_out(ws, cg2))
        U([], halt_if("X1", Bf, 8 * 2048, F32))

        h2k = [("A", m, hb) for m in range(8) for hb in range(2)]

        U([], halt_if("H2", A, 16 * 1536, BF16))

        def emit_up(ws, fc, f):
            us = fc % 2
            for blk in range(2):
                b = nxt("ps", 8)
                for k in range(16):
                    P.add("pe", MM(PS[b][:, :], ap(WBF, ws * 2048 + k * 128, [(1, 128)]),
                                   A[:, k, blk * 512:(blk + 1) * 512], k == 0, k == 15),
                          reads=[("wbf", ws)] + h2k, writes=[("ps", b)])
                sc = nxt("rl", 2)
                P.add("act", ACTF(RL[:, sc, :], PS[b][:, :], AF.Relu), reads=[("ps", b), "arG"], writes=[("rl", sc)])
                P.add("pool", TT(C[:, us * 8 + f, blk * 512:(blk + 1) * 512], RL[:, sc, :], RL[:, sc, :], ALU.mult),
                      reads=[("rl", sc), "arG"], writes=[("uT", us, f, blk)])

        fcol = {}

        def final_a(tt):
            col = nxt("col", 64)
            fcol[tt] = col
            xk = [("x1c", tt, cg) for cg in range(4)]
            P.add("act", ACTF(SQJ[:, :], Bf[:, tt, :], AF.Square, accum_out=SS[:, col:col + 1]),
                  reads=xk, writes=[("ss", col)])
            P.add("act", ACTF(RS[:, col:col + 1], SS[:, col:col + 1], AF.Ln, bias=EPS, scale=1.0 / D),
                  reads=[("ss", col)], writes=[("rs", col)])
            P.add("act", ACTF(RS[:, col:col + 1], RS[:, col:col + 1], AF.Exp, scale=-0.5),
                  reads=[("rs", col)], writes=[("rs", col)])

        def final_b(tt):
            col = fcol[tt]
            xk = [("x1c", tt, cg) for cg in range(4)]
            s = nxt("stg", 2)
            P.add("dve", STT(STG[:, s, :], Bf[:, tt, :], RS[:, col:col + 1], GFIN[:, :], ALU.mult, ALU.mult),
                  reads=xk + [("rs", col), "gfin"], writes=[("stg", s)])
            P.add("sp", DMA(out_d.ap()[tt * 128:(tt + 1) * 128, :], STG[:, s, :]),
                  reads=[("stg", s)], writes=[("out", tt)], dma=True)

        def emit_down(wss, fc, cg):
            us = fc % 2
            assert wss[1] == wss[0] + 1 and wss[0] % 2 == 0, wss
            ukeys = [("uT", us, f, blk) for f in range(8) for blk in range(2)]
            for tt in range(8):
                b = nxt("ps", 8)
                for f in range(8):
                    P.add("pe", MM(PS[b][:, :], C[:, us * 8 + f, tt * 128:(tt + 1) * 128],
                                   ap(WBF, wss[0] * 2048 + f * 256, [(2048, 2), (1, 256)]), f == 0, f == 7),
                          reads=[("wbf", wss[0]), ("wbf", wss[1])] + ukeys, writes=[("ps", b)])
                xsl = Bf[:, tt, cg * 512:(cg + 1) * 512]
                P.add("dve", TT(xsl, PS[b][:, :], xsl, ALU.add), reads=[("ps", b)], writes=[("x1c", tt, cg)])
                if fc == 7 and cg == 3:
                    final_a(tt)
                    if tt >= 1:
                        final_b(tt - 1)
            if fc == 7 and cg == 3:
                final_b(7)

        for fc in range(8):
            for f in range(8):
                U([(wcols(w_up, fc * 1024 + f * 128, 128), 16, 128)], lambda ws, fc=fc, f=f: emit_up(ws[0], fc, f))
            for cg in range(4):
                U([(wcols(w_down, cg * 512 + q * 256, 256, r0=fc * 1024, nk=8), 8, 256) for q in range(2)],
                  lambda ws, fc=fc, cg=cg: emit_down(ws, fc, cg))
        U([], halt_if("X2", Bf, 8 * 2048, F32))

    try:
        run_units()
    except _Stop:
        pass

    tail_reads = [k for k in P.lastw if isinstance(k, tuple) and k[0] == "out"] + (["dbg"] if "dbg" in P.lastw else [])
    P.add("pool", MS(JUNK[:, 15:16], 0.0), reads=tail_reads, writes=["fin"])

    from contextlib import ExitStack
    with ExitStack() as st:
        sems = {e: st.enter_context(nc.semaphore("c_" + e)) for e in ENGS}
        dsems = {"sp": [st.enter_context(nc.semaphore("d_sp%d" % i)) for i in range(Prog.NDSEM)]}
        block = st.enter_context(nc.Block())
        P.emit(block, sems, dsems)
    print("ops:", len(P.ops), {e: sum(1 for o in P.ops if o["eng"] == e) for e in ENGS}, flush=True)
    return nc, (dbg[0] is not None)


_CACHE = {}


def kernel(x, norm_mix, w_qkv, w_gate, b_gate, rpb, w_proj_a, w_proj_b, w_out,
           norm_mlp, w_up, w_down, norm_final):
    f = lambda a: np.ascontiguousarray(np.asarray(a, dtype=np.float32))
    x = f(x)
    plans = make_plans()
    slot_maps, contents = build_tables(plans, f(rpb)[0])
    ntab = sum(contents[0][pi][j].shape[1] for pi in range(len(plans)) for j in range(4))
    nc, has_dbg = build(plans, slot_maps, ntab)

    def colvec(g, n):
        return np.ascontiguousarray(f(g).reshape(n, 128).T)

    shared = {
        "w_qkv": f(w_qkv)[0], "w_gate": f(w_gate)[0], "w_pa": f(w_proj_a)[0], "w_pb": f(w_proj_b)[0],
        "w_out": f(w_out)[0], "w_up": f(w_up)[0], "w_down": f(w_down)[0],
        "gmix": colvec(norm_mix[0], 16), "gmlp": colvec(norm_mlp[0], 16),
        "gfin": np.ascontiguousarray(np.broadcast_to(f(norm_final)[None, :], (128, D))),
        "bgate": colvec(b_gate[0], 32),
        "ident": np.eye(128, dtype=np.float32).astype(ml_dtypes.bfloat16),
    }
    in_maps = []
    for core in range(NCORES):
        b, c = core // 4, core % 4
        m = dict(shared)
        for nm in ("3a", "3b", "near"):
            tok = set_tokens(c, nm)
            ok = (tok >= 0) & (tok < S)
            xs = np.zeros((1536, D), np.float32)
            xs[ok] = x[b, tok[ok]]
            m["x_" + nm] = xs
        m["tabs"] = np.ascontiguousarray(np.concatenate(
            [contents[c][pi][j] for pi in range(len(plans)) for j in range(4)], axis=1))
        in_maps.append(m)
    if os.environ.get("KTRACE"):
        res = run_bass_kernel_spmd(nc, in_maps, core_ids=list(range(NCORES)), trace=True)
        print("exec_time_ns", res.exec_time_ns, flush=True)
    else:
        res = run_bass_kernel_spmd(nc, in_maps, core_ids=list(range(NCORES)))
    if has_dbg:
        kernel.dbg = [np.asarray(r["dbg"]) for r in res.results]
    out = np.zeros((NB, S, D), np.float32)
    for core in range(NCORES):
        b, c = core // 4, core % 4
        out[b, set_tokens(c, "own")] = np.asarray(res.results[core]["out"])
    return out
```

```python
import os
import numpy as np
import ml_dtypes
import concourse.bass as bass
import concourse.mybir as mybir
from concourse.bass_utils import run_bass_kernel_spmd

F32 = mybir.dt.float32
BF16 = mybir.dt.bfloat16
AF = mybir.ActivationFunctionType
ALU = mybir.AluOpType

D = 2048
S = 4096
NB = 2
HD = 128
DFF = 8192
EPS = 1e-6
NEG = -30000.0
SCALE = HD ** -0.5
NCORES = int(os.environ.get('KCORES', '8'))
DILS = (1, 4, 16)

STOP = os.environ.get("KSTOP", "")
DUMP = os.environ.get("KDUMP", "")


def set_tokens(c, name):
    base = 1024 * c - 1024
    if name in ("3a", "3b"):
        r0 = 0 if name == "3a" else 8
        r = np.arange(8)[:, None] + r0
        i = np.arange(192)[None, :]
        return (base + 16 * i + r).reshape(-1)
    if name == "near":
        return base + np.arange(768, 2304)
    if name == "own":
        return base + np.arange(1024, 2048)
    raise KeyError(name)


def pat_idx(start, dims):
    idx = np.array([start])
    for step, cnt in dims:
        idx = (idx[:, None] + step * np.arange(cnt)[None, :]).reshape(-1)
    return idx


def alibi_slopes():
    return 2.0 ** (-8.0 * np.arange(1, 13) / 12.0)


def bias_block(kind, hidx, ktok, qtok, rpb):
    k = ktok[:, None].astype(np.int64)
    q = qtok[None, :].astype(np.int64)
    kin = (k >= 0) & (k < S)
    if kind < 3:
        d = DILS[kind]
        slope = alibi_slopes()[kind * 4 + hidx]
        same = (np.mod(k, d) == np.mod(q, d))
        ik = np.floor_divide(k, d)
        iq = np.floor_divide(q, d)
        rel = np.abs(ik - iq)
        valid = kin & same & (rel <= 64)
        b = -(np.float32(slope) * np.float32(d)) * rel.astype(np.float32)
        return np.where(valid, b, np.float32(NEG)).astype(np.float32)
    rk, ck = np.floor_divide(k, 64), np.mod(k, 64)
    rq, cq = np.floor_divide(q, 64), np.mod(q, 64)
    rs = np.clip(rq - 4, 0, 56)
    cs = np.clip(cq - 8, 0, 48)
    valid = kin & (rk >= rs) & (rk < rs + 8) & (ck >= cs) & (ck < cs + 16)
    dr = np.clip(rk - rq + 7, 0, 14)
    dc = np.clip(ck - cq, -15, 15) + 15
    g = rpb[hidx][dr, dc]
    return np.where(valid, g, np.float32(NEG)).astype(np.float32)


class AttnPlan:
    def __init__(self, name, tokset, kind, head0):
        self.name, self.tokset, self.kind, self.head0 = name, tokset, kind, head0
        self.vtiles = []
        self.qgroups = []
        self.qblocks = []

    def vt(self, start, dims):
        key = (start, tuple(map(tuple, dims)))
        for n, v in enumerate(self.vtiles):
            if v == key:
                return n
        self.vtiles.append(key)
        return len(self.vtiles) - 1


def make_plans():
    plans = []
    for hf, nm in enumerate(("3a", "3b")):
        p = AttnPlan(nm, nm, 2, 8)
        for a in range(4):
            tl = [p.vt(128 * (3 * a + t), [(1, 128)]) for t in range(3)]
            p.qgroups.append((8 * hf + 2 * a, [(1, 2), (16, 64)], tl, (a * 128, [(1, 128)])))
        p.qblocks = [(64, [(192, 8), (1, 64)], 0)]
        plans.append(p)
    near_q = [(256, [(1, 512)], 0), (768, [(1, 512)], 512)]
    p = AttnPlan("g2", "near", 1, 4)
    for rp in range(4):
        kts = [p.vt(4 * k0 - 768 + rp, [(4, 128)]) for k0 in (192, 320, 448)]
        for t in range(2):
            own = (512 * t + rp, [(4, 128)])
            p.qgroups.append((own[0], own[1], [kts[t], kts[t + 1]], own))
    p.qblocks = near_q
    plans.append(p)
    p = AttnPlan("g1", "near", 0, 0)
    kts = [p.vt(192 + 128 * m, [(1, 128)]) for m in range(9)]
    for j in range(8):
        own = (128 * j, [(1, 128)])
        p.qgroups.append((own[0], own[1], [kts[j], kts[j + 1]], own))
    p.qblocks = near_q
    plans.append(p)
    p = AttnPlan("nb", "near", 3, 12)
    for g in range(8):
        R = 16 + 2 * g
        tiles = list(range(R - 4, R + 5, 2))
        if g == 0:
            tiles.append(22)
        if g == 7:
            tiles.insert(0, 24)
        tiles = sorted(set(tiles))
        kts = [p.vt(64 * K0 - 768, [(1, 128)]) for K0 in tiles]
        own = (128 * g, [(1, 128)])
        p.qgroups.append((own[0], own[1], kts, own))
    p.qblocks = near_q
    plans.append(p)
    plans = [plans[0], plans[1], plans[4], plans[2], plans[3]]
    return plans


def build_tables(plans, rpb_all):
    slot_maps, contents = [], [[] for _ in range(4)]
    for p in plans:
        toks = [set_tokens(c, p.tokset) for c in range(4)]
        owns = [set_tokens(c, "own") for c in range(4)]
        pm, pc = [], [[] for _ in range(4)]
        for j in range(4):
            seen, smap, tabs = {}, {}, [[] for _ in range(4)]
            for g, (os_, od, tl, _) in enumerate(p.qgroups):
                qi = pat_idx(os_, od)
                for n, vt in enumerate(tl):
                    ki = pat_idx(*p.vtiles[vt])
                    blks = [bias_block(p.kind, j, toks[c][ki], owns[c][qi],
                                       rpb_all) for c in range(4)]
                    key = b"".join(b.tobytes() for b in blks)
                    if key not in seen:
                        seen[key] = len(tabs[0])
                        for c in range(4):
                            tabs[c].append(blks[c])
                    smap[(g, n)] = seen[key]
            pm.append(smap)
            for c in range(4):
                pc[c].append(np.stack(tabs[c], axis=1))
        slot_maps.append(pm)
        for c in range(4):
            contents[c].append(pc[c])
    return slot_maps, contents


ENGS = ("sp", "act", "dve", "pool", "pe")


class Prog:
    NDSEM = 8

    def __init__(self, nc):
        self.nc = nc
        self.ops = []
        self.lastw = {}
        self.readers = {}

    def add(self, eng, fn, reads=(), writes=(), dma=False):
        idx = len(self.ops)
        deps = set()
        for r in reads:
            w = self.lastw.get(r)
            if w is not None:
                deps.add(w)
        for r in writes:
            w = self.lastw.get(r)
            if w is not None:
                deps.add(w)
            deps.update(self.readers.get(r, ()))
        for r in reads:
            self.readers.setdefault(r, []).append(idx)
        for r in writes:
            self.lastw[r] = idx
            self.readers[r] = []
        deps.discard(idx)
        if eng == "pe":
            deps = {d for d in deps if self.ops[d]["eng"] != "pe"}
        self.ops.append(dict(eng=eng, fn=fn, deps=deps, dma=dma, idx=idx))
        return idx

    def emit(self, block, sems, dsems):
        ops = self.ops
        needed = set()
        for o in ops:
            needed.update(o["deps"])
        cnt = {e: 0 for e in ENGS}
        dcnt = {e: 0 for e in ENGS}
        for o in ops:
            e = o["eng"]
            if o["dma"]:
                j = dcnt[e]
                dcnt[e] += 1
                o["sem"] = dsems[e][j % self.NDSEM]
                o["val"] = 16 * (j // self.NDSEM + 1)
                o["dj"] = j
            elif o["idx"] in needed or o.get("force"):
                cnt[e] += 1
                o["sem"] = sems[e]
                o["val"] = cnt[e]
            else:
                o["sem"] = None
        per = {e: [o for o in ops if o["eng"] == e] for e in ENGS}

        def run(engname, eng):
            waited = {}

            def wait(sem, val):
                if waited.get(sem.num if hasattr(sem, "num") else id(sem), 0) < val:
                    eng.wait_ge(sem, val)
                    waited[sem.num if hasattr(sem, "num") else id(sem)] = val

            for o in per[engname]:
                for d in sorted(o["deps"]):
                    od = ops[d]
                    wait(od["sem"], od["val"])
                if o["dma"] and o["val"] > 16:
                    wait(o["sem"], o["val"] - 16)
                ins = o["fn"](eng)
                if o["sem"] is not None:
                    ins.then_inc(o["sem"], 16 if o["dma"] else 1)

        block.sync(lambda e: run("sp", e))
        block.scalar(lambda e: run("act", e))
        block.vector(lambda e: run("dve", e))
        block.gpsimd(lambda e: run("pool", e))
        block.tensor(lambda e: run("pe", e))


class _Stop(Exception):
    pass


def MM(out, lhsT, rhs, start, stop):
    return lambda e: e.matmul(out, lhsT=lhsT, rhs=rhs, start=start, stop=stop)


def ACTF(out, in_, func, **kw):
    return lambda e: e.activation(out=out, in_=in_, func=func, **kw)


def TT(out, in0, in1, op):
    return lambda e: e.tensor_tensor(out=out, in0=in0, in1=in1, op=op)


def TS(out, in0, s1, s2, op0, op1=None):
    if op1 is None:
        return lambda e: e.tensor_scalar(out=out, in0=in0, scalar1=s1, scalar2=s2, op0=op0)
    return lambda e: e.tensor_scalar(out=out, in0=in0, scalar1=s1, scalar2=s2, op0=op0, op1=op1)


def STT(out, in0, scalar, in1, op0, op1):
    return lambda e: e.scalar_tensor_tensor(out=out, in0=in0, scalar=scalar, in1=in1, op0=op0, op1=op1)


def CP(out, in_):
    return lambda e: e.tensor_copy(out=out, in_=in_)


def DMA(out, in_):
    return lambda e: e.dma_start(out=out, in_=in_)


def TR(out, in_, ident):
    return lambda e: e.transpose(out=out, in_=in_, identity=ident)


def RCP(out, in_):
    return lambda e: e.reciprocal(out=out, in_=in_)


def MS(a, val):
    return lambda e: e.memset(a, val)


def build(plans, slot_maps, ntab):
    nc = bass.Bass("TRN2", target_bir_lowering=False)
    P = Prog(nc)

    def din(name, shape, dt=F32):
        return nc.dram_tensor(name, list(shape), dt, kind="ExternalInput")

    x_sets = {n: din("x_" + n, [1536, D]) for n in ("3a", "3b", "near")}
    w_qkv = din("w_qkv", [D, 3 * D])
    w_gate = din("w_gate", [D, 2 * D])
    w_pa = din("w_pa", [512, D])
    w_pb = din("w_pb", [512, D])
    w_out = din("w_out", [D, D])
    w_up = din("w_up", [D, DFF])
    w_down = din("w_down", [DFF, D])
    gmix_d = din("gmix", [128, 16])
    gmlp_d = din("gmlp", [128, 16])
    gfin_d = din("gfin", [128, D])
    bgate_d = din("bgate", [128, 32])
    ident_d = din("ident", [128, 128], BF16)
    tabs_d = din("tabs", [128, ntab, 128])
    out_d = nc.dram_tensor("out", [1024, D], F32, kind="ExternalOutput")
    dbg = [None]

    pstride = {}

    def sb(name, shape, dt, off):
        t = nc.alloc_sbuf_tensor_at(name, list(shape), dt, offset=16512 + off)
        pstride[t.name] = int(np.prod(shape[1:]))
        return t

    KB = 1024
    A = sb("A", [128, 16, 1536], BF16, 0)
    Bf = sb("Bf", [128, 8, 2048], F32, 48 * KB)
    NUM = sb("NUM", [128, 4, 1024], F32, 48 * KB)
    DEN = sb("DEN", [128, 4, 1024], F32, 64 * KB)
    KT = sb("KT", [128, 4, 1536], BF16, 80 * KB)
    V = sb("V", [128, 12, 512], BF16, 92 * KB)
    QT = sb("QT", [128, 4, 1024], BF16, 104 * KB)
    SCR = sb("SCR", [128, 8, 512], F32, 80 * KB)
    C = sb("C", [128, 16, 1024], BF16, 112 * KB)
    TAB = sb("TAB", [128, 44, 128], F32, 112 * KB)
    TMPS = sb("TMPS", [128, 2, 512], F32, 134 * KB)
    PT = sb("PT", [128, 4, 512], BF16, 138 * KB)
    RD = sb("RD", [128, 512], F32, 142 * KB)
    YA = sb("YA", [128, 4, 1024], BF16, 144 * KB)
    YB = sb("YB", [128, 4, 1024], BF16, 152 * KB)
    YAB = sb("YAB", [128, 8192], BF16, 144 * KB)
    SQJ = sb("SQJ", [128, 2048], BF16, 144 * KB)
    RL = sb("RL", [128, 2, 512], F32, 144 * KB)
    STG = sb("STG", [128, 2, 2048], F32, 160 * KB)
    WBF = sb("WBF", [128, 4, 2048], BF16, 176 * KB)
    GFIN = sb("GFIN", [128, 2048], F32, 152 * KB)
    XN = sb("XN", [128, 2, 2048], BF16, 192 * KB)
    cb = 200 * KB
    IDENT = sb("IDENT", [128, 128], BF16, cb)
    ONES = sb("ONES", [128, 128], BF16, cb + 256)
    GMIX = sb("GMIX", [128, 16], F32, cb + 512)
    GMLP = sb("GMLP", [128, 16], F32, cb + 576)
    BGATE = sb("BGATE", [128, 32], F32, cb + 640)
    SS = sb("SS", [128, 64], F32, cb + 768)
    RS = sb("RS", [128, 64], F32, cb + 1024)
    JUNK = sb("JUNK", [128, 16], F32, cb + 1280)

    PS = [nc.alloc_psum_tensor("ps%d" % i, [128, 512], F32) for i in range(8)]
    for i in range(8):
        pstride[PS[i].name] = 512

    def ap(t, off, dims, p0=0, np_=128):
        ps_ = pstride[t.name]
        return bass.AP(t, p0 * ps_ + off, [[ps_, np_]] + [list(d) for d in dims])

    for dst, srcd, key in ((IDENT, ident_d, "ident"), (GMIX, gmix_d, "gmix"), (GMLP, gmlp_d, "gmlp"),
                           (BGATE, bgate_d, "bgate")):
        P.add("sp", DMA(dst[:, :], srcd.ap()), writes=[key], dma=True)
    P.add("pool", MS(ONES[:, :], 1.0), writes=["ones"])

    ctr = dict(stg=0, wbf=0, xn=0, ps=0, pss=0, pso=0, col=0, tmps=0, pt=0, cast=0, ev=0, rl=0, junk=0, tb=0, pb=0)

    def nxt(k, n):
        v = ctr[k] % n
        ctr[k] += 1
        return v

    def fence(old_keys, new_keys=()):
        jc = nxt("junk", 16)
        P.add("pool", MS(JUNK[:, jc:jc + 1], 0.0), writes=list(old_keys) + list(new_keys) + [("junk", jc)])

    def dump(tag, t, n, dt=F32):
        if DUMP != tag or dbg[0] is not None:
            return
        dbg[0] = nc.dram_tensor("dbg", [128, n], dt, kind="ExternalOutput")
        keys = list(P.lastw.keys())
        P.add("sp", DMA(dbg[0].ap(), ap(t, 0, [(1, n)])), reads=keys, writes=["dbg"], dma=True)
        if STOP == "dump":
            raise _Stop()

    def norm_steps(xin, xkeys, dst_t, dst_off, dst_kstride, gam, gam_key, extra_reads, tbanks, pre=None):
        st = {}

        def part1a():
            if pre is not None:
                pre()
            col = nxt("col", 64)
            st["col"] = col
            P.add("act", ACTF(SQJ[:, :], xin, AF.Square, accum_out=SS[:, col:col + 1]),
                  reads=xkeys, writes=[("ss", col)])
            P.add("act", ACTF(RS[:, col:col + 1], SS[:, col:col + 1], AF.Ln, bias=EPS, scale=1.0 / D),
                  reads=[("ss", col)], writes=[("rs", col)])
            P.add("act", ACTF(RS[:, col:col + 1], RS[:, col:col + 1], AF.Exp, scale=-0.5),
                  reads=[("rs", col)], writes=[("rs", col)])

        def part1b():
            col = st["col"]
            xs = nxt("xn", 2)
            st["xs"] = xs
            P.add("dve", TS(XN[:, xs, :], xin, RS[:, col:col + 1], None, ALU.mult),
                  reads=xkeys + [("rs", col)], writes=[("xn", xs)])

        def part1():
            part1a()
            part1b()

        def part2():
            xs = st["xs"]
            for hb in range(2):
                b = tbanks[nxt("tb", len(tbanks))]
                pb = PS[b][:, :].bitcast(BF16)
                for kk in range(8):
                    k = hb * 8 + kk
                    P.add("pe", TR(pb[:, kk * 128:(kk + 1) * 128], XN[:, xs, k * 128:(k + 1) * 128], IDENT[:, :]),
                          reads=[("xn", xs), "ident"], writes=[("ps", b)])
                o = ap(dst_t, hb * 8 * dst_kstride + dst_off, [(dst_kstride, 8), (1, 128)])
                g = ap(gam, hb * 8, [(1, 8), (0, 128)])
                i0 = pb.rearrange("p (k t) -> p k t", k=8)
                P.add("dve", TT(o, i0, g, ALU.mult),
                      reads=[("ps", b), gam_key] + extra_reads, writes=[("A", dst_off // 128, hb)])

        return part1, part2, part1a, part1b

    def skewed(parts, skew=1):
        steps = []
        n = len(parts)
        for i in range(n + skew):
            if i < n:
                steps.append(parts[i][0])
            if i - skew >= 0:
                steps.append(parts[i - skew][1])
        return steps

    def run_merged(sa, sb_):
        ia = ib = 0
        while ia < len(sa) or ib < len(sb_):
            if ib >= len(sb_) or (ia < len(sa) and ia * len(sb_) <= ib * len(sa)):
                sa[ia]()
                ia += 1
            else:
                sb_[ib]()
                ib += 1

    def load_w(src_ap, nk, ncol, ws):
        s = nxt("stg", 2)
        n = nk * ncol
        P.add("sp", DMA(ap(STG, s * 2048, [(ncol, nk), (1, ncol)]), src_ap), writes=[("stg", s)], dma=True)
        ce = ("dve", "act", "dve")[nxt("cast", 3)]
        o = ap(WBF, ws * 2048, [(1, n)])
        i = ap(STG, s * 2048, [(1, n)])
        if ce == "act":
            P.add("act", ACTF(o, i, AF.Copy), reads=[("stg", s)], writes=[("wbf", ws)])
        else:
            P.add("dve", CP(o, i), reads=[("stg", s)], writes=[("wbf", ws)])
        return ws

    def wcols(w, c0, ncol, r0=0, nk=16):
        a = w.ap()[r0:r0 + nk * 128, c0:c0 + ncol]
        return a.rearrange("(k p) n -> p k n", p=128)

    def evac_copy(dst, src, rk, wk):
        if nxt("ev", 2) == 0:
            P.add("act", ACTF(dst, src, AF.Copy), reads=rk, writes=wk)
        else:
            P.add("dve", CP(dst, src), reads=rk, writes=wk)

    units = []

    def U(loads, fn):
        units.append((loads, fn))

    def run_units():
        wpos = 0
        pending = []
        slots_of = {}
        nl = 0
        for i, (loads, fn) in enumerate(units):
            while nl < len(units):
                lo = units[nl][0]
                if not lo:
                    slots_of[nl] = []
                    nl += 1
                    continue
                n = len(lo)
                start = wpos + (1 if (n == 2 and wpos % 2 == 1) else 0)
                oldest = pending[0][1] if pending else start
                if start + n - oldest > 4 or nl > i + 6:
                    break
                wpos = start + n
                sl = [(start + q) % 4 for q in range(n)]
                for (sa, nk, ncol), ws in zip(lo, sl):
                    load_w(sa, nk, ncol, ws)
                slots_of[nl] = sl
                pending.append((nl, start))
                nl += 1
            assert i in slots_of, (i, nl)
            fn(slots_of[i])
            pending = [p for p in pending if p[0] != i]

    tab_base = {}
    _t = 0
    for _pi in range(len(plans)):
        for _j in range(4):
            tab_base[(_pi, _j)] = _t
            _t += max(slot_maps[_pi][_j].values()) + 1
    hk = [("A", m, hb) for m in range(12) for hb in range(2)]

    def hT_steps(setname, tbanks):
        xd = x_sets[setname]
        parts = []
        for m in range(12):
            box = {}

            def p1(m=m, box=box):
                s = nxt("stg", 2)
                P.add("sp", DMA(STG[:, s, :], xd.ap()[m * 128:(m + 1) * 128, :]), writes=[("stg", s)], dma=True)
                q1, q2, _, _ = norm_steps(STG[:, s, :], [("stg", s)], A, m * 128, 1536, GMIX, "gmix", [], tbanks)
                box["p2"] = q2
                q1()

            def p2(box=box):
                box["p2"]()

            parts.append((p1, p2))
        return skewed(parts, 1)

    def build_hT(setname):
        for s_ in hT_steps(setname, list(range(8))):
            s_()

    PBANKS = [6, 7]

    def k_steps(ws, kidx):
        def step(tb):
            b = PBANKS[nxt("pb", 2)]
            for k in range(16):
                P.add("pe", MM(PS[b][:, :], ap(WBF, ws * 2048 + k * 128, [(1, 128)]),
                               ap(A, k * 1536 + tb * 512, [(1, 512)]), k == 0, k == 15),
                      reads=[("wbf", ws)] + hk, writes=[("ps", b)])
            evac_copy(ap(KT, kidx * 1536 + tb * 512, [(1, 512)]), PS[b][:, :], [("ps", b)], [("KT", kidx, tb)])
        return [lambda tb=tb: step(tb) for tb in range(3)]

    def q_steps(ws, qidx, qblocks):
        def step(st, dims, qoff):
            b = PBANKS[nxt("pb", 2)]
            for k in range(16):
                P.add("pe", MM(PS[b][:, :], ap(WBF, ws * 2048 + k * 128, [(1, 128)]),
                               ap(A, k * 1536 + st, dims), k == 0, k == 15),
                      reads=[("wbf", ws)] + hk, writes=[("ps", b)])
            evac_copy(ap(QT, qidx * 1024 + qoff, [(1, 512)]), PS[b][:, :], [("ps", b)], [("QT", qidx, qoff)])
        return [lambda st=st, dims=dims, qoff=qoff: step(st, dims, qoff) for (st, dims, qoff) in qblocks]

    def v_steps(wss, vset, vtiles):
        assert wss[1] == wss[0] + 1 and wss[0] % 2 == 0, wss

        def step(n, st, dims):
            b = PBANKS[nxt("pb", 2)]
            for k in range(16):
                P.add("pe", MM(PS[b][:, 0:256], ap(A, k * 1536 + st, list(dims)),
                               ap(WBF, wss[0] * 2048 + k * 128, [(2048, 2), (1, 128)]), k == 0, k == 15),
                      reads=[("wbf", wss[0]), ("wbf", wss[1])] + hk, writes=[("ps", b)])
            evac_copy(V[:, n, vset * 256:(vset + 1) * 256], PS[b][:, 0:256], [("ps", b)], [("V", n, vset)])
        return [lambda n=n, st=st, dims=dims: step(n, st, dims) for n, (st, dims) in enumerate(vtiles)]

    from collections import deque
    att_q = deque()
    own_left = [0]

    def mix(own):
        for s_ in own:
            s_()
            own_left[0] -= 1
            if att_q:
                r = -(-len(att_q) // max(own_left[0] + 1, 1))
                for _ in range(min(r, len(att_q))):
                    att_q.popleft()()

    def drain():
        while att_q:
            att_q.popleft()()

    def project_half(plan, half, pset):
        nq = len(plan.qblocks)
        total = 2 * 3 + 2 * nq + len(plan.vtiles)

        def first(ws, jj, fn):
            pass

        for jj in range(2):
            head = plan.head0 + 2 * half + jj
            U([(wcols(w_qkv, 2048 + head * 128, 128), 16, 128)],
              lambda ws, jj=jj, first_=(jj == 0): (own_left.__setitem__(0, total) if first_ else None,
                                                   mix(k_steps(ws[0], pset * 2 + jj))))
        for jj in range(2):
            head = plan.head0 + 2 * half + jj
            U([(wcols(w_qkv, head * 128, 128), 16, 128)],
              lambda ws, jj=jj, qb=plan.qblocks: mix(q_steps(ws[0], pset * 2 + jj, qb)))
        U([(wcols(w_qkv, 4096 + (plan.head0 + 2 * half + q) * 128, 128), 16, 128) for q in range(2)],
          lambda ws, vt=list(plan.vtiles): (mix(v_steps(ws, pset, vt)), drain()))

    SBANKS = [0, 1]
    OBANKS = [(2, 3), (4, 5)]

    def attend_steps(pi, plan, mode, half, pset):
        smaps = slot_maps[pi]
        parts = []
        obox = {"ob": None}
        for jj in range(2):
            j = 2 * half + jj
            kidx = pset * 2 + jj
            vcol = pset * 256 + jj * 128
            smap = smaps[j]
            nslot = max(smap.values()) + 1
            assert nslot <= 44, nslot
            t0 = tab_base[(pi, j)]
            blocks = []
            for g, (os_, od, tl, qpat) in enumerate(plan.qgroups):
                for n, vt in enumerate(tl):
                    blocks.append((g, n, vt, len(tl)))
            kt_keys = [("KT", kidx, tb) for tb in range(3)]
            for c0 in range(0, len(blocks), 4):
                chunk = blocks[c0:c0 + 4]
                box = {}

                def pA(j=j, kidx=kidx, smap=smap, nslot=nslot, t0=t0, chunk=chunk, first=(c0 == 0), box=box, kt_keys=kt_keys):
                    if first:
                        P.add("sp", DMA(TAB[:, 0:nslot, :], tabs_d.ap()[:, t0:t0 + nslot, :]), writes=["tab"], dma=True)
                    sbk = SBANKS[nxt("pss", len(SBANKS))]
                    for ci, (g, n, vt, ntl) in enumerate(chunk):
                        vst, vdims = plan.vtiles[vt]
                        qst, qd = plan.qgroups[g][3]
                        P.add("pe", MM(PS[sbk][:, ci * 128:(ci + 1) * 128], ap(KT, kidx * 1536 + vst, list(vdims)),
                                       ap(QT, kidx * 1024 + qst, list(qd)), True, True),
                              reads=kt_keys + [("QT", kidx, 0), ("QT", kidx, 512)], writes=[("ps", sbk)])
                    ts = nxt("tmps", 2)
                    slots = [smap[(g, n)] for (g, n, vt, ntl) in chunk]
                    i = 0
                    while i < len(slots):
                        jn = i + 1
                        stp = 0
                        if jn < len(slots):
                            stp = slots[jn] - slots[i]
                            jn += 1
                            while jn < len(slots) and slots[jn] - slots[jn - 1] == stp:
                                jn += 1
                        cnt_ = jn - i
                        P.add("dve", STT(ap(TMPS, ts * 512 + i * 128, [(128, cnt_), (1, 128)]),
                                         ap(PS[sbk], i * 128, [(128, cnt_), (1, 128)]), SCALE,
                                         ap(TAB, slots[i] * 128, [(stp * 128, cnt_), (1, 128)]), ALU.mult, ALU.add),
                              reads=[("ps", sbk), "tab"], writes=[("tmps", ts, i)])
                        i = jn
                    pt = nxt("pt", 4)
                    ncol = len(chunk) * 128
                    P.add("act", ACTF(PT[:, pt, 0:ncol], TMPS[:, ts, 0:ncol], AF.Exp),
                          reads=[("tmps", ts, i2) for i2 in range(4)], writes=[("pt", pt)])
                    box["pt"] = pt

                def pB(j=j, vcol=vcol, pset=pset, chunk=chunk, box=box):
                    pt = box["pt"]
                    for ci, (g, n, vt, ntl) in enumerate(chunk):
                        gi = g % 4
                        if gi == 0 and n == 0:
                            obox["ob"] = OBANKS[nxt("pso", len(OBANKS))]
                        ob = obox["ob"]
                        P.add("pe", MM(PS[ob[0]][:, gi * 128:(gi + 1) * 128], V[:, vt, vcol:vcol + 128],
                                       PT[:, pt, ci * 128:(ci + 1) * 128], n == 0, n == ntl - 1),
                              reads=[("V", vt, pset), ("pt", pt)], writes=[("ps", ob[0])])
                        P.add("pe", MM(PS[ob[1]][:, gi * 128:(gi + 1) * 128], ONES[:, :],
                                       PT[:, pt, ci * 128:(ci + 1) * 128], n == 0, n == ntl - 1),
                              reads=["ones", ("pt", pt)], writes=[("ps", ob[1])])
                        if n != ntl - 1:
                            continue
                        os_, od = plan.qgroups[g][0], plan.qgroups[g][1]
                        if len(od) == 2:
                            c1, c0_ = od[1][1], od[0][1]
                            pdims = [(c1, c0_), (1, c1)]
                        else:
                            pdims = [(1, 128)]
                        pn = ap(PS[ob[0]], gi * 128, pdims)
                        pd = ap(PS[ob[1]], gi * 128, pdims)
                        if mode == "set":
                            P.add("dve", CP(ap(NUM, j * 1024 + os_, od), pn), reads=[("ps", ob[0])], writes=[("NUM", j, g)])
                            P.add("act", ACTF(ap(DEN, j * 1024 + os_, od), pd, AF.Copy),
                                  reads=[("ps", ob[1])], writes=[("DEN", j, g)])
                        elif mode == "add":
                            dn = ap(NUM, j * 1024 + os_, od)
                            dd = ap(DEN, j * 1024 + os_, od)
                            P.add("dve", TT(dn, pn, dn, ALU.add), reads=[("ps", ob[0])], writes=["NUMall"])
                            P.add("dve", TT(dd, pd, dd, ALU.add), reads=[("ps", ob[1])], writes=["DENall"])
                        else:
                            rd = ap(RD, 0, pdims)
                            P.add("dve", RCP(rd, pd), reads=[("ps", ob[1])], writes=["rd"])
                            P.add("dve", TT(ap(YB, j * 1024 + os_, od), pn, rd, ALU.mult),
                                  reads=[("ps", ob[0]), "rd"], writes=[("YB", j)])

                parts.append((pA, pB))
        return skewed(parts, 1)

    numden_set = [("NUM", j, g) for j in range(4) for g in range(4)] + [("DEN", j, g) for j in range(4) for g in range(4)]
    vkeys = [("V", n, h) for n in range(12) for h in range(2)]

    class _Halt(Exception):
        pass

    def halt_if(tag, t, n, dt):
        def f(ws):
            dump(tag, t, n, dt)
        return f

    cur = None
    hp = 0
    for pi, plan in enumerate(plans):
        mode = {"3a": "set", "3b": "set", "g2": "add", "g1": "add", "nb": "nb"}[plan.name]
        if plan.tokset != cur:
            def hb(ws, nm=plan.tokset):
                sa = list(att_q)
                att_q.clear()
                run_merged(sa, hT_steps(nm, [6, 7]))
            U([], hb)
            cur = plan.tokset
        for half in range(2):
            pset = hp % 2
            hp += 1
            project_half(plan, half, pset)

            def queue_att(ws, pi=pi, plan=plan, mode=mode, half=half, pset=pset):
                st = attend_steps(pi, plan, mode, half, pset)
                if plan.name == "g2" and half == 0:
                    st = [lambda: fence(numden_set, ["NUMall", "DENall"])] + st
                att_q.extend(st)
            U([], queue_att)
    U([], lambda ws: drain())
    U([], halt_if("NUM", NUM, 4096, F32))

    def fin_ya(ws):
        for j in range(4):
            for h in range(2):
                sl = slice(h * 512, (h + 1) * 512)
                P.add("dve", RCP(DEN[:, j, sl], DEN[:, j, sl]), reads=["DENall"], writes=[("rden", j, h)])
                P.add("dve", TT(YA[:, j, sl], NUM[:, j, sl], DEN[:, j, sl], ALU.mult),
                      reads=["NUMall", ("rden", j, h)], writes=[("YA", j)])
        dump("YAB", YAB, 8192, BF16)
        old = ([("KT", j, tb) for j in range(4) for tb in range(3)] + vkeys +
               [("QT", j, q) for j in range(4) for q in (0, 512)] + ["tab", "rd"] +
               [("tmps", t, i) for t in range(2) for i in range(4)] + [("pt", t) for t in range(4)])
        fence(old, ["arD"])

    full = STOP not in [p.name for p in plans]
    if full:
        U([], fin_ya)
        own_blk = [(256, [(1, 512)]), (768, [(1, 512)])]

        def emit_gate(ws, m, br):
            for blk in range(2):
                st, dims = own_blk[blk]
                b = nxt("ps", 8)
                sc = (m * 2 + blk) % 2
                for k in range(16):
                    P.add("pe", MM(PS[b][:, :], ap(WBF, ws * 2048 + k * 128, [(1, 128)]),
                                   ap(A, k * 1536 + st, dims), k == 0, k == 15),
                          reads=[("wbf", ws)] + hk, writes=[("ps", b)])
                P.add("act", ACTF(SCR[:, sc * 4 + br, :], PS[b][:, :], AF.Sigmoid,
                                  bias=BGATE[:, br * 16 + m:br * 16 + m + 1]),
                      reads=[("ps", b), "bgate", "arD"], writes=[("scr", sc, br)])

        def emit_proj(ws, m):
            for blk in range(2):
                sc = (m * 2 + blk) % 2
                for br, (Y, yk) in enumerate(((YA, "YA"), (YB, "YB"))):
                    b = nxt("ps", 8)
                    for k in range(4):
                        P.add("pe", MM(PS[b][:, :], ap(WBF, ws[br] * 2048 + k * 128, [(1, 128)]),
                                       Y[:, k, blk * 512:(blk + 1) * 512], k == 0, k == 3),
                              reads=[("wbf", ws[br])] + [(yk, jj) for jj in range(4)], writes=[("ps", b)])
                    P.add("dve", TT(SCR[:, sc * 4 + 2 + br, :], PS[b][:, :], SCR[:, sc * 4 + br, :], ALU.mult),
                          reads=[("ps", b), ("scr", sc, br), "arD"], writes=[("scr", sc, 2 + br)])
                P.add("pool", TT(C[:, m, blk * 512:(blk + 1) * 512], SCR[:, sc * 4 + 2, :], SCR[:, sc * 4 + 3, :], ALU.add),
                      reads=[("scr", sc, 2), ("scr", sc, 3), "arD"], writes=[("C", m, blk)])

        for m in range(16):
            U([(wcols(w_gate, m * 128, 128), 16, 128)], lambda ws, m=m: emit_gate(ws[0], m, 0))
            U([(wcols(w_gate, 2048 + m * 128, 128), 16, 128)], lambda ws, m=m: emit_gate(ws[0], m, 1))
            U([(wcols(w_pa, m * 128, 128, nk=4), 4, 128), (wcols(w_pb, m * 128, 128, nk=4), 4, 128)],
              lambda ws, m=m: emit_proj(ws, m))
        U([], halt_if("MG", C, 16 * 1024, BF16))

        ckeys = [("C", m, blk) for m in range(16) for blk in range(2)]

        def start_E(ws):
            fence(["NUMall", "DENall"] + numden_set + [("rden", j, h) for j in range(4) for h in range(2)] +
                  [("scr", s_, i) for s_ in range(2) for i in range(4)], ["arE"])
            for tt in range(8):
                P.add("sp", DMA(Bf[:, tt, :], x_sets["near"].ap()[256 + tt * 128:256 + (tt + 1) * 128, :]),
                      reads=["arE"], writes=[("x1", tt, 0)], dma=True)

        def emit_out(wss, cg2):
            assert wss[1] == wss[0] + 1 and wss[0] % 2 == 0, wss
            last = (cg2 == 7)
            if last:
                fence(hk, ["arF"])
                parts = [norm_steps(Bf[:, tt, :], [("x1c", tt, cg) for cg in range(4)], A, tt * 128, 1536,
                                    GMLP, "gmlp", ["arF"], list(range(8))) for tt in range(8)]
            for tt in range(8):
                b = nxt("ps", 8)
                for k in range(16):
                    P.add("pe", MM(PS[b][:, 0:256], C[:, k, tt * 128:(tt + 1) * 128],
                                   ap(WBF, wss[0] * 2048 + k * 128, [(2048, 2), (1, 128)]), k == 0, k == 15),
                          reads=[("wbf", wss[0]), ("wbf", wss[1])] + ckeys, writes=[("ps", b)])
                xsl = Bf[:, tt, cg2 * 256:(cg2 + 1) * 256]
                P.add("dve", TT(xsl, PS[b][:, 0:256], xsl, ALU.add),
                      reads=[("ps", b), ("x1", tt, 0)], writes=[("x1c", tt, cg2 // 2)])
                if last:
                    parts[tt][2]()
                    if tt >= 1:
                        parts[tt - 1][3]()
                    if tt >= 2:
                        parts[tt - 2][1]()
            if last:
                parts[7][3]()
                parts[6][1]()
                parts[7][1]()
                fence(ckeys + [("YA", j) for j in range(4)], ["arG"])
                P.add("sp", DMA(GFIN[:, :], gfin_d.ap()), reads=["arG"], writes=["gfin"] + [("YB", j) for j in range(4)], dma=True)

        U([], start_E)
        for cg2 in range(8):
            U([(wcols(w_out, cg2 * 256 + q * 128, 128), 16, 128) for q in range(2)],
              lambda ws, cg2=cg2: emit_out(ws, cg2))
        U([], halt_if("X1", Bf, 8 * 2048, F32))

        h2k = [("A", m, hb) for m in range(8) for hb in range(2)]

        U([], halt_if("H2", A, 16 * 1536, BF16))

        def emit_up(ws, fc, f):
            us = fc % 2
            for blk in range(2):
                b = nxt("ps", 8)
                for k in range(16):
                    P.add("pe", MM(PS[b][:, :], ap(WBF, ws * 2048 + k * 128, [(1, 128)]),
                                   A[:, k, blk * 512:(blk + 1) * 512], k == 0, k == 15),
                          reads=[("wbf", ws)] + h2k, writes=[("ps", b)])
                sc = nxt("rl", 2)
                P.add("act", ACTF(RL[:, sc, :], PS[b][:, :], AF.Relu), reads=[("ps", b), "arG"], writes=[("rl", sc)])
                P.add("pool", TT(C[:, us * 8 + f, blk * 512:(blk + 1) * 512], RL[:, sc, :], RL[:, sc, :], ALU.mult),
                      reads=[("rl", sc), "arG"], writes=[("uT", us, f, blk)])

        fcol = {}

        def final_a(tt):
            col = nxt("col", 64)
            fcol[tt] = col
            xk = [("x1c", tt, cg) for cg in range(4)]
            P.add("act", ACTF(SQJ[:, :], Bf[:, tt, :], AF.Square, accum_out=SS[:, col:col + 1]),
                  reads=xk, writes=[("ss", col)])
            P.add("act", ACTF(RS[:, col:col + 1], SS[:, col:col + 1], AF.Ln, bias=EPS, scale=1.0 / D),
                  reads=[("ss", col)], writes=[("rs", col)])
            P.add("act", ACTF(RS[:, col:col + 1], RS[:, col:col + 1], AF.Exp, scale=-0.5),
                  reads=[("rs", col)], writes=[("rs", col)])

        def final_b(tt):
            col = fcol[tt]
            xk = [("x1c", tt, cg) for cg in range(4)]
            s = nxt("stg", 2)
            P.add("dve", STT(STG[:, s, :], Bf[:, tt, :], RS[:, col:col + 1], GFIN[:, :], ALU.mult, ALU.mult),
                  reads=xk + [("rs", col), "gfin"], writes=[("stg", s)])
            P.add("sp", DMA(out_d.ap()[tt * 128:(tt + 1) * 128, :], STG[:, s, :]),
                  reads=[("stg", s)], writes=[("out", tt)], dma=True)

        def emit_down(wss, fc, cg):
            us = fc % 2
            assert wss[1] == wss[0] + 1 and wss[0] % 2 == 0, wss
            ukeys = [("uT", us, f, blk) for f in range(8) for blk in range(2)]
            for tt in range(8):
                b = nxt("ps", 8)
                for f in range(8):
                    P.add("pe", MM(PS[b][:, :], C[:, us * 8 + f, tt * 128:(tt + 1) * 128],
                                   ap(WBF, wss[0] * 2048 + f * 256, [(2048, 2), (1, 256)]), f == 0, f == 7),
                          reads=[("wbf", wss[0]), ("wbf", wss[1])] + ukeys, writes=[("ps", b)])
                xsl = Bf[:, tt, cg * 512:(cg + 1) * 512]
                P.add("dve", TT(xsl, PS[b][:, :], xsl, ALU.add), reads=[("ps", b)], writes=[("x1c", tt, cg)])
                if fc == 7 and cg == 3:
                    final_a(tt)
                    if tt >= 1:
                        final_b(tt - 1)
            if fc == 7 and cg == 3:
                final_b(7)

        for fc in range(8):
            for f in range(8):
                U([(wcols(w_up, fc * 1024 + f * 128, 128), 16, 128)], lambda ws, fc=fc, f=f: emit_up(ws[0], fc, f))
            for cg in range(4):
                U([(wcols(w_down, cg * 512 + q * 256, 256, r0=fc * 1024, nk=8), 8, 256) for q in range(2)],
                  lambda ws, fc=fc, cg=cg: emit_down(ws, fc, cg))
        U([], halt_if("X2", Bf, 8 * 2048, F32))

    try:
        run_units()
    except _Stop:
        pass

    tail_reads = [k for k in P.lastw if isinstance(k, tuple) and k[0] == "out"] + (["dbg"] if "dbg" in P.lastw else [])
    P.add("pool", MS(JUNK[:, 15:16], 0.0), reads=tail_reads, writes=["fin"])

    from contextlib import ExitStack
    with ExitStack() as st:
        sems = {e: st.enter_context(nc.semaphore("c_" + e)) for e in ENGS}
        dsems = {"sp": [st.enter_context(nc.semaphore("d_sp%d" % i)) for i in range(Prog.NDSEM)]}
        block = st.enter_context(nc.Block())
        P.emit(block, sems, dsems)
    print("ops:", len(P.ops), {e: sum(1 for o in P.ops if o["eng"] == e) for e in ENGS}, flush=True)
    return nc, (dbg[0] is not None)


_CACHE = {}


def kernel(x, norm_mix, w_qkv, w_gate, b_gate, rpb, w_proj_a, w_proj_b, w_out,
           norm_mlp, w_up, w_down, norm_final):
    f = lambda a: np.ascontiguousarray(np.asarray(a, dtype=np.float32))
    x = f(x)
    plans = make_plans()
    slot_maps, contents = build_tables(plans, f(rpb)[0])
    ntab = sum(contents[0][pi][j].shape[1] for pi in range(len(plans)) for j in range(4))
    nc, has_dbg = build(plans, slot_maps, ntab)

    def colvec(g, n):
        return np.ascontiguousarray(f(g).reshape(n, 128).T)

    shared = {
        "w_qkv": f(w_qkv)[0], "w_gate": f(w_gate)[0], "w_pa": f(w_proj_a)[0], "w_pb": f(w_proj_b)[0],
        "w_out": f(w_out)[0], "w_up": f(w_up)[0], "w_down": f(w_down)[0],
        "gmix": colvec(norm_mix[0], 16), "gmlp": colvec(norm_mlp[0], 16),
        "gfin": np.ascontiguousarray(np.broadcast_to(f(norm_final)[None, :], (128, D))),
        "bgate": colvec(b_gate[0], 32),
        "ident": np.eye(128, dtype=np.float32).astype(ml_dtypes.bfloat16),
    }
    in_maps = []
    for core in range(NCORES):
        b, c = core // 4, core % 4
        m = dict(shared)
        for nm in ("3a", "3b", "near"):
            tok = set_tokens(c, nm)
            ok = (tok >= 0) & (tok < S)
            xs = np.zeros((1536, D), np.float32)
            xs[ok] = x[b, tok[ok]]
            m["x_" + nm] = xs
        m["tabs"] = np.ascontiguousarray(np.concatenate(
            [contents[c][pi][j] for pi in range(len(plans)) for j in range(4)], axis=1))
        in_maps.append(m)
    if os.environ.get("KTRACE"):
        res = run_bass_kernel_spmd(nc, in_maps, core_ids=list(range(NCORES)), trace=True)
        print("exec_time_ns", res.exec_time_ns, flush=True)
    else:
        res = run_bass_kernel_spmd(nc, in_maps, core_ids=list(range(NCORES)))
    if has_dbg:
        kernel.dbg = [np.asarray(r["dbg"]) for r in res.results]
    out = np.zeros((NB, S, D), np.float32)
    for core in range(NCORES):
        b, c = core // 4, core % 4
        out[b, set_tokens(c, "own")] = np.asarray(res.results[core]["out"])
    return out
```

```python
import os
import numpy as np
import ml_dtypes
import concourse.bass as bass
import concourse.mybir as mybir
from concourse.bass_utils import run_bass_kernel_spmd

F32 = mybir.dt.float32
BF16 = mybir.dt.bfloat16
AF = mybir.ActivationFunctionType
ALU = mybir.AluOpType

D = 2048
S = 4096
NB = 2
HD = 128
DFF = 8192
EPS = 1e-6
NEG = -30000.0
SCALE = HD ** -0.5
NCORES = int(os.environ.get('KCORES', '8'))
DILS = (1, 4, 16)

STOP = os.environ.get("KSTOP", "")
DUMP = os.environ.get("KDUMP", "")


def set_tokens(c, name):
    base = 1024 * c - 1024
    if name in ("3a", "3b"):
        r0 = 0 if name == "3a" else 8
        r = np.arange(8)[:, None] + r0
        i = np.arange(192)[None, :]
        return (base + 16 * i + r).reshape(-1)
    if name == "near":
        return base + np.arange(768, 2304)
    if name == "own":
        return base + np.arange(1024, 2048)
    raise KeyError(name)


def pat_idx(start, dims):
    idx = np.array([start])
    for step, cnt in dims:
        idx = (idx[:, None] + step * np.arange(cnt)[None, :]).reshape(-1)
    return idx


def alibi_slopes():
    return 2.0 ** (-8.0 * np.arange(1, 13) / 12.0)


def bias_block(kind, hidx, ktok, qtok, rpb):
    k = ktok[:, None].astype(np.int64)
    q = qtok[None, :].astype(np.int64)
    kin = (k >= 0) & (k < S)
    if kind < 3:
        d = DILS[kind]
        slope = alibi_slopes()[kind * 4 + hidx]
        same = (np.mod(k, d) == np.mod(q, d))
        ik = np.floor_divide(k, d)
        iq = np.floor_divide(q, d)
        rel = np.abs(ik - iq)
        valid = kin & same & (rel <= 64)
        b = -(np.float32(slope) * np.float32(d)) * rel.astype(np.float32)
        return np.where(valid, b, np.float32(NEG)).astype(np.float32)
    rk, ck = np.floor_divide(k, 64), np.mod(k, 64)
    rq, cq = np.floor_divide(q, 64), np.mod(q, 64)
    rs = np.clip(rq - 4, 0, 56)
    cs = np.clip(cq - 8, 0, 48)
    valid = kin & (rk >= rs) & (rk < rs + 8) & (ck >= cs) & (ck < cs + 16)
    dr = np.clip(rk - rq + 7, 0, 14)
    dc = np.clip(ck - cq, -15, 15) + 15
    g = rpb[hidx][dr, dc]
    return np.where(valid, g, np.float32(NEG)).astype(np.float32)


class AttnPlan:
    def __init__(self, name, tokset, kind, head0):
        self.name, self.tokset, self.kind, self.head0 = name, tokset, kind, head0
        self.vtiles = []
        self.qgroups = []
        self.qblocks = []

    def vt(self, start, dims):
        key = (start, tuple(map(tuple, dims)))
        for n, v in enumerate(self.vtiles):
            if v == key:
                return n
        self.vtiles.append(key)
        return len(self.vtiles) - 1


def make_plans():
    plans = []
    for hf, nm in enumerate(("3a", "3b")):
        p = AttnPlan(nm, nm, 2, 8)
        for a in range(4):
            tl = [p.vt(128 * (3 * a + t), [(1, 128)]) for t in range(3)]
            p.qgroups.append((8 * hf + 2 * a, [(1, 2), (16, 64)], tl, (a * 128, [(1, 128)])))
        p.qblocks = [(64, [(192, 8), (1, 64)], 0)]
        plans.append(p)
    near_q = [(256, [(1, 512)], 0), (768, [(1, 512)], 512)]
    p = AttnPlan("g2", "near", 1, 4)
    for rp in range(4):
        kts = [p.vt(4 * k0 - 768 + rp, [(4, 128)]) for k0 in (192, 320, 448)]
        for t in range(2):
            own = (512 * t + rp, [(4, 128)])
            p.qgroups.append((own[0], own[1], [kts[t], kts[t + 1]], own))
    p.qblocks = near_q
    plans.append(p)
    p = AttnPlan("g1", "near", 0, 0)
    kts = [p.vt(192 + 128 * m, [(1, 128)]) for m in range(9)]
    for j in range(8):
        own = (128 * j, [(1, 128)])
        p.qgroups.append((own[0], own[1], [kts[j], kts[j + 1]], own))
    p.qblocks = near_q
    plans.append(p)
    p = AttnPlan("nb", "near", 3, 12)
    for g in range(8):
        R = 16 + 2 * g
        tiles = list(range(R - 4, R + 5, 2))
        if g == 0:
            tiles.append(22)
        if g == 7:
            tiles.insert(0, 24)
        tiles = sorted(set(tiles))
        kts = [p.vt(64 * K0 - 768, [(1, 128)]) for K0 in tiles]
        own = (128 * g, [(1, 128)])
        p.qgroups.append((own[0], own[1], kts, own))
    p.qblocks = near_q
    plans.append(p)
    plans = [plans[0], plans[1], plans[4], plans[2], plans[3]]
    return plans


def build_tables(plans, rpb_all):
    slot_maps, contents = [], [[] for _ in range(4)]
    for p in plans:
        toks = [set_tokens(c, p.tokset) for c in range(4)]
        owns = [set_tokens(c, "own") for c in range(4)]
        pm, pc = [], [[] for _ in range(4)]
        for j in range(4):
            seen, smap, tabs = {}, {}, [[] for _ in range(4)]
            for g, (os_, od, tl, _) in enumerate(p.qgroups):
                qi = pat_idx(os_, od)
                for n, vt in enumerate(tl):
                    ki = pat_idx(*p.vtiles[vt])
                    blks = [bias_block(p.kind, j, toks[c][ki], owns[c][qi],
                                       rpb_all) for c in range(4)]
                    key = b"".join(b.tobytes() for b in blks)
                    if key not in seen:
                        seen[key] = len(tabs[0])
                        for c in range(4):
                            tabs[c].append(blks[c])
                    smap[(g, n)] = seen[key]
            pm.append(smap)
            for c in range(4):
                pc[c].append(np.stack(tabs[c], axis=1))
        slot_maps.append(pm)
        for c in range(4):
            contents[c].append(pc[c])
    return slot_maps, contents


ENGS = ("sp", "act", "dve", "pool", "pe")


class Prog:
    NDSEM = 8

    def __init__(self, nc):
        self.nc = nc
        self.ops = []
        self.lastw = {}
        self.readers = {}

    def add(self, eng, fn, reads=(), writes=(), dma=False):
        idx = len(self.ops)
        deps = set()
        for r in reads:
            w = self.lastw.get(r)
            if w is not None:
                deps.add(w)
        for r in writes:
            w = self.lastw.get(r)
            if w is not None:
                deps.add(w)
            deps.update(self.readers.get(r, ()))
        for r in reads:
            self.readers.setdefault(r, []).append(idx)
        for r in writes:
            self.lastw[r] = idx
            self.readers[r] = []
        deps.discard(idx)
        if eng == "pe":
            deps = {d for d in deps if self.ops[d]["eng"] != "pe"}
        self.ops.append(dict(eng=eng, fn=fn, deps=deps, dma=dma, idx=idx))
        return idx

    def emit(self, block, sems, dsems):
        ops = self.ops
        needed = set()
        for o in ops:
            needed.update(o["deps"])
        cnt = {e: 0 for e in ENGS}
        dcnt = {e: 0 for e in ENGS}
        for o in ops:
            e = o["eng"]
            if o["dma"]:
                j = dcnt[e]
                dcnt[e] += 1
                o["sem"] = dsems[e][j % self.NDSEM]
                o["val"] = 16 * (j // self.NDSEM + 1)
                o["dj"] = j
            elif o["idx"] in needed or o.get("force"):
                cnt[e] += 1
                o["sem"] = sems[e]
                o["val"] = cnt[e]
            else:
                o["sem"] = None
        per = {e: [o for o in ops if o["eng"] == e] for e in ENGS}

        def run(engname, eng):
            waited = {}

            def wait(sem, val):
                if waited.get(sem.num if hasattr(sem, "num") else id(sem), 0) < val:
                    eng.wait_ge(sem, val)
                    waited[sem.num if hasattr(sem, "num") else id(sem)] = val

            for o in per[engname]:
                for d in sorted(o["deps"]):
                    od = ops[d]
                    wait(od["sem"], od["val"])
                if o["dma"] and o["val"] > 16:
                    wait(o["sem"], o["val"] - 16)
                ins = o["fn"](eng)
                if o["sem"] is not None:
                    ins.then_inc(o["sem"], 16 if o["dma"] else 1)

        block.sync(lambda e: run("sp", e))
        block.scalar(lambda e: run("act", e))
        block.vector(lambda e: run("dve", e))
        block.gpsimd(lambda e: run("pool", e))
        block.tensor(lambda e: run("pe", e))


class _Stop(Exception):
    pass


def MM(out, lhsT, rhs, start, stop):
    return lambda e: e.matmul(out, lhsT=lhsT, rhs=rhs, start=start, stop=stop)


def ACTF(out, in_, func, **kw):
    return lambda e: e.activation(out=out, in_=in_, func=func, **kw)


def TT(out, in0, in1, op):
    return lambda e: e.tensor_tensor(out=out, in0=in0, in1=in1, op=op)


def TS(out, in0, s1, s2, op0, op1=None):
    if op1 is None:
        return lambda e: e.tensor_scalar(out=out, in0=in0, scalar1=s1, scalar2=s2, op0=op0)
    return lambda e: e.tensor_scalar(out=out, in0=in0, scalar1=s1, scalar2=s2, op0=op0, op1=op1)


def STT(out, in0, scalar, in1, op0, op1):
    return lambda e: e.scalar_tensor_tensor(out=out, in0=in0, scalar=scalar, in1=in1, op0=op0, op1=op1)


def CP(out, in_):
    return lambda e: e.tensor_copy(out=out, in_=in_)


def DMA(out, in_):
    return lambda e: e.dma_start(out=out, in_=in_)


def TR(out, in_, ident):
    return lambda e: e.transpose(out=out, in_=in_, identity=ident)


def RCP(out, in_):
    return lambda e: e.reciprocal(out=out, in_=in_)


def MS(a, val):
    return lambda e: e.memset(a, val)


def build(plans, slot_maps, ntab):
    nc = bass.Bass("TRN2", target_bir_lowering=False)
    P = Prog(nc)

    def din(name, shape, dt=F32):
        return nc.dram_tensor(name, list(shape), dt, kind="ExternalInput")

    x_sets = {n: din("x_" + n, [1536, D]) for n in ("3a", "3b", "near")}
    w_qkv = din("w_qkv", [D, 3 * D])
    w_gate = din("w_gate", [D, 2 * D])
    w_pa = din("w_pa", [512, D])
    w_pb = din("w_pb", [512, D])
    w_out = din("w_out", [D, D])
    w_up = din("w_up", [D, DFF])
    w_down = din("w_down", [DFF, D])
    gmix_d = din("gmix", [128, 16])
    gmlp_d = din("gmlp", [128, 16])
    gfin_d = din("gfin", [128, D])
    bgate_d = din("bgate", [128, 32])
    ident_d = din("ident", [128, 128], BF16)
    tabs_d = din("tabs", [128, ntab, 128])
    out_d = nc.dram_tensor("out", [1024, D], F32, kind="ExternalOutput")
    dbg = [None]

    pstride = {}

    def sb(name, shape, dt, off):
        t = nc.alloc_sbuf_tensor_at(name, list(shape), dt, offset=16512 + off)
        pstride[t.name] = int(np.prod(shape[1:]))
        return t

    KB = 1024
    A = sb("A", [128, 16, 1536], BF16, 0)
    Bf = sb("Bf", [128, 8, 2048], F32, 48 * KB)
    NUM = sb("NUM", [128, 4, 1024], F32, 48 * KB)
    DEN = sb("DEN", [128, 4, 1024], F32, 64 * KB)
    KT = sb("KT", [128, 4, 1536], BF16, 80 * KB)
    V = sb("V", [128, 12, 512], BF16, 92 * KB)
    QT = sb("QT", [128, 4, 1024], BF16, 104 * KB)
    SCR = sb("SCR", [128, 8, 512], F32, 80 * KB)
    C = sb("C", [128, 16, 1024], BF16, 112 * KB)
    TAB = sb("TAB", [128, 44, 128], F32, 112 * KB)
    TMPS = sb("TMPS", [128, 2, 512], F32, 134 * KB)
    PT = sb("PT", [128, 4, 512], BF16, 138 * KB)
    RD = sb("RD", [128, 512], F32, 142 * KB)
    YA = sb("YA", [128, 4, 1024], BF16, 144 * KB)
    YB = sb("YB", [128, 4, 1024], BF16, 152 * KB)
    YAB = sb("YAB", [128, 8192], BF16, 144 * KB)
    SQJ = sb("SQJ", [128, 2048], BF16, 144 * KB)
    RL = sb("RL", [128, 2, 512], F32, 144 * KB)
    STG = sb("STG", [128, 2, 2048], F32, 160 * KB)
    WBF = sb("WBF", [128, 4, 2048], BF16, 176 * KB)
    GFIN = sb("GFIN", [128, 2048], F32, 152 * KB)
    XN = sb("XN", [128, 2, 2048], BF16, 192 * KB)
    cb = 200 * KB
    IDENT = sb("IDENT", [128, 128], BF16, cb)
    ONES = sb("ONES", [128, 128], BF16, cb + 256)
    GMIX = sb("GMIX", [128, 16], F32, cb + 512)
    GMLP = sb("GMLP", [128, 16], F32, cb + 576)
    BGATE = sb("BGATE", [128, 32], F32, cb + 640)
    SS = sb("SS", [128, 64], F32, cb + 768)
    RS = sb("RS", [128, 64], F32, cb + 1024)
    JUNK = sb("JUNK", [128, 16], F32, cb + 1280)

    PS = [nc.alloc_psum_tensor("ps%d" % i, [128, 512], F32) for i in range(8)]
    for i in range(8):
        pstride[PS[i].name] = 512

    def ap(t, off, dims, p0=0, np_=128):
        ps_ = pstride[t.name]
        return bass.AP(t, p0 * ps_ + off, [[ps_, np_]] + [list(d) for d in dims])

    for dst, srcd, key in ((IDENT, ident_d, "ident"), (GMIX, gmix_d, "gmix"), (GMLP, gmlp_d, "gmlp"),
                           (BGATE, bgate_d, "bgate")):
        P.add("sp", DMA(dst[:, :], srcd.ap()), writes=[key], dma=True)
    P.add("pool", MS(ONES[:, :], 1.0), writes=["ones"])

    ctr = dict(stg=0, wbf=0, xn=0, ps=0, pss=0, pso=0, col=0, tmps=0, pt=0, cast=0, ev=0, rl=0, junk=0, tb=0, pb=0)

    def nxt(k, n):
        v = ctr[k] % n
        ctr[k] += 1
        return v

    def fence(old_keys, new_keys=()):
        jc = nxt("junk", 16)
        P.add("pool", MS(JUNK[:, jc:jc + 1], 0.0), writes=list(old_keys) + list(new_keys) + [("junk", jc)])

    def dump(tag, t, n, dt=F32):
        if DUMP != tag or dbg[0] is not None:
            return
        dbg[0] = nc.dram_tensor("dbg", [128, n], dt, kind="ExternalOutput")
        keys = list(P.lastw.keys())
        P.add("sp", DMA(dbg[0].ap(), ap(t, 0, [(1, n)])), reads=keys, writes=["dbg"], dma=True)
        if STOP == "dump":
            raise _Stop()

    def norm_steps(xin, xkeys, dst_t, dst_off, dst_kstride, gam, gam_key, extra_reads, tbanks, pre=None):
        st = {}

        def part1a():
            if pre is not None:
                pre()
            col = nxt("col", 64)
            st["col"] = col
            P.add("act", ACTF(SQJ[:, :], xin, AF.Square, accum_out=SS[:, col:col + 1]),
                  reads=xkeys, writes=[("ss", col)])
            P.add("act", ACTF(RS[:, col:col + 1], SS[:, col:col + 1], AF.Ln, bias=EPS, scale=1.0 / D),
                  reads=[("ss", col)], writes=[("rs", col)])
            P.add("act", ACTF(RS[:, col:col + 1], RS[:, col:col + 1], AF.Exp, scale=-0.5),
                  reads=[("rs", col)], writes=[("rs", col)])

        def part1b():
            col = st["col"]
            xs = nxt("xn", 2)
            st["xs"] = xs
            P.add("dve", TS(XN[:, xs, :], xin, RS[:, col:col + 1], None, ALU.mult),
                  reads=xkeys + [("rs", col)], writes=[("xn", xs)])

        def part1():
            part1a()
            part1b()

        def part2():
            xs = st["xs"]
            for hb in range(2):
                b = tbanks[nxt("tb", len(tbanks))]
                pb = PS[b][:, :].bitcast(BF16)
                for kk in range(8):
                    k = hb * 8 + kk
                    P.add("pe", TR(pb[:, kk * 128:(kk + 1) * 128], XN[:, xs, k * 128:(k + 1) * 128], IDENT[:, :]),
                          reads=[("xn", xs), "ident"], writes=[("ps", b)])
                o = ap(dst_t, hb * 8 * dst_kstride + dst_off, [(dst_kstride, 8), (1, 128)])
                g = ap(gam, hb * 8, [(1, 8), (0, 128)])
                i0 = pb.rearrange("p (k t) -> p k t", k=8)
                P.add("dve", TT(o, i0, g, ALU.mult),
                      reads=[("ps", b), gam_key] + extra_reads, writes=[("A", dst_off // 128, hb)])

        return part1, part2, part1a, part1b

    def skewed(parts, skew=1):
        steps = []
        n = len(parts)
        for i in range(n + skew):
            if i < n:
                steps.append(parts[i][0])
            if i - skew >= 0:
                steps.append(parts[i - skew][1])
        return steps

    def run_merged(sa, sb_):
        ia = ib = 0
        while ia < len(sa) or ib < len(sb_):
            if ib >= len(sb_) or (ia < len(sa) and ia * len(sb_) <= ib * len(sa)):
                sa[ia]()
                ia += 1
            else:
                sb_[ib]()
                ib += 1

    def load_w(src_ap, nk, ncol, ws):
        s = nxt("stg", 2)
        n = nk * ncol
        P.add("sp", DMA(ap(STG, s * 2048, [(ncol, nk), (1, ncol)]), src_ap), writes=[("stg", s)], dma=True)
        ce = ("dve", "act")[nxt("cast", 2)]
        o = ap(WBF, ws * 2048, [(1, n)])
        i = ap(STG, s * 2048, [(1, n)])
        if ce == "act":
            P.add("act", ACTF(o, i, AF.Copy), reads=[("stg", s)], writes=[("wbf", ws)])
        else:
            P.add("dve", CP(o, i), reads=[("stg", s)], writes=[("wbf", ws)])
        return ws

    def wcols(w, c0, ncol, r0=0, nk=16):
        a = w.ap()[r0:r0 + nk * 128, c0:c0 + ncol]
        return a.rearrange("(k p) n -> p k n", p=128)

    def evac_copy(dst, src, rk, wk):
        if nxt("ev", 4) != 3:
            P.add("act", ACTF(dst, src, AF.Copy), reads=rk, writes=wk)
        else:
            P.add("dve", CP(dst, src), reads=rk, writes=wk)

    units = []

    def U(loads, fn):
        units.append((loads, fn))

    def run_units():
        wpos = 0
        pending = []
        slots_of = {}
        nl = 0
        for i, (loads, fn) in enumerate(units):
            while nl < len(units):
                lo = units[nl][0]
                if not lo:
                    slots_of[nl] = []
                    nl += 1
                    continue
                n = len(lo)
                start = wpos + (1 if (n == 2 and wpos % 2 == 1) else 0)
                oldest = pending[0][1] if pending else start
                if start + n - oldest > 4 or nl > i + 6:
                    break
                wpos = start + n
                sl = [(start + q) % 4 for q in range(n)]
                for (sa, nk, ncol), ws in zip(lo, sl):
                    load_w(sa, nk, ncol, ws)
                slots_of[nl] = sl
                pending.append((nl, start))
                nl += 1
            assert i in slots_of, (i, nl)
            fn(slots_of[i])
            pending = [p for p in pending if p[0] != i]

    tab_base = {}
    _t = 0
    for _pi in range(len(plans)):
        for _j in range(4):
            tab_base[(_pi, _j)] = _t
            _t += max(slot_maps[_pi][_j].values()) + 1
    hk = [("A", m, hb) for m in range(12) for hb in range(2)]

    def hT_steps(setname, tbanks):
        xd = x_sets[setname]
        parts = []
        for m in range(12):
            box = {}

            def p1(m=m, box=box):
                s = nxt("stg", 2)
                P.add("sp", DMA(STG[:, s, :], xd.ap()[m * 128:(m + 1) * 128, :]), writes=[("stg", s)], dma=True)
                q1, q2, _, _ = norm_steps(STG[:, s, :], [("stg", s)], A, m * 128, 1536, GMIX, "gmix", [], tbanks)
                box["p2"] = q2
                q1()

            def p2(box=box):
                box["p2"]()

            parts.append((p1, p2))
        return skewed(parts, 1)

    def build_hT(setname):
        for s_ in hT_steps(setname, list(range(8))):
            s_()

    PBANKS = [6, 7]

    def k_steps(ws, kidx):
        def step(tb):
            b = PBANKS[nxt("pb", 2)]
            for k in range(16):
                P.add("pe", MM(PS[b][:, :], ap(WBF, ws * 2048 + k * 128, [(1, 128)]),
                               ap(A, k * 1536 + tb * 512, [(1, 512)]), k == 0, k == 15),
                      reads=[("wbf", ws)] + hk, writes=[("ps", b)])
            evac_copy(ap(KT, kidx * 1536 + tb * 512, [(1, 512)]), PS[b][:, :], [("ps", b)], [("KT", kidx, tb)])
        return [lambda tb=tb: step(tb) for tb in range(3)]

    def q_steps(ws, qidx, qblocks):
        def step(st, dims, qoff):
            b = PBANKS[nxt("pb", 2)]
            for k in range(16):
                P.add("pe", MM(PS[b][:, :], ap(WBF, ws * 2048 + k * 128, [(1, 128)]),
                               ap(A, k * 1536 + st, dims), k == 0, k == 15),
                      reads=[("wbf", ws)] + hk, writes=[("ps", b)])
            evac_copy(ap(QT, qidx * 1024 + qoff, [(1, 512)]), PS[b][:, :], [("ps", b)], [("QT", qidx, qoff)])
        return [lambda st=st, dims=dims, qoff=qoff: step(st, dims, qoff) for (st, dims, qoff) in qblocks]

    def v_steps(wss, vset, vtiles):
        assert wss[1] == wss[0] + 1 and wss[0] % 2 == 0, wss

        def step(n, st, dims):
            b = PBANKS[nxt("pb", 2)]
            for k in range(16):
                P.add("pe", MM(PS[b][:, 0:256], ap(A, k * 1536 + st, list(dims)),
                               ap(WBF, wss[0] * 2048 + k * 128, [(2048, 2), (1, 128)]), k == 0, k == 15),
                      reads=[("wbf", wss[0]), ("wbf", wss[1])] + hk, writes=[("ps", b)])
            evac_copy(V[:, n, vset * 256:(vset + 1) * 256], PS[b][:, 0:256], [("ps", b)], [("V", n, vset)])
        return [lambda n=n, st=st, dims=dims: step(n, st, dims) for n, (st, dims) in enumerate(vtiles)]

    from collections import deque
    att_q = deque()
    own_left = [0]

    def mix(own):
        for s_ in own:
            s_()
            own_left[0] -= 1
            if att_q:
                r = -(-len(att_q) // max(own_left[0] + 1, 1))
                for _ in range(min(r, len(att_q))):
                    att_q.popleft()()

    def drain():
        while att_q:
            att_q.popleft()()

    def project_half(plan, half, pset):
        nq = len(plan.qblocks)
        total = 2 * 3 + 2 * nq + len(plan.vtiles)

        def first(ws, jj, fn):
            pass

        for jj in range(2):
            head = plan.head0 + 2 * half + jj
            U([(wcols(w_qkv, 2048 + head * 128, 128), 16, 128)],
              lambda ws, jj=jj, first_=(jj == 0): (own_left.__setitem__(0, total) if first_ else None,
                                                   mix(k_steps(ws[0], pset * 2 + jj))))
        for jj in range(2):
            head = plan.head0 + 2 * half + jj
            U([(wcols(w_qkv, head * 128, 128), 16, 128)],
              lambda ws, jj=jj, qb=plan.qblocks: mix(q_steps(ws[0], pset * 2 + jj, qb)))
        U([(wcols(w_qkv, 4096 + (plan.head0 + 2 * half + q) * 128, 128), 16, 128) for q in range(2)],
          lambda ws, vt=list(plan.vtiles): (mix(v_steps(ws, pset, vt)), drain()))

    SBANKS = [0, 1]
    OBANKS = [(2, 3), (4, 5)]

    def attend_steps(pi, plan, mode, half, pset):
        smaps = slot_maps[pi]
        parts = []
        obox = {"ob": None}
        for jj in range(2):
            j = 2 * half + jj
            kidx = pset * 2 + jj
            vcol = pset * 256 + jj * 128
            smap = smaps[j]
            nslot = max(smap.values()) + 1
            assert nslot <= 44, nslot
            t0 = tab_base[(pi, j)]
            blocks = []
            for g, (os_, od, tl, qpat) in enumerate(plan.qgroups):
                for n, vt in enumerate(tl):
                    blocks.append((g, n, vt, len(tl)))
            kt_keys = [("KT", kidx, tb) for tb in range(3)]
            for c0 in range(0, len(blocks), 4):
                chunk = blocks[c0:c0 + 4]
                box = {}

                def pA(j=j, kidx=kidx, smap=smap, nslot=nslot, t0=t0, chunk=chunk, first=(c0 == 0), box=box, kt_keys=kt_keys):
                    if first:
                        P.add("sp", DMA(TAB[:, 0:nslot, :], tabs_d.ap()[:, t0:t0 + nslot, :]), writes=["tab"], dma=True)
                    sbk = SBANKS[nxt("pss", len(SBANKS))]
                    for ci, (g, n, vt, ntl) in enumerate(chunk):
                        vst, vdims = plan.vtiles[vt]
                        qst, qd = plan.qgroups[g][3]
                        P.add("pe", MM(PS[sbk][:, ci * 128:(ci + 1) * 128], ap(KT, kidx * 1536 + vst, list(vdims)),
                                       ap(QT, kidx * 1024 + qst, list(qd)), True, True),
                              reads=kt_keys + [("QT", kidx, 0), ("QT", kidx, 512)], writes=[("ps", sbk)])
                    ts = nxt("tmps", 2)
                    slots = [smap[(g, n)] for (g, n, vt, ntl) in chunk]
                    i = 0
                    while i < len(slots):
                        jn = i + 1
                        stp = 0
                        if jn < len(slots):
                            stp = slots[jn] - slots[i]
                            jn += 1
                            while jn < len(slots) and slots[jn] - slots[jn - 1] == stp:
                                jn += 1
                        cnt_ = jn - i
                        P.add("dve", STT(ap(TMPS, ts * 512 + i * 128, [(128, cnt_), (1, 128)]),
                                         ap(PS[sbk], i * 128, [(128, cnt_), (1, 128)]), SCALE,
                                         ap(TAB, slots[i] * 128, [(stp * 128, cnt_), (1, 128)]), ALU.mult, ALU.add),
                              reads=[("ps", sbk), "tab"], writes=[("tmps", ts, i)])
                        i = jn
                    pt = nxt("pt", 4)
                    ncol = len(chunk) * 128
                    P.add("act", ACTF(PT[:, pt, 0:ncol], TMPS[:, ts, 0:ncol], AF.Exp),
                          reads=[("tmps", ts, i2) for i2 in range(4)], writes=[("pt", pt)])
                    box["pt"] = pt

                def pB(j=j, vcol=vcol, pset=pset, chunk=chunk, box=box):
                    pt = box["pt"]
                    for ci, (g, n, vt, ntl) in enumerate(chunk):
                        gi = g % 4
                        if gi == 0 and n == 0:
                            obox["ob"] = OBANKS[nxt("pso", len(OBANKS))]
                        ob = obox["ob"]
                        P.add("pe", MM(PS[ob[0]][:, gi * 128:(gi + 1) * 128], V[:, vt, vcol:vcol + 128],
                                       PT[:, pt, ci * 128:(ci + 1) * 128], n == 0, n == ntl - 1),
                              reads=[("V", vt, pset), ("pt", pt)], writes=[("ps", ob[0])])
                        P.add("pe", MM(PS[ob[1]][:, gi * 128:(gi + 1) * 128], ONES[:, :],
                                       PT[:, pt, ci * 128:(ci + 1) * 128], n == 0, n == ntl - 1),
                              reads=["ones", ("pt", pt)], writes=[("ps", ob[1])])
                        if n != ntl - 1:
                            continue
                        os_, od = plan.qgroups[g][0], plan.qgroups[g][1]
                        if len(od) == 2:
                            c1, c0_ = od[1][1], od[0][1]
                            pdims = [(c1, c0_), (1, c1)]
                        else:
                            pdims = [(1, 128)]
                        pn = ap(PS[ob[0]], gi * 128, pdims)
                        pd = ap(PS[ob[1]], gi * 128, pdims)
                        if mode == "set":
                            P.add("dve", CP(ap(NUM, j * 1024 + os_, od), pn), reads=[("ps", ob[0])], writes=[("NUM", j, g)])
                            P.add("act", ACTF(ap(DEN, j * 1024 + os_, od), pd, AF.Copy),
                                  reads=[("ps", ob[1])], writes=[("DEN", j, g)])
                        elif mode == "add":
                            dn = ap(NUM, j * 1024 + os_, od)
                            dd = ap(DEN, j * 1024 + os_, od)
                            P.add("dve", TT(dn, pn, dn, ALU.add), reads=[("ps", ob[0])], writes=["NUMall"])
                            P.add("dve", TT(dd, pd, dd, ALU.add), reads=[("ps", ob[1])], writes=["DENall"])
                        else:
                            rd = ap(RD, 0, pdims)
                            P.add("dve", RCP(rd, pd), reads=[("ps", ob[1])], writes=["rd"])
                            P.add("dve", TT(ap(YB, j * 1024 + os_, od), pn, rd, ALU.mult),
                                  reads=[("ps", ob[0]), "rd"], writes=[("YB", j)])

                parts.append((pA, pB))
        return skewed(parts, 1)

    numden_set = [("NUM", j, g) for j in range(4) for g in range(4)] + [("DEN", j, g) for j in range(4) for g in range(4)]
    vkeys = [("V", n, h) for n in range(12) for h in range(2)]

    class _Halt(Exception):
        pass

    def halt_if(tag, t, n, dt):
        def f(ws):
            dump(tag, t, n, dt)
        return f

    cur = None
    hp = 0
    for pi, plan in enumerate(plans):
        mode = {"3a": "set", "3b": "set", "g2": "add", "g1": "add", "nb": "nb"}[plan.name]
        if plan.tokset != cur:
            def hb(ws, nm=plan.tokset):
                sa = list(att_q)
                att_q.clear()
                run_merged(sa, hT_steps(nm, [6, 7]))
            U([], hb)
            cur = plan.tokset
        for half in range(2):
            pset = hp % 2
            hp += 1
            project_half(plan, half, pset)

            def queue_att(ws, pi=pi, plan=plan, mode=mode, half=half, pset=pset):
                st = attend_steps(pi, plan, mode, half, pset)
                if plan.name == "g2" and half == 0:
                    st = [lambda: fence(numden_set, ["NUMall", "DENall"])] + st
                att_q.extend(st)
            U([], queue_att)
    U([], lambda ws: drain())
    U([], halt_if("NUM", NUM, 4096, F32))

    def fin_ya(ws):
        for j in range(4):
            for h in range(2):
                sl = slice(h * 512, (h + 1) * 512)
                P.add("dve", RCP(DEN[:, j, sl], DEN[:, j, sl]), reads=["DENall"], writes=[("rden", j, h)])
                P.add("dve", TT(YA[:, j, sl], NUM[:, j, sl], DEN[:, j, sl], ALU.mult),
                      reads=["NUMall", ("rden", j, h)], writes=[("YA", j)])
        dump("YAB", YAB, 8192, BF16)
        old = ([("KT", j, tb) for j in range(4) for tb in range(3)] + vkeys +
               [("QT", j, q) for j in range(4) for q in (0, 512)] + ["tab", "rd"] +
               [("tmps", t, i) for t in range(2) for i in range(4)] + [("pt", t) for t in range(4)])
        fence(old, ["arD"])

    full = STOP not in [p.name for p in plans]
    if full:
        U([], fin_ya)
        own_blk = [(256, [(1, 512)]), (768, [(1, 512)])]

        def emit_gate(ws, m, br):
            for blk in range(2):
                st, dims = own_blk[blk]
                b = nxt("ps", 8)
                sc = (m * 2 + blk) % 2
                for k in range(16):
                    P.add("pe", MM(PS[b][:, :], ap(WBF, ws * 2048 + k * 128, [(1, 128)]),
                                   ap(A, k * 1536 + st, dims), k == 0, k == 15),
                          reads=[("wbf", ws)] + hk, writes=[("ps", b)])
                P.add("act", ACTF(SCR[:, sc * 4 + br, :], PS[b][:, :], AF.Sigmoid,
                                  bias=BGATE[:, br * 16 + m:br * 16 + m + 1]),
                      reads=[("ps", b), "bgate", "arD"], writes=[("scr", sc, br)])

        def emit_proj(ws, m):
            for blk in range(2):
                sc = (m * 2 + blk) % 2
                for br, (Y, yk) in enumerate(((YA, "YA"), (YB, "YB"))):
                    b = nxt("ps", 8)
                    for k in range(4):
                        P.add("pe", MM(PS[b][:, :], ap(WBF, ws[br] * 2048 + k * 128, [(1, 128)]),
                                       Y[:, k, blk * 512:(blk + 1) * 512], k == 0, k == 3),
                              reads=[("wbf", ws[br])] + [(yk, jj) for jj in range(4)], writes=[("ps", b)])
                    P.add("dve", TT(SCR[:, sc * 4 + 2 + br, :], PS[b][:, :], SCR[:, sc * 4 + br, :], ALU.mult),
                          reads=[("ps", b), ("scr", sc, br), "arD"], writes=[("scr", sc, 2 + br)])
                P.add("pool", TT(C[:, m, blk * 512:(blk + 1) * 512], SCR[:, sc * 4 + 2, :], SCR[:, sc * 4 + 3, :], ALU.add),
                      reads=[("scr", sc, 2), ("scr", sc, 3), "arD"], writes=[("C", m, blk)])

        for m in range(16):
            U([(wcols(w_gate, m * 128, 128), 16, 128)], lambda ws, m=m: emit_gate(ws[0], m, 0))
            U([(wcols(w_gate, 2048 + m * 128, 128), 16, 128)], lambda ws, m=m: emit_gate(ws[0], m, 1))
            U([(wcols(w_pa, m * 128, 128, nk=4), 4, 128), (wcols(w_pb, m * 128, 128, nk=4), 4, 128)],
              lambda ws, m=m: emit_proj(ws, m))
        U([], halt_if("MG", C, 16 * 1024, BF16))

        ckeys = [("C", m, blk) for m in range(16) for blk in range(2)]

        def start_E(ws):
            fence(["NUMall", "DENall"] + numden_set + [("rden", j, h) for j in range(4) for h in range(2)] +
                  [("scr", s_, i) for s_ in range(2) for i in range(4)], ["arE"])
            for tt in range(8):
                P.add("sp", DMA(Bf[:, tt, :], x_sets["near"].ap()[256 + tt * 128:256 + (tt + 1) * 128, :]),
                      reads=["arE"], writes=[("x1", tt, 0)], dma=True)

        def emit_out(wss, cg2):
            assert wss[1] == wss[0] + 1 and wss[0] % 2 == 0, wss
            last = (cg2 == 7)
            if last:
                fence(hk, ["arF"])
                parts = [norm_steps(Bf[:, tt, :], [("x1c", tt, cg) for cg in range(4)], A, tt * 128, 1536,
                                    GMLP, "gmlp", ["arF"], list(range(8))) for tt in range(8)]
            for tt in range(8):
                b = nxt("ps", 8)
                for k in range(16):
                    P.add("pe", MM(PS[b][:, 0:256], C[:, k, tt * 128:(tt + 1) * 128],
                                   ap(WBF, wss[0] * 2048 + k * 128, [(2048, 2), (1, 128)]), k == 0, k == 15),
                          reads=[("wbf", wss[0]), ("wbf", wss[1])] + ckeys, writes=[("ps", b)])
                xsl = Bf[:, tt, cg2 * 256:(cg2 + 1) * 256]
                P.add("dve", TT(xsl, PS[b][:, 0:256], xsl, ALU.add),
                      reads=[("ps", b), ("x1", tt, 0)], writes=[("x1c", tt, cg2 // 2)])
                if last:
                    parts[tt][2]()
                    if tt >= 1:
                        parts[tt - 1][3]()
                    if tt >= 2:
                        parts[tt - 2][1]()
            if last:
                parts[7][3]()
                parts[6][1]()
                parts[7][1]()
                fence(ckeys + [("YA", j) for j in range(4)], ["arG"])
                P.add("sp", DMA(GFIN[:, :], gfin_d.ap()), reads=["arG"], writes=["gfin"] + [("YB", j) for j in range(4)], dma=True)

        U([], start_E)
        for cg2 in range(8):
            U([(wcols(w_out, cg2 * 256 + q * 128, 128), 16, 128) for q in range(2)],
              lambda ws, cg2=cg2: emit_out(ws, cg2))
        U([], halt_if("X1", Bf, 8 * 2048, F32))

        h2k = [("A", m, hb) for m in range(8) for hb in range(2)]

        U([], halt_if("H2", A, 16 * 1536, BF16))

        def emit_up(ws, fc, f):
            us = fc % 2
            for blk in range(2):
                b = nxt("ps", 8)
                for k in range(16):
                    P.add("pe", MM(PS[b][:, :], ap(WBF, ws * 2048 + k * 128, [(1, 128)]),
                                   A[:, k, blk * 512:(blk + 1) * 512], k == 0, k == 15),
                          reads=[("wbf", ws)] + h2k, writes=[("ps", b)])
                sc = nxt("rl", 2)
                P.add("act", ACTF(RL[:, sc, :], PS[b][:, :], AF.Relu), reads=[("ps", b), "arG"], writes=[("rl", sc)])
                P.add("pool", TT(C[:, us * 8 + f, blk * 512:(blk + 1) * 512], RL[:, sc, :], RL[:, sc, :], ALU.mult),
                      reads=[("rl", sc), "arG"], writes=[("uT", us, f, blk)])

        fcol = {}

        def final_a(tt):
            col = nxt("col", 64)
            fcol[tt] = col
            xk = [("x1c", tt, cg) for cg in range(4)]
            P.add("act", ACTF(SQJ[:, :], Bf[:, tt, :], AF.Square, accum_out=SS[:, col:col + 1]),
                  reads=xk, writes=[("ss", col)])
            P.add("act", ACTF(RS[:, col:col + 1], SS[:, col:col + 1], AF.Ln, bias=EPS, scale=1.0 / D),
                  reads=[("ss", col)], writes=[("rs", col)])
            P.add("act", ACTF(RS[:, col:col + 1], RS[:, col:col + 1], AF.Exp, scale=-0.5),
                  reads=[("rs", col)], writes=[("rs", col)])

        def final_b(tt):
            col = fcol[tt]
            xk = [("x1c", tt, cg) for cg in range(4)]
            s = nxt("stg", 2)
            P.add("dve", STT(STG[:, s, :], Bf[:, tt, :], RS[:, col:col + 1], GFIN[:, :], ALU.mult, ALU.mult),
                  reads=xk + [("rs", col), "gfin"], writes=[("stg", s)])
            P.add("sp", DMA(out_d.ap()[tt * 128:(tt + 1) * 128, :], STG[:, s, :]),
                  reads=[("stg", s)], writes=[("out", tt)], dma=True)

        def emit_down(wss, fc, cg):
            us = fc % 2
            assert wss[1] == wss[0] + 1 and wss[0] % 2 == 0, wss
            ukeys = [("uT", us, f, blk) for f in range(8) for blk in range(2)]
            for tt in range(8):
                b = nxt("ps", 8)
                for f in range(8):
                    P.add("pe", MM(PS[b][:, :], C[:, us * 8 + f, tt * 128:(tt + 1) * 128],
                                   ap(WBF, wss[0] * 2048 + f * 256, [(2048, 2), (1, 256)]), f == 0, f == 7),
                          reads=[("wbf", wss[0]), ("wbf", wss[1])] + ukeys, writes=[("ps", b)])
                xsl = Bf[:, tt, cg * 512:(cg + 1) * 512]
                P.add("dve", TT(xsl, PS[b][:, :], xsl, ALU.add), reads=[("ps", b)], writes=[("x1c", tt, cg)])
                if fc == 7 and cg == 3:
                    final_a(tt)
                    if tt >= 1:
                        final_b(tt - 1)
            if fc == 7 and cg == 3:
                final_b(7)

        for fc in range(8):
            for f in range(8):
                U([(wcols(w_up, fc * 1024 + f * 128, 128), 16, 128)], lambda ws, fc=fc, f=f: emit_up(ws[0], fc, f))
            for cg in range(4):
                U([(wcols(w_down, cg * 512 + q * 256, 256, r0=fc * 1024, nk=8), 8, 256) for q in range(2)],
                  lambda ws, fc=fc, cg=cg: emit_down(ws, fc, cg))
        U([], halt_if("X2", Bf, 8 * 2048, F32))

    try:
        run_units()
    except _Stop:
        pass

    tail_reads = [k for k in P.lastw if isinstance(k, tuple) and k[0] == "out"] + (["dbg"] if "dbg" in P.lastw else [])
    P.add("pool", MS(JUNK[:, 15:16], 0.0), reads=tail_reads, writes=["fin"])

    from contextlib import ExitStack
    with ExitStack() as st:
        sems = {e: st.enter_context(nc.semaphore("c_" + e)) for e in ENGS}
        dsems = {"sp": [st.enter_context(nc.semaphore("d_sp%d" % i)) for i in range(Prog.NDSEM)]}
        block = st.enter_context(nc.Block())
        P.emit(block, sems, dsems)
    print("ops:", len(P.ops), {e: sum(1 for o in P.ops if o["eng"] == e) for e in ENGS}, flush=True)
    return nc, (dbg[0] is not None)


_CACHE = {}


def kernel(x, norm_mix, w_qkv, w_gate, b_gate, rpb, w_proj_a, w_proj_b, w_out,
           norm_mlp, w_up, w_down, norm_final):
    f = lambda a: np.ascontiguousarray(np.asarray(a, dtype=np.float32))
    x = f(x)
    plans = make_plans()
    slot_maps, contents = build_tables(plans, f(rpb)[0])
    ntab = sum(contents[0][pi][j].shape[1] for pi in range(len(plans)) for j in range(4))
    nc, has_dbg = build(plans, slot_maps, ntab)

    def colvec(g, n):
        return np.ascontiguousarray(f(g).reshape(n, 128).T)

    shared = {
        "w_qkv": f(w_qkv)[0], "w_gate": f(w_gate)[0], "w_pa": f(w_proj_a)[0], "w_pb": f(w_proj_b)[0],
        "w_out": f(w_out)[0], "w_up": f(w_up)[0], "w_down": f(w_down)[0],
        "gmix": colvec(norm_mix[0], 16), "gmlp": colvec(norm_mlp[0], 16),
        "gfin": np.ascontiguousarray(np.broadcast_to(f(norm_final)[None, :], (128, D))),
        "bgate": colvec(b_gate[0], 32),
        "ident": np.eye(128, dtype=np.float32).astype(ml_dtypes.bfloat16),
    }
    in_maps = []
    for core in range(NCORES):
        b, c = core // 4, core % 4
        m = dict(shared)
        for nm in ("3a", "3b", "near"):
            tok = set_tokens(c, nm)
            ok = (tok >= 0) & (tok < S)
            xs = np.zeros((1536, D), np.float32)
            xs[ok] = x[b, tok[ok]]
            m["x_" + nm] = xs
        m["tabs"] = np.ascontiguousarray(np.concatenate(
            [contents[c][pi][j] for pi in range(len(plans)) for j in range(4)], axis=1))
        in_maps.append(m)
    if os.environ.get("KTRACE"):
        res = run_bass_kernel_spmd(nc, in_maps, core_ids=list(range(NCORES)), trace=True)
        print("exec_time_ns", res.exec_time_ns, flush=True)
    else:
        res = run_bass_kernel_spmd(nc, in_maps, core_ids=list(range(NCORES)))
    if has_dbg:
        kernel.dbg = [np.asarray(r["dbg"]) for r in res.results]
    out = np.zeros((NB, S, D), np.float32)
    for core in range(NCORES):
        b, c = core // 4, core % 4
        out[b, set_tokens(c, "own")] = np.asarray(res.results[core]["out"])
    return out
```
